# Optimizing a Trainium2 kernel written in Bass

```python
import jax, jax.numpy as jnp
from jax import lax
import numpy as np

D_MODEL = 1024
BATCH = 2
SEQ = 8192
DEPTH = 2

CHUNK = 64
N_MEM = 256
CONV_WIDTH = 4
D_LRU = D_MODEL
LRU_BLOCKS = 8
LRU_BLOCK = D_LRU // LRU_BLOCKS
LRU_C = 8.0
D_SSD = 2 * D_MODEL
SSD_HEAD_DIM = 64
SSD_HEADS = D_SSD // SSD_HEAD_DIM
SSD_GROUPS = 4
SSD_HEADS_PER_GROUP = SSD_HEADS // SSD_GROUPS
SSD_STATE = 128
D_BC = SSD_GROUPS * SSD_STATE
D_XBC = D_SSD + 2 * D_BC
XA_HEADS = 4
XA_HEAD_DIM = 256
D_XA = XA_HEADS * XA_HEAD_DIM
N_BRANCH = 3
D_FF = ((8 * D_MODEL // 3 + 255) // 256) * 256
ALPHA = (2 * DEPTH) ** 0.25
BETA = (8 * DEPTH) ** -0.25
EPS = 1e-5

_SPLITS = (D_LRU, D_LRU, D_SSD, D_XBC, SSD_HEADS, D_XA, N_BRANCH * D_MODEL)
N_IN = sum(_SPLITS)
_OFFSETS = tuple(sum(_SPLITS[:i + 1]) for i in range(len(_SPLITS) - 1))

kernel_name = 'hybrid_rglru_ssd_memxattn_deepnorm'


def layer_norm(x, g, b):
    xf = x.astype(jnp.float32)
    mu = jnp.mean(xf, axis=-1, keepdims=True)
    var = jnp.mean(jnp.square(xf - mu), axis=-1, keepdims=True)
    return ((xf - mu) * lax.rsqrt(var + EPS) * g + b).astype(x.dtype)


def causal_depthwise_conv(x, w, b):
    c = x.shape[-1]
    y = lax.conv_general_dilated(
        x, w[:, None, :].astype(x.dtype), window_strides=(1,),
        padding=[(CONV_WIDTH - 1, 0)], dimension_numbers=('NWC', 'WIO', 'NWC'),
        feature_group_count=c)
    return y + b


def rg_lru(x, w_a, b_a, w_i, b_i, lam):
    xf = x.astype(jnp.float32)
    xb = xf.reshape(*xf.shape[:-1], LRU_BLOCKS, LRU_BLOCK)
    r = jax.nn.sigmoid(jnp.einsum('bsnk,nkj->bsnj', xb, w_a.astype(jnp.float32)).reshape(xf.shape) + b_a)
    i = jax.nn.sigmoid(jnp.einsum('bsnk,nkj->bsnj', xb, w_i.astype(jnp.float32)).reshape(xf.shape) + b_i)
    log_a = -LRU_C * r * jax.nn.softplus(-lam.astype(jnp.float32))
    a = jnp.exp(log_a)
    u = jnp.sqrt(-jnp.expm1(2.0 * log_a)) * (i * xf)

    def combine(lhs, rhs):
        a1, b1 = lhs
        a2, b2 = rhs
        return a1 * a2, a2 * b1 + b2

    _, h = lax.associative_scan(combine, (a, u), axis=1)
    return h


def ssd_chunked(xs, dt, a, bm, cm):
    bsz, s = xs.shape[:2]
    nc = s // CHUNK
    g, k, p, n = SSD_GROUPS, SSD_HEADS_PER_GROUP, SSD_HEAD_DIM, SSD_STATE
    x_c = (xs * dt[..., None]).reshape(bsz, nc, CHUNK, g, k, p)
    da = (dt * a).reshape(bsz, nc, CHUNK, g, k)
    b_c = bm.reshape(bsz, nc, CHUNK, g, n)
    c_c = cm.reshape(bsz, nc, CHUNK, g, n)
    cs = jnp.cumsum(da, axis=2)
    idx = jnp.arange(CHUNK)
    causal = (idx[:, None] >= idx[None, :])[:, :, None, None]
    seg = cs[:, :, :, None] - cs[:, :, None, :]
    decay = jnp.exp(jnp.where(causal, seg, -jnp.inf))
    cb = jnp.einsum('bclgn,bcsgn->bclsg', c_c, b_c)
    y_diag = jnp.einsum('bclsgk,bcsgkp->bclgkp', cb[..., None] * decay, x_c)
    decay_end = jnp.exp(cs[:, :, -1:] - cs)
    states = jnp.einsum('bclgn,bclgkp->bcgkpn', b_c, x_c * decay_end[..., None])
    chunk_decay = jnp.exp(cs[:, :, -1])

    def step(h, inp):
        st, dec = inp
        return h * dec[..., None, None] + st, h

    h0 = jnp.zeros((bsz, g, k, p, n), jnp.float32)
    _, prev = lax.scan(step, h0, (jnp.moveaxis(states, 1, 0), jnp.moveaxis(chunk_decay, 1, 0)))
    prev = jnp.moveaxis(prev, 0, 1)
    y_off = jnp.einsum('bclgn,bcgkpn->bclgkp', c_c, prev) * jnp.exp(cs)[..., None]
    return (y_diag + y_off).reshape(bsz, s, SSD_HEADS, p)


def ssd_branch(z, xbc, dt_raw, conv_w, conv_b, dt_bias, a_log, d_skip, norm_w):
    bsz, s = z.shape[:2]
    xbc = jax.nn.silu(causal_depthwise_conv(xbc, conv_w, conv_b)).astype(jnp.float32)
    xs, bm, cm = jnp.split(xbc, [D_SSD, D_SSD + D_BC], axis=-1)
    xs = xs.reshape(bsz, s, SSD_HEADS, SSD_HEAD_DIM)
    bm = bm.reshape(bsz, s, SSD_GROUPS, SSD_STATE)
    cm = cm.reshape(bsz, s, SSD_GROUPS, SSD_STATE)
    dt = jax.nn.softplus(dt_raw.astype(jnp.float32) + dt_bias)
    a = -jnp.exp(a_log.astype(jnp.float32))
    y = ssd_chunked(xs, dt, a, bm, cm) + d_skip[:, None] * xs
    y = y.reshape(bsz, s, D_SSD) * jax.nn.silu(z.astype(jnp.float32))
    yg = y.reshape(bsz, s, SSD_GROUPS, D_SSD // SSD_GROUPS)
    yg = yg * lax.rsqrt(jnp.mean(jnp.square(yg), axis=-1, keepdims=True) + EPS)
    return (yg.reshape(bsz, s, D_SSD) * norm_w).astype(z.dtype)


def memory_cross_attention(q, mem, w_kv):
    bsz, s = q.shape[:2]
    m = mem.shape[1]
    k, v = jnp.split(mem @ w_kv, 2, axis=-1)
    q = q.reshape(bsz, s, XA_HEADS, XA_HEAD_DIM)
    k = k.reshape(bsz, m, XA_HEADS, XA_HEAD_DIM)
    v = v.reshape(bsz, m, XA_HEADS, XA_HEAD_DIM)
    scores = jnp.einsum('bshd,bmhd->bhsm', q, k).astype(jnp.float32) * (XA_HEAD_DIM ** -0.5)
    probs = jax.nn.softmax(scores, axis=-1).astype(v.dtype)
    return jnp.einsum('bhsm,bmhd->bshd', probs, v).reshape(bsz, s, D_XA)


def hybrid_mixer(x, mem, w_in, b_gate, lru_conv_w, lru_conv_b, lru_w_a, lru_b_a, lru_w_i, lru_b_i,
                 lru_lambda, ssd_conv_w, ssd_conv_b, ssd_dt_bias, ssd_a_log, ssd_d, ssd_norm_w,
                 mem_w_kv, w_br_lru, w_br_ssd, w_br_xa, w_out):
    bsz, s = x.shape[:2]
    proj = x @ w_in
    lru_x, lru_gate, ssd_z, ssd_xbc, ssd_dt, xa_q, gate_logits = jnp.split(proj, _OFFSETS, axis=-1)
    h = rg_lru(causal_depthwise_conv(lru_x, lru_conv_w, lru_conv_b), lru_w_a, lru_b_a, lru_w_i, lru_b_i, lru_lambda)
    y_lru = (jax.nn.gelu(lru_gate.astype(jnp.float32)) * h).astype(x.dtype)
    y_ssd = ssd_branch(ssd_z, ssd_xbc, ssd_dt, ssd_conv_w, ssd_conv_b, ssd_dt_bias, ssd_a_log, ssd_d, ssd_norm_w)
    y_xa = memory_cross_attention(xa_q, mem, mem_w_kv)
    gates = jax.nn.sigmoid(gate_logits.reshape(bsz, s, N_BRANCH, D_MODEL) + b_gate)
    merged = (gates[:, :, 0] * (y_lru @ w_br_lru)
              + gates[:, :, 1] * (y_ssd @ w_br_ssd)
              + gates[:, :, 2] * (y_xa @ w_br_xa))
    return merged @ w_out


def swiglu(x, w_in, w_down):
    gate, up = jnp.split(x @ w_in, 2, axis=-1)
    return (jax.nn.silu(gate) * up) @ w_down


def setup_inputs(seed: int = 0) -> dict:
    key = jax.random.key(seed)
    ks = jax.random.split(key, 32)

    def nrm(k, shape, scale):
        return jax.random.normal(k, shape, jnp.float32) * scale

    a0 = jax.random.uniform(ks[10], (DEPTH, D_LRU), jnp.float32, 0.9, 0.999)
    root = a0 ** (1.0 / LRU_C)
    lru_lambda = jnp.log(root) - jnp.log1p(-root)
    dt0 = jnp.exp(jax.random.uniform(ks[13], (DEPTH, SSD_HEADS), jnp.float32, np.log(0.001), np.log(0.1)))
    ssd_dt_bias = dt0 + jnp.log(-jnp.expm1(-dt0))
    ssd_a_log = jnp.log(jax.random.uniform(ks[14], (DEPTH, SSD_HEADS), jnp.float32, 1.0, 16.0))
    return {
        'x': nrm(ks[0], (BATCH, SEQ, D_MODEL), 1.0),
        'mem': nrm(ks[1], (BATCH, N_MEM, D_MODEL), 1.0),
        'w_in': nrm(ks[2], (DEPTH, D_MODEL, N_IN), D_MODEL ** -0.5),
        'b_gate': nrm(ks[3], (DEPTH, N_BRANCH, D_MODEL), 0.1),
        'lru_conv_w': nrm(ks[4], (DEPTH, CONV_WIDTH, D_LRU), CONV_WIDTH ** -0.5),
        'lru_conv_b': nrm(ks[5], (DEPTH, D_LRU), 0.02),
        'lru_w_a': nrm(ks[6], (DEPTH, LRU_BLOCKS, LRU_BLOCK, LRU_BLOCK), LRU_BLOCK ** -0.5),
        'lru_b_a': nrm(ks[7], (DEPTH, D_LRU), 0.02),
        'lru_w_i': nrm(ks[8], (DEPTH, LRU_BLOCKS, LRU_BLOCK, LRU_BLOCK), LRU_BLOCK ** -0.5),
        'lru_b_i': nrm(ks[9], (DEPTH, D_LRU), 0.02),
        'lru_lambda': lru_lambda,
        'ssd_conv_w': nrm(ks[11], (DEPTH, CONV_WIDTH, D_XBC), CONV_WIDTH ** -0.5),
        'ssd_conv_b': nrm(ks[12], (DEPTH, D_XBC), 0.02),
        'ssd_dt_bias': ssd_dt_bias,
        'ssd_a_log': ssd_a_log,
        'ssd_d': 1.0 + nrm(ks[15], (DEPTH, SSD_HEADS), 0.02),
        'ssd_norm_w': 1.0 + nrm(ks[16], (DEPTH, D_SSD), 0.02),
        'mem_w_kv': nrm(ks[17], (DEPTH, D_MODEL, 2 * D_XA), D_MODEL ** -0.5),
        'w_br_lru': nrm(ks[18], (DEPTH, D_LRU, D_MODEL), D_LRU ** -0.5),
        'w_br_ssd': nrm(ks[19], (DEPTH, D_SSD, D_MODEL), D_SSD ** -0.5),
        'w_br_xa': nrm(ks[20], (DEPTH, D_XA, D_MODEL), D_XA ** -0.5),
        'w_out': nrm(ks[21], (DEPTH, D_MODEL, D_MODEL), BETA * D_MODEL ** -0.5),
        'ln1_g': 1.0 + nrm(ks[22], (DEPTH, D_MODEL), 0.02),
        'ln1_b': nrm(ks[23], (DEPTH, D_MODEL), 0.02),
        'ffn_w_in': nrm(ks[24], (DEPTH, D_MODEL, 2 * D_FF), D_MODEL ** -0.5),
        'ffn_w_down': nrm(ks[25], (DEPTH, D_FF, D_MODEL), BETA * D_FF ** -0.5),
        'ln2_g': 1.0 + nrm(ks[26], (DEPTH, D_MODEL), 0.02),
        'ln2_b': nrm(ks[27], (DEPTH, D_MODEL), 0.02),
    }


def reference(x, mem, w_in, b_gate, lru_conv_w, lru_conv_b, lru_w_a, lru_b_a, lru_w_i, lru_b_i,
              lru_lambda, ssd_conv_w, ssd_conv_b, ssd_dt_bias, ssd_a_log, ssd_d, ssd_norm_w,
              mem_w_kv, w_br_lru, w_br_ssd, w_br_xa, w_out, ln1_g, ln1_b, ffn_w_in, ffn_w_down,
              ln2_g, ln2_b):
    for l in range(DEPTH):
        mix = hybrid_mixer(x, mem, w_in[l], b_gate[l], lru_conv_w[l], lru_conv_b[l], lru_w_a[l], lru_b_a[l],
                           lru_w_i[l], lru_b_i[l], lru_lambda[l], ssd_conv_w[l], ssd_conv_b[l], ssd_dt_bias[l],
                           ssd_a_log[l], ssd_d[l], ssd_norm_w[l], mem_w_kv[l], w_br_lru[l], w_br_ssd[l],
                           w_br_xa[l], w_out[l])
        x = layer_norm(ALPHA * x + mix, ln1_g[l], ln1_b[l])
        x = layer_norm(ALPHA * x + swiglu(x, ffn_w_in[l], ffn_w_down[l]), ln2_g[l], ln2_b[l])
    return x
```

```python
import numpy as np
from contextlib import ExitStack
import ml_dtypes
import concourse.bass as bass
import concourse.mybir as mybir
from concourse.bass_utils import run_bass_kernel_spmd

F32 = mybir.dt.float32
BF16 = mybir.dt.bfloat16
AF = mybir.ActivationFunctionType
ALU = mybir.AluOpType
AX = mybir.AxisListType

D = 1024
SEQ = 8192
NMEM = 256
DFF = 2816
ALPHA = 4.0 ** 0.25
EPS = 1e-5
TT = 512
SAME_ENG_SYNC = True
ARENA_KB = 196
GROUPS = [[0, 1, 2, 3], [4, 5, 6, 7]]
BLK = {'pe': 'tensor', 'act': 'scalar', 'dve': 'vector', 'pool': 'gpsimd', 'sp': 'sync'}


class Buf:
    __slots__ = ("w", "r")

    def __init__(self):
        self.w = []
        self.r = []


class V:
    __slots__ = ("ap", "tile", "key")

    def __init__(self, ap, tile, key):
        self.ap = ap
        self.tile = tile
        self.key = key

    def __getitem__(self, idx):
        return V(self.ap[idx], self.tile, self.key)

    def rr(self, pat, **kw):
        return V(self.ap.rearrange(pat, **kw), self.tile, self.key)

    def bc(self, shape):
        return V(self.ap.to_broadcast(list(shape)), self.tile, self.key)

    def unsq(self, ax):
        return V(self.ap.unsqueeze(ax), self.tile, self.key)

    def bitcast(self, dt):
        return V(self.ap.bitcast(dt), self.tile, self.key)


class TK:
    def __init__(self, tile, key):
        self.tile = tile
        self.key = key

    def __getitem__(self, idx):
        return V(self.tile.base[idx], self.tile, self.key)


class Tile:
    def __init__(self, base, is_psum=False):
        self.base = base
        self.whole = Buf()
        self.kids = {}
        self.is_psum = is_psum

    def __getitem__(self, idx):
        return V(self.base[idx], self, None)

    def k(self, key):
        return TK(self, key)

    def chk(self, key):
        if key is None:
            return [self.whole] + list(self.kids.values())
        if key not in self.kids:
            self.kids[key] = Buf()
        return [self.whole, self.kids[key]]

    def upd(self, key):
        if key is None:
            return self.whole
        if key not in self.kids:
            self.kids[key] = Buf()
        return self.kids[key]


class Op:
    __slots__ = ("eng", "fn", "deps", "marked", "dma", "sem", "val", "inc")


class Prog:
    def __init__(self, nc, es):
        self.nc = nc
        self.es = es
        self.ops = {e: [] for e in BLK}
        self.esem = {e: es.enter_context(nc.semaphore(f"s_{e}")) for e in ('pe', 'act', 'dve', 'pool')}
        self.dsem = {e: [es.enter_context(nc.semaphore(f"d_{e}{i}")) for i in range(8)] for e in ('sp', 'pool', 'act')}
        self.dcnt = {e: 0 for e in self.dsem}
        self.nt = 0
        self.psb = []
        for i in range(8):
            t = es.enter_context(nc.psum_tensor(f"psb{i}", [128, 512], F32))
            self.psb.append(Tile(t, is_psum=True))
        self.psi = 0
        self.pcnt = {}
        self.out_dmas = []
        self.phase_dmas = []
        self.cache = {}
        self.arena = es.enter_context(nc.sbuf_tensor("arena", [128, ARENA_KB * 512], BF16))
        self.aoff = 0

    def sb(self, shape, dt, name=None):
        n = 1
        for d in shape[1:]:
            n *= d
        nb = n * (4 if dt == F32 else 2)
        nb = (nb + 63) // 64 * 64
        off = self.aoff
        self.aoff += nb
        assert self.aoff <= ARENA_KB * 1024, f"arena overflow {self.aoff}"
        ap = self.arena[:, off // 2:(off + nb) // 2]
        if dt == F32:
            ap = ap.bitcast(F32)
        ap = ap[:, 0:n]
        if len(shape) == 3:
            ap = ap.rearrange("p (a b) -> p a b", a=shape[1])
        elif len(shape) == 4:
            ap = ap.rearrange("p (a b c) -> p a b c", a=shape[1], b=shape[2])
        return Tile(ap)

    def barrier(self):
        lasts = []
        for eng in BLK:
            for op in reversed(self.ops[eng]):
                if op.fn is not None and not op.dma:
                    op.marked = True
                    lasts.append(op)
                    break
        dmas = list(self.phase_dmas)
        self.phase_dmas = []
        for eng in BLK:
            w = Op()
            w.eng = eng
            w.fn = None
            w.marked = False
            w.dma = False
            w.sem = None
            w.val = 0
            w.deps = list(lasts) + dmas
            self.ops[eng].append(w)
        self.aoff = 0

    def allgather(self, out_v, in_v):
        self.nt += 1
        sem = self.es.enter_context(self.nc.semaphore(f"cc{self.nt}"))
        out_ap, in_ap = out_v.ap.opt(), in_v.ap.opt()
        op = self._op('pool', lambda e: e.collective_compute("AllGather", ALU.bypass, replica_groups=GROUPS, ins=[in_ap], outs=[out_ap]),
                      [out_v], [in_v], dma=True)
        self.dcnt['pool'] -= 1
        op.sem = sem
        op.val = 1
        op.inc = 1
        return op

    def dma_fn(self, q, out, in_, fn):
        return self._op(q, fn, [out], [in_], dma=True)

    def ps(self, pool=None):
        if pool is None:
            t = self.psb[self.psi % 8]
            self.psi += 1
            return t
        key = tuple(pool)
        n = self.pcnt.get(key, 0)
        self.pcnt[key] = n + 1
        return self.psb[pool[n % len(pool)]]

    def dram(self, name, shape, dt, kind=None):
        if kind is None:
            t = self.nc.dram_tensor(name, list(shape), dt)
        else:
            t = self.nc.dram_tensor(name, list(shape), dt, kind=kind)
        return Tile(t.ap())

    def _op(self, eng, fn, outs, ins, dma=False):
        op = Op()
        op.eng = eng
        op.fn = fn
        op.marked = False
        op.dma = dma
        op.sem = None
        op.val = 0
        op.inc = 16
        deps = set()
        for v in ins:
            for b in v.tile.chk(v.key):
                deps.update(b.w)
                if v.tile.is_psum:
                    deps.update(o for o in b.r if o.eng != eng)
        for v in outs:
            for b in v.tile.chk(v.key):
                deps.update(b.w)
                deps.update(b.r)
        deps.discard(op)
        keep = []
        for p in deps:
            if p.dma:
                keep.append(p)
            elif p.eng == eng:
                if eng == 'pe':
                    continue
                if dma or SAME_ENG_SYNC:
                    p.marked = True
                    keep.append(p)
            else:
                p.marked = True
                keep.append(p)
        op.deps = keep
        for v in ins:
            b = v.tile.upd(v.key)
            if not dma:
                b.r = [o for o in b.r if o.dma or o.eng != eng]
            b.r.append(op)
        for v in outs:
            b = v.tile.upd(v.key)
            b.w = [op]
            b.r = []
        if dma:
            n = self.dcnt[eng]
            self.dcnt[eng] = n + 1
            op.sem = self.dsem[eng][n % 8]
            op.val = 16 * (n // 8 + 1)
            self.phase_dmas.append(op)
        self.ops[eng].append(op)
        return op

    def mm(self, out, lhsT, rhs, start=True, stop=True):
        self._op('pe', lambda e: e.matmul(out.ap, lhsT.ap, rhs.ap, start=start, stop=stop), [out], [lhsT, rhs])

    def tr(self, out, in_, ident):
        self._op('pe', lambda e: e.transpose(out.ap, in_.ap, ident.ap), [out], [in_, ident])

    def act(self, out, in_, func, bias=None, scale=None):
        ins = [in_]
        kw = {}
        if bias is not None:
            if isinstance(bias, V):
                ins.append(bias)
                kw['bias'] = bias.ap
            else:
                kw['bias'] = float(bias)
        if scale is not None:
            if isinstance(scale, V):
                ins.append(scale)
                kw['scale'] = scale.ap
            else:
                kw['scale'] = float(scale)
        self._op('act', lambda e: e.activation(out.ap, in_.ap, func, **kw), [out], ins)

    def copy(self, out, in_, eng='act'):
        if eng == 'act':
            self._op('act', lambda e: e.copy(out.ap, in_.ap), [out], [in_])
        else:
            self._op(eng, lambda e: e.tensor_copy(out.ap, in_.ap), [out], [in_])

    def tt(self, out, in0, in1, op, eng='dve'):
        self._op(eng, lambda e: e.tensor_tensor(out.ap, in0.ap, in1.ap, op), [out], [in0, in1])

    def ts(self, out, in0, s1, s2, op0, op1=None, eng='dve'):
        ins = [in0]
        a1 = s1
        a2 = s2
        if isinstance(s1, V):
            ins.append(s1)
            a1 = s1.ap
        if isinstance(s2, V):
            ins.append(s2)
            a2 = s2.ap
        if op1 is None:
            self._op(eng, lambda e: e.tensor_scalar(out.ap, in0.ap, a1, a2, op0), [out], ins)
        else:
            self._op(eng, lambda e: e.tensor_scalar(out.ap, in0.ap, a1, a2, op0, op1), [out], ins)

    def stt(self, out, in0, scalar, in1, op0, op1):
        ins = [in0, in1]
        a = scalar
        if isinstance(scalar, V):
            ins.append(scalar)
            a = scalar.ap
        self._op('dve', lambda e: e.scalar_tensor_tensor(out.ap, in0.ap, a, in1.ap, op0, op1), [out], ins)

    def scan(self, out, d0, d1, init, op0, op1):
        ins = [d0, d1]
        a = init
        if isinstance(init, V):
            ins.append(init)
            a = init.ap
        self._op('dve', lambda e: e.tensor_tensor_scan(out.ap, d0.ap, d1.ap, a, op0, op1), [out], ins)

    def red(self, out, in_, op):
        self._op('dve', lambda e: e.tensor_reduce(out.ap, in_.ap, AX.X, op), [out], [in_])

    def recip(self, out, in_):
        self._op('dve', lambda e: e.reciprocal(out.ap, in_.ap), [out], [in_])

    def bnstats(self, out, in_):
        self._op('dve', lambda e: e.bn_stats(out.ap, in_.ap), [out], [in_])

    def bnaggr(self, out, in_):
        self._op('dve', lambda e: e.bn_aggr(out.ap, in_.ap), [out], [in_])

    def memset(self, out, val, eng='dve'):
        self._op(eng, lambda e: e.memset(out.ap, val), [out], [])

    def dma(self, q, out, in_, is_out=False):
        op = self._op(q, lambda e: e.dma_start(out=out.ap, in_=in_.ap), [out], [in_], dma=True)
        if is_out:
            self.out_dmas.append(op)
        return op

    def run(self):
        fin = Op()
        fin.eng = 'sp'
        fin.fn = None
        fin.marked = False
        fin.dma = False
        fin.deps = list(self.out_dmas)
        fin.sem = None
        fin.val = 0
        self.ops['sp'].append(fin)
        for eng, lst in self.ops.items():
            c = 0
            for op in lst:
                if op.dma:
                    continue
                if op.marked:
                    c += 1
                    op.val = c
                    op.sem = self.esem[eng]
        with self.nc.Block() as block:
            for eng in BLK:
                if not self.ops[eng]:
                    continue

                def body(e, eng=eng):
                    seen = {}
                    for op in self.ops[eng]:
                        need = {}
                        for p in op.deps:
                            key = id(p.sem)
                            if seen.get(key, 0) >= p.val:
                                continue
                            if key not in need or need[key][1] < p.val:
                                need[key] = (p.sem, p.val)
                        for key, (sem_, val_) in need.items():
                            e.wait_ge(sem_, val_)
                            seen[key] = val_
                        if op.fn is None:
                            continue
                        ins = op.fn(e)
                        if op.dma:
                            ins.then_inc(op.sem, op.inc)
                        elif op.marked:
                            ins.then_inc(op.sem, 1)

                getattr(block, BLK[eng])(body)


def make_consts():
    j = np.arange(128)[:, None]
    l = np.arange(128)[None, :]
    same = (j // 64) == (l // 64)
    c = np.zeros((128, 8, 128), np.float32)
    c[:, 0] = (j == l)
    c[:, 1] = same & (j <= l)
    c[:, 2] = same
    c[:, 3] = (j < 64)
    c[:, 4] = (j >= 64)
    c[:, 5] = same & (j > l)
    c[:, 6] = (j <= l)
    c[:, 7] = same & (l >= j)
    return np.ascontiguousarray(c.reshape(128, 1024))


def softplus(P, out, x, tmp, neg=False):
    ax, y, w, q = tmp
    P.ts(ax, x, -1.0, None, ALU.mult)
    P.tt(ax, ax, x, ALU.max)
    P.act(y, ax, AF.Exp, scale=-1.0)
    P.ts(w, y, 2.0, None, ALU.add)
    P.recip(w, w)
    P.tt(w, w, y, ALU.mult)
    P.tt(y, w, w, ALU.mult)
    P.ts(q, y, 1.0 / 13.0, None, ALU.mult)
    for c in (1.0 / 11.0, 1.0 / 9.0, 1.0 / 7.0, 1.0 / 5.0, 1.0 / 3.0):
        P.stt(q, q, c, y, ALU.add, ALU.mult)
    P.stt(q, q, 1.0, w, ALU.add, ALU.mult)
    P.ts(ax, x, (-1.0 if neg else 1.0), 0.0, ALU.mult, ALU.max)
    P.stt(out, q, 2.0, ax, ALU.mult, ALU.add)

C_LRUX, C_LRUG, C_XS, C_B, C_C, C_Z, C_DT = 0, 256, 512, 1024, 1152, 1280, 1792
NWR = 1800
P_CW, P_CB, P_BA, P_BI, P_LAM, P_DTB, P_ALOG, P_DSK, P_NW = 0, 32, 40, 42, 44, 46, 54, 62, 70
NPR = 70 + 512


def emit_R(P, io, ntiles=SEQ // TT):
    wR_d, wai_d, prm_d, cst_d, yT_d = io['wR'], io['wai'], io['prm'], io['cst'], io['yT']
    prm = P.sb([128, NPR], F32)
    cst = P.sb([128, 8, 128], F32)
    identb = P.sb([128, 128], BF16)
    wR = P.sb([128, 8, NWR], BF16)
    wai = P.sb([128, 4, 128], BF16)
    P.dma('sp', prm[:, :], prm_d[:, :])
    P.dma('sp', cst[:, :, :], cst_d[:, :].rr("p (a b) -> p a b", a=8))
    P.dma('pool', wai[:, :, :], wai_d[:, :].rr("p (a b) -> p a b", a=4))
    wRv = wR_d[:, :].rr("(kc p) n -> p kc n", p=128)
    for kc in range(8):
        P.dma('pool', wR[:, kc, :], wRv[:, kc, :])
    P.copy(identb[:, :], cst[:, 0, :])
    TriBD, OnesBD, Half0, Half1, GtBD, mask2, maskCB = (cst[:, i, :] for i in range(1, 8))

    clru = P.sb([128, 2], F32)
    spw = P.sb([128, 8], F32)
    softplus(P, clru[:, :], prm[:, P_LAM:P_LAM + 2], [spw[:, 2 * i:2 * i + 2] for i in range(4)], neg=True)
    P.ts(clru[:, :], clru[:, :], -8.0, None, ALU.mult)
    Abc = P.sb([128, 8], F32)
    P.act(Abc[:, :], prm[:, P_ALOG:P_ALOG + 8], AF.Exp)
    P.ts(Abc[:, :], Abc[:, :], -1.0, None, ALU.mult)

    cbuf = [P.sb([128, 3 + TT], F32) for _ in range(8)]
    for c in range(8):
        P.memset(cbuf[c][:, 0:3], 0.0)
    hst = P.sb([128, 2], F32)
    P.memset(hst[:, :], 0.0)
    state = P.sb([128, 8, 64], F32)
    P.memset(state[:, :, :], 0.0)
    CTp = [P.sb([128, 4, 2, 128], BF16) for _ in range(2)]
    for t in CTp:
        P.memset(t[:, :, :, :], 0.0)

    xTs = [P.sb([128, 8, TT], BF16) for _ in range(2)]
    accs = [P.sb([128, TT], F32) for _ in range(2)]
    xc = [P.sb([128, TT], F32) for _ in range(2)]
    xcb = [P.sb([128, TT], BF16) for _ in range(2)]
    xsT = [P.sb([128, 4, TT], BF16) for _ in range(2)]
    BT = [P.sb([128, TT], BF16) for _ in range(2)]
    CT = [P.sb([128, TT], BF16) for _ in range(2)]
    lr = [P.sb([128, TT], F32) for _ in range(6)]
    ylT = [P.sb([128, 2, TT], BF16) for _ in range(2)]
    ysT = [P.sb([128, 4, TT], BF16) for _ in range(2)]
    zs = [P.sb([128, 512], F32) for _ in range(2)]
    sm = [P.sb([128, 96], F32) for _ in range(2)]
    Rm = [P.sb([128, 8, 128], F32) for _ in range(2)]
    dec = [P.sb([128, 8, 128], F32) for _ in range(2)]
    CBm = [P.sb([128, 128], F32) for _ in range(2)]
    MT = [P.sb([128, 8, 128], BF16) for _ in range(2)]
    xs_tok = [P.sb([128, 8, 64], BF16) for _ in range(2)]
    B_tok = [P.sb([128, 128], BF16) for _ in range(2)]
    xdt = [P.sb([128, 8, 64], BF16) for _ in range(2)]
    xdec = [P.sb([128, 8, 64], BF16) for _ in range(2)]
    prevb = [P.sb([128, 512], BF16) for _ in range(4)]
    yb = [P.sb([128, 8, 64], F32) for _ in range(2)]
    tmpx = [P.sb([128, 8, 64], F32) for _ in range(2)]
    y3 = [P.sb([128, 512], BF16) for _ in range(2)]
    dtall = [P.sb([128, 32], F32) for _ in range(2)]
    dtw = [P.sb([128, 32], F32) for _ in range(5)]

    ccol = [C_LRUX, C_LRUX + 128, C_XS, C_XS + 128, C_XS + 256, C_XS + 384, C_B, C_C]

    PX, PY = [6, 7], [[0, 1, 2], [3, 4, 5]]

    def gen_X(ti):
        xT = xTs[ti % 2]
        io['load_xT'](ti, xT)
        xsTt, BTt, CTt, CTpt = xsT[ti % 2], BT[ti % 2], CT[ti % 2], CTp[ti % 2]

        def fm_chunk(col):
            ps = P.ps(PX)
            for kc in range(8):
                P.mm(ps[:, :], wR[:, kc, col:col + 128], xT[:, kc, :], start=(kc == 0), stop=(kc == 7))
            return ps

        for c in range(8):
            ps = fm_chunk(ccol[c])
            cb = cbuf[c]
            P.copy(cb[:, 3:3 + TT], ps[:, :])
            acc = accs[c % 2]
            P.ts(acc[:, :], cb[:, 3:3 + TT], prm[:, P_CW + c * 4 + 3:P_CW + c * 4 + 4], None, ALU.mult)
            yield
            for j in range(3):
                P.stt(acc[:, :], cb[:, j:j + TT], prm[:, P_CW + c * 4 + j:P_CW + c * 4 + j + 1], acc[:, :], ALU.mult, ALU.add)
            P.copy(cb[:, 0:3], cb[:, TT:TT + 3], eng='dve')
            yield
            bias = prm[:, P_CB + c:P_CB + c + 1]
            if c < 2:
                P.act(xc[c][:, :], acc[:, :], AF.Identity, bias=bias)
                P.act(xcb[c][:, :], acc[:, :], AF.Identity, bias=bias)
            elif c < 6:
                P.act(xsTt[:, c - 2, :], acc[:, :], AF.Silu, bias=bias)
            elif c == 6:
                P.act(BTt[:, :], acc[:, :], AF.Silu, bias=bias)
            else:
                P.act(CTt[:, :], acc[:, :], AF.Silu, bias=bias)
                a4 = acc[:, :].rr("p (s h t) -> p s h t", s=4, h=2)
                for h in range(2):
                    P.act(CTpt[:, :, h, h * 64:(h + 1) * 64], a4[:, :, h, :], AF.Silu, bias=bias)
            yield
        psd = P.ps(PX)
        for sub in range(4):
            for kc in range(8):
                P.mm(psd[:, sub * 8:(sub + 1) * 8], xT[:, kc, sub * 128:(sub + 1) * 128], wR[:, kc, C_DT:C_DT + 8], start=(kc == 0), stop=(kc == 7))
        dtx = dtw[0][:, :]
        P.tt(dtx.rr("p (s k) -> p s k", s=4), psd[:, 0:32].rr("p (s k) -> p s k", s=4), prm[:, P_DTB:P_DTB + 8].unsq(1).bc([128, 4, 8]), ALU.add)
        yield
        softplus(P, dtall[ti % 2][:, :], dtx, [t[:, :] for t in dtw[1:5]])
        yield
        ylTt = ylT[ti % 2]
        for n in range(2):
            r_, i_, a_, s_, u_, h_ = lr
            ps = P.ps(PX)
            P.mm(ps[:, :], wai[:, n, :], xcb[n][:, :])
            P.act(r_[:, :], ps[:, :], AF.Sigmoid, bias=prm[:, P_BA + n:P_BA + n + 1])
            ps = P.ps(PX)
            P.mm(ps[:, :], wai[:, 2 + n, :], xcb[n][:, :])
            P.act(i_[:, :], ps[:, :], AF.Sigmoid, bias=prm[:, P_BI + n:P_BI + n + 1])
            yield
            P.act(a_[:, :], r_[:, :], AF.Exp, scale=clru[:, n:n + 1])
            P.tt(s_[:, :], a_[:, :], a_[:, :], ALU.mult)
            P.act(s_[:, :], s_[:, :], AF.Sqrt, bias=1.0, scale=-1.0)
            yield
            P.tt(u_[:, :], i_[:, :], xc[n][:, :], ALU.mult)
            P.tt(u_[:, :], u_[:, :], s_[:, :], ALU.mult)
            yield
            P.scan(h_[:, :], a_[:, :], u_[:, :], hst[:, n:n + 1], ALU.mult, ALU.add)
            P.copy(hst[:, n:n + 1], h_[:, TT - 1:TT], eng='dve')
            yield
            psg = fm_chunk(C_LRUG + n * 128)
            P.act(r_[:, :], psg[:, :], AF.Square)
            P.ts(r_[:, :], r_[:, :], 0.044715, 1.0, ALU.mult, ALU.add)
            yield
            P.tt(r_[:, :], r_[:, :], psg[:, :], ALU.mult)
            P.act(r_[:, :], r_[:, :], AF.Sigmoid, scale=1.5957691216057308)
            yield
            P.tt(r_[:, :], r_[:, :], psg[:, :], ALU.mult)
            P.tt(ylTt[:, n, :], r_[:, :], h_[:, :], ALU.mult)
            yield
        yt2 = yT_d.k(ti)[ti, :].rr("(f t) -> f t", t=TT)
        P.dma('sp', yt2[0:256, :].rr("(c p) t -> p c t", p=128), ylTt[:, :, :])

    def gen_Y(ti, sub):
        xT = xTs[ti % 2]
        xsTt, BTt, CTt, CTpt = xsT[ti % 2], BT[ti % 2], CT[ti % 2], CTp[ti % 2]
        ysTt = ysT[ti % 2]
        q = sub % 2
        pool = PY[q]
        tok = slice(sub * 128, (sub + 1) * 128)
        smq = sm[q]
        da = smq[:, 24:32]
        cs_sb, expcs, dend, cdec = smq[:, 32:40], smq[:, 40:48], smq[:, 48:56], smq[:, 56:72]
        tmp8, ss, rstd = smq[:, 72:80], smq[:, 80:81], smq[:, 81:82]
        psz = P.ps(pool)
        for kc in range(8):
            P.mm(psz[:, :], xT[:, kc, tok], wR[:, kc, C_Z:C_Z + 512], start=(kc == 0), stop=(kc == 7))
        P.act(zs[q][:, :], psz[:, :], AF.Silu)
        dt = dtall[ti % 2][:, sub * 8:(sub + 1) * 8]
        P.tt(da, dt, Abc[:, :], ALU.mult)
        yield
        pst = P.ps(pool)
        pstb = pst[:, :].bitcast(BF16)
        for c in range(4):
            P.tr(pstb[:, c * 128:(c + 1) * 128], xsTt[:, c, tok], identb[:, :])
        P.copy(xs_tok[q][:, :, :], pstb[:, 0:512].rr("p (k d) -> p k d", k=8))
        psb_ = P.ps(pool)
        psbb = psb_[:, :].bitcast(BF16)
        P.tr(psbb[:, 0:128], BTt[:, tok], identb[:, :])
        P.copy(B_tok[q][:, :], psbb[:, 0:128])
        yield
        psc = P.ps(pool)
        P.mm(psc[:, 0:8], TriBD, da)
        P.mm(psc[:, 8:16], OnesBD, da)
        P.mm(psc[:, 16:24], Half0, da)
        P.mm(psc[:, 24:32], Half1, da)
        P.copy(cs_sb, psc[:, 0:8])
        P.act(expcs, psc[:, 0:8], AF.Exp)
        yield
        P.tt(tmp8, psc[:, 8:16], cs_sb, ALU.subtract)
        P.act(dend, tmp8, AF.Exp)
        P.act(cdec, psc[:, 16:32], AF.Exp)
        yield
        P.tt(Rm[q][:, :, :], da.unsq(2).bc([128, 8, 128]), mask2.unsq(1).bc([128, 8, 128]), ALU.mult)
        yield
        Rf = Rm[q][:, :, :].rr("p k l -> p (k l)")
        psA = P.ps(pool)
        P.mm(psA[:, :], GtBD, Rf[:, 0:512])
        psB = P.ps(pool)
        P.mm(psB[:, :], GtBD, Rf[:, 512:1024])
        df = dec[q][:, :, :].rr("p k l -> p (k l)")
        P.act(df[:, 0:512], psA[:, :], AF.Exp)
        P.act(df[:, 512:1024], psB[:, :], AF.Exp)
        yield
        psC = P.ps(pool)
        P.mm(psC[:, 0:128], BTt[:, tok], CTt[:, tok])
        P.tt(CBm[q][:, :], psC[:, 0:128], maskCB, ALU.mult)
        yield
        P.tt(MT[q][:, :, :], dec[q][:, :, :], CBm[q][:, :].unsq(1).bc([128, 8, 128]), ALU.mult)
        yield
        P.tt(xdt[q][:, :, :], xs_tok[q][:, :, :], dt.unsq(2).bc([128, 8, 64]), ALU.mult)
        P.tt(xdec[q][:, :, :], xdt[q][:, :, :], dend.unsq(2).bc([128, 8, 64]), ALU.mult)
        yield
        psO = P.ps(pool)
        xdf = xdec[q][:, :, :].rr("p k d -> p (k d)")
        sf = state[:, :, :].rr("p k d -> p (k d)")
        for h in range(2):
            psS = P.ps(pool)
            P.mm(psS[:, :], B_tok[q][64 * h:64 * h + 64, :], xdf[64 * h:64 * h + 64, :])
            pv = prevb[(2 * q + h)]
            P.copy(pv[:, :], sf)
            P.tt(state[:, :, :], state[:, :, :], cdec[:, 8 * h:8 * h + 8].unsq(2).bc([128, 8, 64]), ALU.mult)
            P.tt(sf, sf, psS[:, :], ALU.add)
            P.mm(psO[:, :], CTpt[:, sub, h, :], pv[:, :], start=(h == 0), stop=(h == 1))
            yield
        y = yb[q]
        yf = y[:, :, :].rr("p k d -> p (k d)")
        P.tt(y[:, :, :], psO[:, :].rr("p (k d) -> p k d", k=8), expcs.unsq(2).bc([128, 8, 64]), ALU.mult)
        yield
        psY = P.ps(pool)
        for k in range(8):
            P.mm(psY[:, k * 64:(k + 1) * 64], MT[q][:, k, :], xdt[q][:, k, :])
        P.tt(yf, yf, psY[:, :], ALU.add)
        yield
        P.tt(tmpx[q][:, :, :], xs_tok[q][:, :, :], prm[:, P_DSK:P_DSK + 8].unsq(2).bc([128, 8, 64]), ALU.mult)
        P.tt(y[:, :, :], y[:, :, :], tmpx[q][:, :, :], ALU.add)
        yield
        P.tt(yf, yf, zs[q][:, :], ALU.mult)
        tf = tmpx[q][:, :, :].rr("p k d -> p (k d)")
        P.tt(tf, yf, yf, ALU.mult)
        yield
        P.red(ss, tf, ALU.add)
        P.ts(ss, ss, 1.0 / 512.0, EPS, ALU.mult, ALU.add)
        P.act(ss, ss, AF.Sqrt)
        P.recip(rstd, ss)
        yield
        P.stt(y3[q][:, :], yf, rstd, prm[:, P_NW:P_NW + 512], ALU.mult, ALU.mult)
        psT = P.ps(pool)
        psTb = psT[:, :].bitcast(BF16)
        for c in range(4):
            P.tr(psTb[:, c * 128:(c + 1) * 128], y3[q][:, c * 128:(c + 1) * 128], identb[:, :])
        P.copy(ysTt[:, :, tok], psTb[:, 0:512].rr("p (c t) -> p c t", c=4))

    def run_tile(ygens, bg):
        SKEW = 5
        active = []
        pending = list(ygens)
        bg_done = bg is None
        while pending or active:
            if pending and len(active) < 2 and (not active or active[-1][1] >= SKEW):
                active.append([pending.pop(0), 0])
            for a in list(active):
                try:
                    next(a[0])
                    a[1] += 1
                except StopIteration:
                    active.remove(a)
            if not bg_done:
                try:
                    next(bg)
                except StopIteration:
                    bg_done = True
        while not bg_done:
            try:
                next(bg)
            except StopIteration:
                bg_done = True

    run_tile([], gen_X(0))
    for ti in range(ntiles):
        bg = gen_X(ti + 1) if ti + 1 < ntiles else None
        run_tile([gen_Y(ti, sub) for sub in range(4)], bg)
        yt2 = yT_d.k(ti)[ti, :].rr("(f t) -> f t", t=TT)
        P.dma('sp', yt2[256:768, :].rr("(c p) t -> p c t", p=128), ysT[ti % 2][:, :, :])
        io['tile_done'](ti)


NTOK = 2048
P_BG = 0
NPT = 24


class WStream:
    def __init__(self, P, nslots):
        self.P = P
        self.slots = [P.sb([128, 8, 512], BF16) for _ in range(nslots)]
        self.i = 0

    def load(self, wd, r0, nrows, c0, ncols=512):
        slot = self.slots[self.i % len(self.slots)]
        self.i += 1
        nkc = nrows // 128
        src = wd[r0:r0 + nrows, c0:c0 + ncols].rr("(kc p) n -> p kc n", p=128)
        self.P.dma('pool', slot[:, 0:nkc, 0:ncols], src)
        return slot


def emit_T(P, io, ntiles=NTOK // TT):
    ws = WStream(P, 4)
    prmT = P.sb([128, NPT], F32)
    lnp = P.sb([128, 4, D], F32)
    identf = P.sb([128, 128], F32)
    identb = P.sb([128, 128], BF16)
    P.dma('sp', prmT[:, :], io['prmT'][:, :])
    P.dma('sp', lnp[:, :, :], io['lnp'][:, :].rr("p (a b) -> p a b", a=4))
    P.dma('sp', identf[:, :], io['cst'][:, 0:128])
    P.copy(identb[:, :], identf[:, :])

    memT = P.sb([128, 8, NMEM], BF16)
    P.dma('pool', memT[:, :, :], io['memT'][:, :].rr("(kc p) m -> p kc m", p=128))
    KT = P.sb([128, 8, NMEM], BF16)
    Vt = P.sb([128, 2, 1024], BF16)
    for ph in range(2):
        pan = ws.load(io['wkv'], 0, 1024, ph * 512)
        for c in range(4):
            ps = P.ps()
            for kc in range(8):
                P.mm(ps[:, 0:NMEM], pan[:, kc, c * 128:(c + 1) * 128], memT[:, kc, :], start=(kc == 0), stop=(kc == 7))
            P.copy(KT.k(ph * 4 + c)[:, ph * 4 + c, :], ps[:, 0:NMEM])
    for ph in range(2):
        pan = ws.load(io['wkv'], 0, 1024, 1024 + ph * 512)
        for mc in range(2):
            ps = P.ps()
            for kc in range(8):
                P.mm(ps[:, :], memT[:, kc, mc * 128:(mc + 1) * 128], pan[:, kc, :], start=(kc == 0), stop=(kc == 7))
            P.copy(Vt.k((mc, ph))[:, mc, ph * 512:(ph + 1) * 512], ps[:, :])

    xT = P.sb([128, 8, TT], BF16)
    xtok = P.sb([128, 4, D], F32)
    x1 = P.sb([128, 4, D], F32)
    qm = P.sb([128, 8, TT], BF16)
    x1T = P.sb([128, 8, TT], BF16)
    PT = [P.sb([128, 2, TT], BF16) for _ in range(2)]
    yxaT = P.sb([128, 8, TT], BF16)
    ybuf = P.sb([128, 16, TT], BF16)
    acc = P.sb([128, 8, TT], F32)
    hT = P.sb([128, 22, TT], BF16)
    gs = [P.sb([128, TT], F32) for _ in range(2)]
    es_ = [P.sb([128, NMEM], F32) for _ in range(2)]
    pn = [P.sb([128, NMEM], BF16) for _ in range(2)]
    sm = [P.sb([128, 8], F32) for _ in range(2)]
    st = [P.sb([128, 12], F32) for _ in range(2)]
    mv = [P.sb([128, 4], F32) for _ in range(2)]

    lncnt = [0]

    def layer_norm(v, g, b):
        i = lncnt[0] % 2
        lncnt[0] += 1
        P.bnstats(st[i][:, 0:6], v[:, 0:512])
        P.bnstats(st[i][:, 6:12], v[:, 512:1024])
        P.bnaggr(mv[i][:, 0:2], st[i][:, :])
        P.ts(mv[i][:, 2:3], mv[i][:, 1:2], EPS, None, ALU.add)
        P.act(mv[i][:, 2:3], mv[i][:, 2:3], AF.Sqrt)
        P.recip(mv[i][:, 3:4], mv[i][:, 2:3])
        P.ts(v, v, mv[i][:, 0:1], mv[i][:, 3:4], ALU.subtract, ALU.mult)
        P.tt(v, v, g, ALU.mult)
        P.tt(v, v, b, ALU.add)

    for ti in range(ntiles):
        t0 = ti * TT
        io['load_xT'](ti, xT)
        io['load_xtok'](ti, xtok)
        for ph in range(2):
            pan = ws.load(io['wq'], 0, 1024, ph * 512)
            for c in range(4):
                dc = ph * 4 + c
                ps = P.ps()
                for kc in range(8):
                    P.mm(ps[:, :], pan[:, kc, c * 128:(c + 1) * 128], xT[:, kc, :], start=(kc == 0), stop=(kc == 7))
                P.copy(qm.k(dc)[:, dc, :], ps[:, :])
        def gen_A(pool):
            cnt = 0
            for h in range(4):
                PTh = PT[h % 2]
                for s in range(4):
                    i = cnt % 2
                    cnt += 1
                    tok = slice(s * 128, (s + 1) * 128)
                    ps = P.ps(pool)
                    for dd in range(2):
                        P.mm(ps[:, 0:NMEM], qm.k(2 * h + dd)[:, 2 * h + dd, tok], KT.k(2 * h + dd)[:, 2 * h + dd, :], start=(dd == 0), stop=(dd == 1))
                    mx, nb, sm_, rs = sm[i][:, 0:1], sm[i][:, 1:2], sm[i][:, 2:3], sm[i][:, 3:4]
                    P.red(mx, ps[:, 0:NMEM], ALU.max)
                    P.ts(nb, mx, -1.0 / 16.0, None, ALU.mult)
                    yield
                    P.act(es_[i][:, :], ps[:, 0:NMEM], AF.Exp, bias=nb, scale=1.0 / 16.0)
                    P.red(sm_, es_[i][:, :], ALU.add)
                    P.recip(rs, sm_)
                    yield
                    P.ts(pn[i][:, :], es_[i][:, :], rs, None, ALU.mult)
                    pst = P.ps(pool)
                    pstb = pst[:, :].bitcast(BF16)
                    for mc in range(2):
                        P.tr(pstb[:, mc * 128:(mc + 1) * 128], pn[i][:, mc * 128:(mc + 1) * 128], identb[:, :])
                    P.copy(PTh[:, :, tok], pstb[:, 0:256].rr("p (m t) -> p m t", m=2))
                    yield
                for dd in range(2):
                    ps = P.ps(pool)
                    for mc in range(2):
                        P.mm(ps[:, :], Vt[:, mc, h * 256 + dd * 128:h * 256 + (dd + 1) * 128], PTh[:, mc, :], start=(mc == 0), stop=(mc == 1))
                    P.copy(yxaT.k(2 * h + dd)[:, 2 * h + dd, :], ps[:, :])
                yield

        def gen_B(bs, pool):
            for b in bs:
                wname, nkp = [('wbl', 1), ('wbs', 2), ('wbx', 1)][b]
                if b < 2:
                    io['load_y'](ti, b, ybuf)
                ysb = ybuf if b < 2 else yxaT
                for ph in range(2):
                    pans = [ws.load(io[wname], kp * 1024, 1024, ph * 512) for kp in range(nkp)]
                    gp = ws.load(io['wg'], 0, 1024, b * 1024 + ph * 512)
                    for c in range(4):
                        dc = ph * 4 + c
                        psp = P.ps(pool)
                        n = nkp * 8
                        i = 0
                        for kp in range(nkp):
                            for kc in range(8):
                                P.mm(psp[:, :], pans[kp][:, kc, c * 128:(c + 1) * 128], ysb[:, kp * 8 + kc, :], start=(i == 0), stop=(i == n - 1))
                                i += 1
                            yield
                        psg = P.ps(pool)
                        for kc in range(8):
                            P.mm(psg[:, :], gp[:, kc, c * 128:(c + 1) * 128], xT[:, kc, :], start=(kc == 0), stop=(kc == 7))
                        g = gs[dc % 2]
                        P.act(g[:, :], psg[:, :], AF.Sigmoid, bias=prmT[:, P_BG + b * 8 + dc:P_BG + b * 8 + dc + 1])
                        yield
                        if b == 0:
                            P.tt(acc.k(dc)[:, dc, :], g[:, :], psp[:, :], ALU.mult)
                        else:
                            P.tt(g[:, :], g[:, :], psp[:, :], ALU.mult)
                            if b == 1:
                                P.tt(acc.k(dc)[:, dc, :], acc.k(dc)[:, dc, :], g[:, :], ALU.add)
                            else:
                                P.tt(qm.k(dc)[:, dc, :], acc.k(dc)[:, dc, :], g[:, :], ALU.add)
                        yield

        gens = [gen_A([0, 1, 2]), gen_B([0, 1], [3, 4, 5, 6, 7])]
        while gens:
            for g_ in list(gens):
                try:
                    next(g_)
                except StopIteration:
                    gens.remove(g_)
        for _ in gen_B([2], None):
            pass
        for ph in range(2):
            pan = ws.load(io['wo'], 0, 1024, ph * 512)
            for s in range(4):
                tok = slice(s * 128, (s + 1) * 128)
                ps = P.ps()
                for kc in range(8):
                    P.mm(ps[:, :], qm[:, kc, tok], pan[:, kc, :], start=(kc == 0), stop=(kc == 7))
                P.stt(x1.k(s)[:, s, ph * 512:(ph + 1) * 512], xtok.k(s)[:, s, ph * 512:(ph + 1) * 512], ALPHA, ps[:, :], ALU.mult, ALU.add)
        for s in range(4):
            tok = slice(s * 128, (s + 1) * 128)
            layer_norm(x1.k(s)[:, s, :], lnp[:, 0, :], lnp[:, 1, :])
            for g4 in range(2):
                ps = P.ps()
                for j in range(4):
                    dc = g4 * 4 + j
                    P.tr(ps[:, j * 128:(j + 1) * 128], x1.k(s)[:, s, dc * 128:(dc + 1) * 128], identf[:, :])
                P.copy(x1T[:, g4 * 4:(g4 + 1) * 4, tok], ps[:, :].rr("p (j t) -> p j t", j=4))
        for p in range(11):
            pan = ws.load(io['wfi'], 0, 1024, p * 512)
            for c in range(4):
                col = p * 512 + c * 128
                ps = P.ps()
                for kc in range(8):
                    P.mm(ps[:, :], pan[:, kc, c * 128:(c + 1) * 128], x1T[:, kc, :], start=(kc == 0), stop=(kc == 7))
                if col < DFF:
                    ffc = col // 128
                    P.act(hT.k(ffc)[:, ffc, :], ps[:, :], AF.Silu)
                else:
                    ffc = (col - DFF) // 128
                    P.tt(hT.k(ffc)[:, ffc, :], hT.k(ffc)[:, ffc, :], ps[:, :], ALU.mult)
        for ph in range(2):
            pss = [P.ps() for _ in range(4)]
            for (r0, nr) in [(0, 1024), (1024, 1024), (2048, 768)]:
                pan = ws.load(io['wfd'], r0, nr, ph * 512)
                for s in range(4):
                    tok = slice(s * 128, (s + 1) * 128)
                    for kc in range(nr // 128):
                        ffc = r0 // 128 + kc
                        P.mm(pss[s][:, :], hT.k(ffc)[:, ffc, tok], pan[:, kc, :], start=(ffc == 0), stop=(ffc == 21))
            for s in range(4):
                P.stt(xtok.k(s)[:, s, ph * 512:(ph + 1) * 512], x1.k(s)[:, s, ph * 512:(ph + 1) * 512], ALPHA, pss[s][:, :], ALU.mult, ALU.add)
        for s in range(4):
            layer_norm(xtok.k(s)[:, s, :], lnp[:, 2, :], lnp[:, 3, :])
        io['store_out'](ti, xtok)
        if io.get('store_xT') is not None:
            for s in range(4):
                tok = slice(s * 128, (s + 1) * 128)
                for g4 in range(2):
                    ps = P.ps()
                    for j in range(4):
                        dc = g4 * 4 + j
                        P.tr(ps[:, j * 128:(j + 1) * 128], xtok.k(s)[:, s, dc * 128:(dc + 1) * 128], identf[:, :])
                    P.copy(qm[:, g4 * 4:(g4 + 1) * 4, tok], ps[:, :].rr("p (j t) -> p j t", j=4))
            io['store_xT'](ti, qm)


R_W = {'wR': [D, NWR], 'wai': [128, 512], 'prm': [128, NPR]}
T_W = {'wq': [D, 1024], 'wg': [D, 3072], 'wkv': [D, 2048], 'wbl': [1024, D], 'wbs': [2048, D], 'wbx': [1024, D],
       'wo': [D, D], 'wfi': [D, 2 * DFF], 'wfd': [DFF, D], 'prmT': [128, NPT], 'lnp': [128, 4 * D]}


def build_fused(nc, es, stop=None):
    P = Prog(nc, es)

    def ext(n, sh):
        return P.dram(n, sh, F32, "ExternalInput")

    xT0 = ext("xT0", [D, SEQ])
    xTt = ext("xTt", [D, NTOK])
    xtok_d = ext("xtok", [NTOK, D])
    memT = ext("memT", [D, NMEM])
    cst = ext("cst", [128, 1024])
    out_d = P.dram("out", [NTOK, D], F32, "ExternalOutput")
    WR = [{k: ext(f"{k}{l}", sh) for k, sh in R_W.items()} for l in range(2)]
    WT = [{k: ext(f"{k}{l}", sh) for k, sh in T_W.items()} for l in range(2)]
    YT = 768 * TT
    y_loc = [P.dram(f"y_loc{l}", [16, YT], BF16) for l in range(2)]
    y_all = [P.dram(f"y_all{l}", [16, 4 * YT], BF16) for l in range(2)]
    y_mine = [P.dram(f"y_mine{l}", [4, 4 * YT], BF16, "ExternalOutput" if stop == f"R{l}" else None) for l in range(2)]
    x1_d = P.dram("x1_d", [NTOK, D], F32, "ExternalOutput" if stop in ("T0", "DBG") else None)
    XT = 512 * TT
    x1T_loc = P.dram("x1T_loc", [8, XT], BF16)
    x1T_all = P.dram("x1T_all", [8, 4 * XT], BF16)

    def tokview(t):
        return t[:, :].rr("(s p) d -> p s d", p=128)

    for l in range(2):
        if l == 0:
            xv = xT0[:, :].rr("(kc p) t -> p kc t", p=128)

            def load_xT_R(ti, xT, xv=xv):
                P.dma('pool', xT[:, :, :], xv[:, :, ti * TT:(ti + 1) * TT])
        else:
            def load_xT_R(ti, xT):
                q, i = ti // 4, ti % 4
                for h in range(2):
                    src = x1T_all[i * 2 + h, q * XT:(q + 1) * XT].rr("(c p t) -> p c t", p=128, t=TT)
                    P.dma('pool', xT[:, 4 * h:4 * h + 4, :], src)
        yl, yal = y_loc[l], y_all[l]

        pend = []

        def tile_done(ti, yl=yl, yal=yal, pend=pend):
            while pend:
                pend.pop(0)()
            pend.append(lambda: P.allgather(yal.k(ti)[ti, :], yl.k(ti)[ti, :]))
        ioR = dict(WR[l])
        ioR.update(cst=cst, yT=y_loc[l], load_xT=load_xT_R, tile_done=tile_done)
        emit_R(P, ioR)
        while pend:
            pend.pop(0)()
        P.barrier()

        ya = y_all[l]

        ym = y_mine[l]

        def fn(e, ym=ym, ya=ya):
            q = e.partition_id() % 4
            yv = ya.base.rearrange("(q i) (r c) -> q (i r) c", q=4, c=16384)
            src = yv[bass.ds(q, 1), :, :].rearrange("o r c -> (o r) c")
            dst = ym.base.rearrange("i (r c) -> (i r) c", c=16384)
            return e.dma_start(out=dst, in_=src)
        op_ = P.dma_fn('pool', ym[:, :], ya[:, :], fn)
        if stop == f"R{l}":
            P.out_dmas.append(op_)
            break

        def load_y(ti, b, ybuf, ym=ym):
            r0, r1, ncc = (0, 256, 2) if b == 0 else (256, 768, 4)
            for j in range(4):
                src = ym[ti, j * YT:(j + 1) * YT].rr("(f t) -> f t", t=TT)[r0:r1, :].rr("(c p) t -> p c t", p=128)
                P.dma('sp', ybuf[:, j * ncc:(j + 1) * ncc, :], src)

        ioT = dict(WT[l])
        ioT.update(cst=cst, memT=memT, load_y=load_y)
        if l == 0:
            xtv = xTt[:, :].rr("(kc p) t -> p kc t", p=128)
            ioT['load_xT'] = lambda ti, xT, xtv=xtv: P.dma('pool', xT[:, :, :], xtv[:, :, ti * TT:(ti + 1) * TT])
            ioT['load_xtok'] = lambda ti, xt: P.dma('sp', xt[:, :, :], tokview(xtok_d)[:, ti * 4:(ti + 1) * 4, :])
            ioT['store_out'] = lambda ti, xt: P.dma('sp', tokview(x1_d)[:, ti * 4:(ti + 1) * 4, :], xt[:, :, :])
            pend2 = []

            def store_xT(ti, qm):
                while pend2:
                    pend2.pop(0)()
                for h in range(2):
                    n = ti * 2 + h
                    dst = x1T_loc.k(n)[n, :].rr("(c p t) -> p c t", p=128, t=TT)
                    P.dma('sp', dst, qm[:, 4 * h:4 * h + 4, :])
                    pend2.append(lambda n=n: P.allgather(x1T_all.k(n)[n, :], x1T_loc.k(n)[n, :]))
            ioT['store_xT'] = store_xT
        else:
            def load_xT_T(ti, xT):
                for h in range(2):
                    src = x1T_loc[ti * 2 + h, :].rr("(c p t) -> p c t", p=128, t=TT)
                    P.dma('pool', xT[:, 4 * h:4 * h + 4, :], src)
            ioT['load_xT'] = load_xT_T
            ioT['load_xtok'] = lambda ti, xt: P.dma('sp', xt[:, :, :], tokview(x1_d)[:, ti * 4:(ti + 1) * 4, :])
            ioT['store_out'] = lambda ti, xt: P.dma('sp', tokview(out_d)[:, ti * 4:(ti + 1) * 4, :], xt[:, :, :], is_out=True)
            ioT['store_xT'] = None
        emit_T(P, ioT)
        if l == 1 and stop == "DBG":
            dbg = P.dram("dbg_x1T", [8, XT], BF16, "ExternalOutput")
            P.dma('sp', dbg[:, :], x1T_loc[:, :], is_out=True)
        if l == 0:
            while pend2:
                pend2.pop(0)()
            P.barrier()
            if stop == "T0":
                break
    P.run()


OFF = {'lrux': 0, 'lrug': 1024, 'z': 2048, 'xbc': 4096, 'dt': 7168, 'q': 7200, 'g': 8224}
_CST = make_consts()
_prog = []


def get_prog():
    if not _prog:
        nc = bass.Bass("TRN2", target_bir_lowering=False)
        es = ExitStack()
        build_fused(nc, es)
        _prog.append((nc, es))
    return _prog[0][0]


def rep(v, n=128):
    v = np.asarray(v, np.float32)
    return np.ascontiguousarray(np.broadcast_to(v.reshape(1, -1), (n, v.size)))


def pcol(v):
    v = np.asarray(v, np.float32)
    return np.ascontiguousarray(v.reshape(-1, 128).T)


def r_inputs(inp, l, j):
    w_in = inp['w_in'][l]
    lr = slice(256 * j, 256 * (j + 1))
    gr = slice(512 * j, 512 * (j + 1))
    hr = slice(8 * j, 8 * (j + 1))
    xbc0 = OFF['xbc']
    bsl = slice(2048 + 128 * j, 2048 + 128 * (j + 1))
    csl = slice(2560 + 128 * j, 2560 + 128 * (j + 1))
    wR = np.concatenate([
        w_in[:, OFF['lrux'] + 256 * j:OFF['lrux'] + 256 * (j + 1)],
        w_in[:, OFF['lrug'] + 256 * j:OFF['lrug'] + 256 * (j + 1)],
        w_in[:, xbc0 + 512 * j:xbc0 + 512 * (j + 1)],
        w_in[:, xbc0 + bsl.start:xbc0 + bsl.stop],
        w_in[:, xbc0 + csl.start:xbc0 + csl.stop],
        w_in[:, OFF['z'] + 512 * j:OFF['z'] + 512 * (j + 1)],
        w_in[:, OFF['dt'] + 8 * j:OFF['dt'] + 8 * (j + 1)],
    ], axis=1)
    lcw = inp['lru_conv_w'][l][:, lr]
    scw = inp['ssd_conv_w'][l]
    scb = inp['ssd_conv_b'][l]
    cw_cols = np.concatenate([lcw, scw[:, gr], scw[:, bsl], scw[:, csl]], axis=1)
    cb_cols = np.concatenate([inp['lru_conv_b'][l][lr], scb[gr], scb[bsl], scb[csl]])
    cw = cw_cols.reshape(4, 8, 128).transpose(2, 1, 0).reshape(128, 32)
    prm = np.concatenate([
        cw, pcol(cb_cols), pcol(inp['lru_b_a'][l][lr]), pcol(inp['lru_b_i'][l][lr]), pcol(inp['lru_lambda'][l][lr]),
        rep(inp['ssd_dt_bias'][l][hr]), rep(inp['ssd_a_log'][l][hr]), rep(inp['ssd_d'][l][hr]),
        rep(inp['ssd_norm_w'][l][gr]),
    ], axis=1).astype(np.float32)
    wa = inp['lru_w_a'][l][2 * j:2 * j + 2]
    wi = inp['lru_w_i'][l][2 * j:2 * j + 2]
    wai = np.concatenate([wa, wi], axis=0).transpose(1, 0, 2).reshape(128, 512)
    return {f"wR{l}": np.ascontiguousarray(wR, dtype=np.float32), f"wai{l}": np.ascontiguousarray(wai, dtype=np.float32),
            f"prm{l}": np.ascontiguousarray(prm)}


def t_inputs(inp, l):
    w_in = inp['w_in'][l]
    bg = inp['b_gate'][l]
    prmT = bg.reshape(3, 8, 128).transpose(2, 0, 1).reshape(128, 24)
    lnp = np.concatenate([rep(inp['ln1_g'][l]), rep(inp['ln1_b'][l]), rep(inp['ln2_g'][l]), rep(inp['ln2_b'][l])], axis=1)
    c = np.ascontiguousarray
    return {
        f"wq{l}": c(w_in[:, OFF['q']:OFF['q'] + 1024]), f"wg{l}": c(w_in[:, OFF['g']:OFF['g'] + 3072]),
        f"wkv{l}": c(inp['mem_w_kv'][l]), f"wbl{l}": c(inp['w_br_lru'][l]), f"wbs{l}": c(inp['w_br_ssd'][l]),
        f"wbx{l}": c(inp['w_br_xa'][l]), f"wo{l}": c(inp['w_out'][l]), f"wfi{l}": c(inp['ffn_w_in'][l]),
        f"wfd{l}": c(inp['ffn_w_down'][l]), f"prmT{l}": c(prmT.astype(np.float32)), f"lnp{l}": c(lnp),
    }


def kernel(**inputs):
    inp = {k: np.asarray(v, dtype=np.float32) for k, v in inputs.items()}
    x = inp['x']
    nc = get_prog()
    xTs = [np.ascontiguousarray(x[b].T) for b in range(2)]
    memTs = [np.ascontiguousarray(inp['mem'][b].T) for b in range(2)]
    tw = {}
    for l in range(2):
        tw.update(t_inputs(inp, l))
    rw = [{} for _ in range(4)]
    for j in range(4):
        for l in range(2):
            rw[j].update(r_inputs(inp, l, j))
    in_maps = []
    for c in range(8):
        b, q = c // 4, c % 4
        m = {"xT0": xTs[b], "xTt": np.ascontiguousarray(xTs[b][:, q * NTOK:(q + 1) * NTOK]),
             "xtok": np.ascontiguousarray(x[b, q * NTOK:(q + 1) * NTOK]), "memT": memTs[b], "cst": _CST}
        m.update(tw)
        m.update(rw[q])
        in_maps.append(m)
    res = run_bass_kernel_spmd(nc, in_maps, core_ids=list(range(8)))
    out = np.empty((2, SEQ, D), np.float32)
    for c in range(8):
        b, q = c // 4, c % 4
        out[b, q * NTOK:(q + 1) * NTOK] = np.asarray(res.results[c]["out"])
    return out
```

```python
import numpy as np
from contextlib import ExitStack
import ml_dtypes
import concourse.bass as bass
import concourse.mybir as mybir
from concourse.bass_utils import run_bass_kernel_spmd

F32 = mybir.dt.float32
BF16 = mybir.dt.bfloat16
AF = mybir.ActivationFunctionType
ALU = mybir.AluOpType
AX = mybir.AxisListType

D = 1024
SEQ = 8192
NMEM = 256
DFF = 2816
ALPHA = 4.0 ** 0.25
EPS = 1e-5
TT = 512
SAME_ENG_SYNC = True
ARENA_KB = 196
GROUPS = [[0, 1, 2, 3], [4, 5, 6, 7]]
BLK = {'pe': 'tensor', 'act': 'scalar', 'dve': 'vector', 'pool': 'gpsimd', 'sp': 'sync'}


class Buf:
    __slots__ = ("w", "r")

    def __init__(self):
        self.w = []
        self.r = []


class V:
    __slots__ = ("ap", "tile", "key")

    def __init__(self, ap, tile, key):
        self.ap = ap
        self.tile = tile
        self.key = key

    def __getitem__(self, idx):
        return V(self.ap[idx], self.tile, self.key)

    def rr(self, pat, **kw):
        return V(self.ap.rearrange(pat, **kw), self.tile, self.key)

    def bc(self, shape):
        return V(self.ap.to_broadcast(list(shape)), self.tile, self.key)

    def unsq(self, ax):
        return V(self.ap.unsqueeze(ax), self.tile, self.key)

    def bitcast(self, dt):
        return V(self.ap.bitcast(dt), self.tile, self.key)


class TK:
    def __init__(self, tile, key):
        self.tile = tile
        self.key = key

    def __getitem__(self, idx):
        return V(self.tile.base[idx], self.tile, self.key)


class Tile:
    def __init__(self, base, is_psum=False):
        self.base = base
        self.whole = Buf()
        self.kids = {}
        self.is_psum = is_psum

    def __getitem__(self, idx):
        return V(self.base[idx], self, None)

    def k(self, key):
        return TK(self, key)

    def chk(self, key):
        if key is None:
            return [self.whole] + list(self.kids.values())
        if key not in self.kids:
            self.kids[key] = Buf()
        return [self.whole, self.kids[key]]

    def upd(self, key):
        if key is None:
            return self.whole
        if key not in self.kids:
            self.kids[key] = Buf()
        return self.kids[key]


class Op:
    __slots__ = ("eng", "fn", "deps", "marked", "dma", "sem", "val", "inc")


class Prog:
    def __init__(self, nc, es):
        self.nc = nc
        self.es = es
        self.ops = {e: [] for e in BLK}
        self.esem = {e: es.enter_context(nc.semaphore(f"s_{e}")) for e in ('pe', 'act', 'dve', 'pool')}
        self.dsem = {e: [es.enter_context(nc.semaphore(f"d_{e}{i}")) for i in range(8)] for e in ('sp', 'pool', 'act')}
        self.dcnt = {e: 0 for e in self.dsem}
        self.nt = 0
        self.psb = []
        for i in range(8):
            t = es.enter_context(nc.psum_tensor(f"psb{i}", [128, 512], F32))
            self.psb.append(Tile(t, is_psum=True))
        self.psi = 0
        self.pcnt = {}
        self.out_dmas = []
        self.phase_dmas = []
        self.cache = {}
        self.arena = es.enter_context(nc.sbuf_tensor("arena", [128, ARENA_KB * 512], BF16))
        self.aoff = 0

    def sb(self, shape, dt, name=None):
        n = 1
        for d in shape[1:]:
            n *= d
        nb = n * (4 if dt == F32 else 2)
        nb = (nb + 63) // 64 * 64
        off = self.aoff
        self.aoff += nb
        assert self.aoff <= ARENA_KB * 1024, f"arena overflow {self.aoff}"
        ap = self.arena[:, off // 2:(off + nb) // 2]
        if dt == F32:
            ap = ap.bitcast(F32)
        ap = ap[:, 0:n]
        if len(shape) == 3:
            ap = ap.rearrange("p (a b) -> p a b", a=shape[1])
        elif len(shape) == 4:
            ap = ap.rearrange("p (a b c) -> p a b c", a=shape[1], b=shape[2])
        return Tile(ap)

    def barrier(self):
        lasts = []
        for eng in BLK:
            for op in reversed(self.ops[eng]):
                if op.fn is not None and not op.dma:
                    op.marked = True
                    lasts.append(op)
                    break
        dmas = list(self.phase_dmas)
        self.phase_dmas = []
        for eng in BLK:
            w = Op()
            w.eng = eng
            w.fn = None
            w.marked = False
            w.dma = False
            w.sem = None
            w.val = 0
            w.deps = list(lasts) + dmas
            self.ops[eng].append(w)
        self.aoff = 0

    def allgather(self, out_v, in_v):
        self.nt += 1
        sem = self.es.enter_context(self.nc.semaphore(f"cc{self.nt}"))
        out_ap, in_ap = out_v.ap.opt(), in_v.ap.opt()
        op = self._op('pool', lambda e: e.collective_compute("AllGather", ALU.bypass, replica_groups=GROUPS, ins=[in_ap], outs=[out_ap]),
                      [out_v], [in_v], dma=True)
        self.dcnt['pool'] -= 1
        op.sem = sem
        op.val = 1
        op.inc = 1
        return op

    def dma_fn(self, q, out, in_, fn):
        return self._op(q, fn, [out], [in_], dma=True)

    def ps(self, pool=None):
        if pool is None:
            t = self.psb[self.psi % 8]
            self.psi += 1
            return t
        key = tuple(pool)
        n = self.pcnt.get(key, 0)
        self.pcnt[key] = n + 1
        return self.psb[pool[n % len(pool)]]

    def dram(self, name, shape, dt, kind=None):
        if kind is None:
            t = self.nc.dram_tensor(name, list(shape), dt)
        else:
            t = self.nc.dram_tensor(name, list(shape), dt, kind=kind)
        return Tile(t.ap())

    def _op(self, eng, fn, outs, ins, dma=False):
        op = Op()
        op.eng = eng
        op.fn = fn
        op.marked = False
        op.dma = dma
        op.sem = None
        op.val = 0
        op.inc = 16
        deps = set()
        for v in ins:
            for b in v.tile.chk(v.key):
                deps.update(b.w)
                if v.tile.is_psum:
                    deps.update(o for o in b.r if o.eng != eng)
        for v in outs:
            for b in v.tile.chk(v.key):
                deps.update(b.w)
                deps.update(b.r)
        deps.discard(op)
        keep = []
        for p in deps:
            if p.dma:
                keep.append(p)
            elif p.eng == eng:
                if eng == 'pe':
                    continue
                if dma or SAME_ENG_SYNC:
                    p.marked = True
                    keep.append(p)
            else:
                p.marked = True
                keep.append(p)
        op.deps = keep
        for v in ins:
            b = v.tile.upd(v.key)
            if not dma:
                b.r = [o for o in b.r if o.dma or o.eng != eng]
            b.r.append(op)
        for v in outs:
            b = v.tile.upd(v.key)
            b.w = [op]
            b.r = []
        if dma:
            n = self.dcnt[eng]
            self.dcnt[eng] = n + 1
            op.sem = self.dsem[eng][n % 8]
            op.val = 16 * (n // 8 + 1)
            self.phase_dmas.append(op)
        self.ops[eng].append(op)
        return op

    def mm(self, out, lhsT, rhs, start=True, stop=True):
        self._op('pe', lambda e: e.matmul(out.ap, lhsT.ap, rhs.ap, start=start, stop=stop), [out], [lhsT, rhs])

    def tr(self, out, in_, ident):
        self._op('pe', lambda e: e.transpose(out.ap, in_.ap, ident.ap), [out], [in_, ident])

    def act(self, out, in_, func, bias=None, scale=None):
        ins = [in_]
        kw = {}
        if bias is not None:
            if isinstance(bias, V):
                ins.append(bias)
                kw['bias'] = bias.ap
            else:
                kw['bias'] = float(bias)
        if scale is not None:
            if isinstance(scale, V):
                ins.append(scale)
                kw['scale'] = scale.ap
            else:
                kw['scale'] = float(scale)
        self._op('act', lambda e: e.activation(out.ap, in_.ap, func, **kw), [out], ins)

    def copy(self, out, in_, eng='act'):
        if eng == 'act':
            self._op('act', lambda e: e.copy(out.ap, in_.ap), [out], [in_])
        else:
            self._op(eng, lambda e: e.tensor_copy(out.ap, in_.ap), [out], [in_])

    def tt(self, out, in0, in1, op, eng='dve'):
        self._op(eng, lambda e: e.tensor_tensor(out.ap, in0.ap, in1.ap, op), [out], [in0, in1])

    def ts(self, out, in0, s1, s2, op0, op1=None, eng='dve'):
        ins = [in0]
        a1 = s1
        a2 = s2
        if isinstance(s1, V):
            ins.append(s1)
            a1 = s1.ap
        if isinstance(s2, V):
            ins.append(s2)
            a2 = s2.ap
        if op1 is None:
            self._op(eng, lambda e: e.tensor_scalar(out.ap, in0.ap, a1, a2, op0), [out], ins)
        else:
            self._op(eng, lambda e: e.tensor_scalar(out.ap, in0.ap, a1, a2, op0, op1), [out], ins)

    def stt(self, out, in0, scalar, in1, op0, op1):
        ins = [in0, in1]
        a = scalar
        if isinstance(scalar, V):
            ins.append(scalar)
            a = scalar.ap
        self._op('dve', lambda e: e.scalar_tensor_tensor(out.ap, in0.ap, a, in1.ap, op0, op1), [out], ins)

    def scan(self, out, d0, d1, init, op0, op1):
        ins = [d0, d1]
        a = init
        if isinstance(init, V):
            ins.append(init)
            a = init.ap
        self._op('dve', lambda e: e.tensor_tensor_scan(out.ap, d0.ap, d1.ap, a, op0, op1), [out], ins)

    def red(self, out, in_, op):
        self._op('dve', lambda e: e.tensor_reduce(out.ap, in_.ap, AX.X, op), [out], [in_])

    def recip(self, out, in_):
        self._op('dve', lambda e: e.reciprocal(out.ap, in_.ap), [out], [in_])

    def bnstats(self, out, in_):
        self._op('dve', lambda e: e.bn_stats(out.ap, in_.ap), [out], [in_])

    def bnaggr(self, out, in_):
        self._op('dve', lambda e: e.bn_aggr(out.ap, in_.ap), [out], [in_])

    def memset(self, out, val, eng='dve'):
        self._op(eng, lambda e: e.memset(out.ap, val), [out], [])

    def dma(self, q, out, in_, is_out=False):
        op = self._op(q, lambda e: e.dma_start(out=out.ap, in_=in_.ap), [out], [in_], dma=True)
        if is_out:
            self.out_dmas.append(op)
        return op

    def run(self):
        fin = Op()
        fin.eng = 'sp'
        fin.fn = None
        fin.marked = False
        fin.dma = False
        fin.deps = list(self.out_dmas)
        fin.sem = None
        fin.val = 0
        self.ops['sp'].append(fin)
        for eng, lst in self.ops.items():
            c = 0
            for op in lst:
                if op.dma:
                    continue
                if op.marked:
                    c += 1
                    op.val = c
                    op.sem = self.esem[eng]
        with self.nc.Block() as block:
            for eng in BLK:
                if not self.ops[eng]:
                    continue

                def body(e, eng=eng):
                    seen = {}
                    for op in self.ops[eng]:
                        need = {}
                        for p in op.deps:
                            key = id(p.sem)
                            if seen.get(key, 0) >= p.val:
                                continue
                            if key not in need or need[key][1] < p.val:
                                need[key] = (p.sem, p.val)
                        for key, (sem_, val_) in need.items():
                            e.wait_ge(sem_, val_)
                            seen[key] = val_
                        if op.fn is None:
                            continue
                        ins = op.fn(e)
                        if op.dma:
                            ins.then_inc(op.sem, op.inc)
                        elif op.marked:
                            ins.then_inc(op.sem, 1)

                getattr(block, BLK[eng])(body)


def make_consts():
    j = np.arange(128)[:, None]
    l = np.arange(128)[None, :]
    same = (j // 64) == (l // 64)
    c = np.zeros((128, 8, 128), np.float32)
    c[:, 0] = (j == l)
    c[:, 1] = same & (j <= l)
    c[:, 2] = same
    c[:, 3] = (j < 64)
    c[:, 4] = (j >= 64)
    c[:, 5] = same & (j > l)
    c[:, 6] = (j <= l)
    c[:, 7] = same & (l >= j)
    return np.ascontiguousarray(c.reshape(128, 1024))


def softplus(P, out, x, tmp, neg=False):
    ax, y, w, q = tmp
    P.ts(ax, x, -1.0, None, ALU.mult)
    P.tt(ax, ax, x, ALU.max)
    P.act(y, ax, AF.Exp, scale=-1.0)
    P.ts(w, y, 2.0, None, ALU.add)
    P.recip(w, w)
    P.tt(w, w, y, ALU.mult)
    P.tt(y, w, w, ALU.mult)
    P.ts(q, y, 1.0 / 13.0, None, ALU.mult)
    for c in (1.0 / 11.0, 1.0 / 9.0, 1.0 / 7.0, 1.0 / 5.0, 1.0 / 3.0):
        P.stt(q, q, c, y, ALU.add, ALU.mult)
    P.stt(q, q, 1.0, w, ALU.add, ALU.mult)
    P.ts(ax, x, (-1.0 if neg else 1.0), 0.0, ALU.mult, ALU.max)
    P.stt(out, q, 2.0, ax, ALU.mult, ALU.add)

C_LRUX, C_LRUG, C_XS, C_B, C_C, C_Z, C_DT = 0, 256, 512, 1024, 1152, 1280, 1792
NWR = 1800
P_CW, P_CB, P_BA, P_BI, P_LAM, P_DTB, P_ALOG, P_DSK, P_NW = 0, 32, 40, 42, 44, 46, 54, 62, 70
NPR = 70 + 512


def emit_R(P, io, ntiles=SEQ // TT):
    wR_d, wai_d, prm_d, cst_d, yT_d = io['wR'], io['wai'], io['prm'], io['cst'], io['yT']
    prm = P.sb([128, NPR], F32)
    cst = P.sb([128, 8, 128], F32)
    identb = P.sb([128, 128], BF16)
    wR = P.sb([128, 8, NWR], BF16)
    wai = P.sb([128, 4, 128], BF16)
    P.dma('sp', prm[:, :], prm_d[:, :])
    P.dma('sp', cst[:, :, :], cst_d[:, :].rr("p (a b) -> p a b", a=8))
    P.dma('pool', wai[:, :, :], wai_d[:, :].rr("p (a b) -> p a b", a=4))
    wRv = wR_d[:, :].rr("(kc p) n -> p kc n", p=128)
    for kc in range(8):
        P.dma('pool', wR[:, kc, :], wRv[:, kc, :])
    P.copy(identb[:, :], cst[:, 0, :])
    TriBD, OnesBD, Half0, Half1, GtBD, mask2, maskCB = (cst[:, i, :] for i in range(1, 8))

    clru = P.sb([128, 2], F32)
    spw = P.sb([128, 8], F32)
    softplus(P, clru[:, :], prm[:, P_LAM:P_LAM + 2], [spw[:, 2 * i:2 * i + 2] for i in range(4)], neg=True)
    P.ts(clru[:, :], clru[:, :], -8.0, None, ALU.mult)
    Abc = P.sb([128, 8], F32)
    P.act(Abc[:, :], prm[:, P_ALOG:P_ALOG + 8], AF.Exp)
    P.ts(Abc[:, :], Abc[:, :], -1.0, None, ALU.mult)

    cbuf = [P.sb([128, 3 + TT], F32) for _ in range(8)]
    for c in range(8):
        P.memset(cbuf[c][:, 0:3], 0.0)
    hst = P.sb([128, 2], F32)
    P.memset(hst[:, :], 0.0)
    state = P.sb([128, 8, 64], F32)
    P.memset(state[:, :, :], 0.0)
    CTp = [P.sb([128, 4, 2, 128], BF16) for _ in range(2)]
    for t in CTp:
        P.memset(t[:, :, :, :], 0.0)

    xTs = [P.sb([128, 8, TT], BF16) for _ in range(2)]
    accs = [P.sb([128, TT], F32) for _ in range(2)]
    xc = [P.sb([128, TT], F32) for _ in range(2)]
    xcb = [P.sb([128, TT], BF16) for _ in range(2)]
    xsT = [P.sb([128, 4, TT], BF16) for _ in range(2)]
    BT = [P.sb([128, TT], BF16) for _ in range(2)]
    CT = [P.sb([128, TT], BF16) for _ in range(2)]
    lr = [P.sb([128, TT], F32) for _ in range(6)]
    ylT = [P.sb([128, 2, TT], BF16) for _ in range(2)]
    ysT = [P.sb([128, 4, TT], BF16) for _ in range(2)]
    zs = [P.sb([128, 512], F32) for _ in range(2)]
    sm = [P.sb([128, 96], F32) for _ in range(2)]
    Rm = [P.sb([128, 8, 128], F32) for _ in range(2)]
    dec = [P.sb([128, 8, 128], F32) for _ in range(2)]
    CBm = [P.sb([128, 128], F32) for _ in range(2)]
    MT = [P.sb([128, 8, 128], BF16) for _ in range(2)]
    xs_tok = [P.sb([128, 8, 64], BF16) for _ in range(2)]
    B_tok = [P.sb([128, 128], BF16) for _ in range(2)]
    xdt = [P.sb([128, 8, 64], BF16) for _ in range(2)]
    xdec = [P.sb([128, 8, 64], BF16) for _ in range(2)]
    prevb = [P.sb([128, 512], BF16) for _ in range(4)]
    yb = [P.sb([128, 8, 64], F32) for _ in range(2)]
    tmpx = [P.sb([128, 8, 64], F32) for _ in range(2)]
    y3 = [P.sb([128, 512], BF16) for _ in range(2)]
    dtall = [P.sb([128, 32], F32) for _ in range(2)]
    dtw = [P.sb([128, 32], F32) for _ in range(5)]

    ccol = [C_LRUX, C_LRUX + 128, C_XS, C_XS + 128, C_XS + 256, C_XS + 384, C_B, C_C]

    PX, PY = [6, 7], [[0, 1, 2], [3, 4, 5]]

    def gen_X(ti):
        xT = xTs[ti % 2]
        io['load_xT'](ti, xT)
        xsTt, BTt, CTt, CTpt = xsT[ti % 2], BT[ti % 2], CT[ti % 2], CTp[ti % 2]

        def fm_chunk(col):
            ps = P.ps(PX)
            for kc in range(8):
                P.mm(ps[:, :], wR[:, kc, col:col + 128], xT[:, kc, :], start=(kc == 0), stop=(kc == 7))
            return ps

        for c in range(8):
            ps = fm_chunk(ccol[c])
            cb = cbuf[c]
            P.copy(cb[:, 3:3 + TT], ps[:, :])
            acc = accs[c % 2]
            P.ts(acc[:, :], cb[:, 3:3 + TT], prm[:, P_CW + c * 4 + 3:P_CW + c * 4 + 4], None, ALU.mult)
            yield
            for j in range(3):
                P.stt(acc[:, :], cb[:, j:j + TT], prm[:, P_CW + c * 4 + j:P_CW + c * 4 + j + 1], acc[:, :], ALU.mult, ALU.add)
            P.copy(cb[:, 0:3], cb[:, TT:TT + 3], eng='dve')
            yield
            bias = prm[:, P_CB + c:P_CB + c + 1]
            if c < 2:
                P.act(xc[c][:, :], acc[:, :], AF.Identity, bias=bias)
                P.act(xcb[c][:, :], acc[:, :], AF.Identity, bias=bias)
            elif c < 6:
                P.act(xsTt[:, c - 2, :], acc[:, :], AF.Silu, bias=bias)
            elif c == 6:
                P.act(BTt[:, :], acc[:, :], AF.Silu, bias=bias)
            else:
                P.act(CTt[:, :], acc[:, :], AF.Silu, bias=bias)
                a4 = acc[:, :].rr("p (s h t) -> p s h t", s=4, h=2)
                for h in range(2):
                    P.act(CTpt[:, :, h, h * 64:(h + 1) * 64], a4[:, :, h, :], AF.Silu, bias=bias)
            yield
        psd = P.ps(PX)
        for sub in range(4):
            for kc in range(8):
                P.mm(psd[:, sub * 8:(sub + 1) * 8], xT[:, kc, sub * 128:(sub + 1) * 128], wR[:, kc, C_DT:C_DT + 8], start=(kc == 0), stop=(kc == 7))
        dtx = dtw[0][:, :]
        P.tt(dtx.rr("p (s k) -> p s k", s=4), psd[:, 0:32].rr("p (s k) -> p s k", s=4), prm[:, P_DTB:P_DTB + 8].unsq(1).bc([128, 4, 8]), ALU.add)
        yield
        softplus(P, dtall[ti % 2][:, :], dtx, [t[:, :] for t in dtw[1:5]])
        yield
        ylTt = ylT[ti % 2]
        for n in range(2):
            r_, i_, a_, s_, u_, h_ = lr
            ps = P.ps(PX)
            P.mm(ps[:, :], wai[:, n, :], xcb[n][:, :])
            P.act(r_[:, :], ps[:, :], AF.Sigmoid, bias=prm[:, P_BA + n:P_BA + n + 1])
            ps = P.ps(PX)
            P.mm(ps[:, :], wai[:, 2 + n, :], xcb[n][:, :])
            P.act(i_[:, :], ps[:, :], AF.Sigmoid, bias=prm[:, P_BI + n:P_BI + n + 1])
            yield
            P.act(a_[:, :], r_[:, :], AF.Exp, scale=clru[:, n:n + 1])
            P.tt(s_[:, :], a_[:, :], a_[:, :], ALU.mult)
            P.act(s_[:, :], s_[:, :], AF.Sqrt, bias=1.0, scale=-1.0)
            yield
            P.tt(u_[:, :], i_[:, :], xc[n][:, :], ALU.mult)
            P.tt(u_[:, :], u_[:, :], s_[:, :], ALU.mult)
            yield
            P.scan(h_[:, :], a_[:, :], u_[:, :], hst[:, n:n + 1], ALU.mult, ALU.add)
            P.copy(hst[:, n:n + 1], h_[:, TT - 1:TT], eng='dve')
            yield
            psg = fm_chunk(C_LRUG + n * 128)
            P.act(r_[:, :], psg[:, :], AF.Square)
            P.ts(r_[:, :], r_[:, :], 0.044715, 1.0, ALU.mult, ALU.add)
            yield
            P.tt(r_[:, :], r_[:, :], psg[:, :], ALU.mult)
            P.act(r_[:, :], r_[:, :], AF.Sigmoid, scale=1.5957691216057308)
            yield
            P.tt(r_[:, :], r_[:, :], psg[:, :], ALU.mult)
            P.tt(ylTt[:, n, :], r_[:, :], h_[:, :], ALU.mult)
            yield
        yt2 = yT_d.k(ti)[ti, :].rr("(f t) -> f t", t=TT)
        P.dma('sp', yt2[0:256, :].rr("(c p) t -> p c t", p=128), ylTt[:, :, :])

    def gen_Y(ti, sub):
        xT = xTs[ti % 2]
        xsTt, BTt, CTt, CTpt = xsT[ti % 2], BT[ti % 2], CT[ti % 2], CTp[ti % 2]
        ysTt = ysT[ti % 2]
        q = sub % 2
        pool = PY[q]
        tok = slice(sub * 128, (sub + 1) * 128)
        smq = sm[q]
        da = smq[:, 24:32]
        cs_sb, expcs, dend, cdec = smq[:, 32:40], smq[:, 40:48], smq[:, 48:56], smq[:, 56:72]
        tmp8, ss, rstd = smq[:, 72:80], smq[:, 80:81], smq[:, 81:82]
        psz = P.ps(pool)
        for kc in range(8):
            P.mm(psz[:, :], xT[:, kc, tok], wR[:, kc, C_Z:C_Z + 512], start=(kc == 0), stop=(kc == 7))
        P.act(zs[q][:, :], psz[:, :], AF.Silu)
        dt = dtall[ti % 2][:, sub * 8:(sub + 1) * 8]
        P.tt(da, dt, Abc[:, :], ALU.mult)
        yield
        pst = P.ps(pool)
        pstb = pst[:, :].bitcast(BF16)
        for c in range(4):
            P.tr(pstb[:, c * 128:(c + 1) * 128], xsTt[:, c, tok], identb[:, :])
        P.copy(xs_tok[q][:, :, :], pstb[:, 0:512].rr("p (k d) -> p k d", k=8))
        psb_ = P.ps(pool)
        psbb = psb_[:, :].bitcast(BF16)
        P.tr(psbb[:, 0:128], BTt[:, tok], identb[:, :])
        P.copy(B_tok[q][:, :], psbb[:, 0:128])
        yield
        psc = P.ps(pool)
        P.mm(psc[:, 0:8], TriBD, da)
        P.mm(psc[:, 8:16], OnesBD, da)
        P.mm(psc[:, 16:24], Half0, da)
        P.mm(psc[:, 24:32], Half1, da)
        P.copy(cs_sb, psc[:, 0:8])
        P.act(expcs, psc[:, 0:8], AF.Exp)
        yield
        P.tt(tmp8, psc[:, 8:16], cs_sb, ALU.subtract)
        P.act(dend, tmp8, AF.Exp)
        P.act(cdec, psc[:, 16:32], AF.Exp)
        yield
        P.tt(Rm[q][:, :, :], da.unsq(2).bc([128, 8, 128]), mask2.unsq(1).bc([128, 8, 128]), ALU.mult)
        yield
        Rf = Rm[q][:, :, :].rr("p k l -> p (k l)")
        psA = P.ps(pool)
        P.mm(psA[:, :], GtBD, Rf[:, 0:512])
        psB = P.ps(pool)
        P.mm(psB[:, :], GtBD, Rf[:, 512:1024])
        df = dec[q][:, :, :].rr("p k l -> p (k l)")
        P.act(df[:, 0:512], psA[:, :], AF.Exp)
        P.act(df[:, 512:1024], psB[:, :], AF.Exp)
        yield
        psC = P.ps(pool)
        P.mm(psC[:, 0:128], BTt[:, tok], CTt[:, tok])
        P.tt(CBm[q][:, :], psC[:, 0:128], maskCB, ALU.mult)
        yield
        P.tt(MT[q][:, :, :], dec[q][:, :, :], CBm[q][:, :].unsq(1).bc([128, 8, 128]), ALU.mult)
        yield
        P.tt(xdt[q][:, :, :], xs_tok[q][:, :, :], dt.unsq(2).bc([128, 8, 64]), ALU.mult)
        P.tt(xdec[q][:, :, :], xdt[q][:, :, :], dend.unsq(2).bc([128, 8, 64]), ALU.mult)
        yield
        psO = P.ps(pool)
        xdf = xdec[q][:, :, :].rr("p k d -> p (k d)")
        sf = state[:, :, :].rr("p k d -> p (k d)")
        for h in range(2):
            psS = P.ps(pool)
            P.mm(psS[:, :], B_tok[q][64 * h:64 * h + 64, :], xdf[64 * h:64 * h + 64, :])
            pv = prevb[(2 * q + h)]
            P.copy(pv[:, :], sf)
            P.tt(state[:, :, :], state[:, :, :], cdec[:, 8 * h:8 * h + 8].unsq(2).bc([128, 8, 64]), ALU.mult)
            P.tt(sf, sf, psS[:, :], ALU.add)
            P.mm(psO[:, :], CTpt[:, sub, h, :], pv[:, :], start=(h == 0), stop=(h == 1))
            yield
        y = yb[q]
        yf = y[:, :, :].rr("p k d -> p (k d)")
        P.tt(y[:, :, :], psO[:, :].rr("p (k d) -> p k d", k=8), expcs.unsq(2).bc([128, 8, 64]), ALU.mult)
        yield
        psY = P.ps(pool)
        for k in range(8):
            P.mm(psY[:, k * 64:(k + 1) * 64], MT[q][:, k, :], xdt[q][:, k, :])
        P.tt(yf, yf, psY[:, :], ALU.add)
        yield
        P.tt(tmpx[q][:, :, :], xs_tok[q][:, :, :], prm[:, P_DSK:P_DSK + 8].unsq(2).bc([128, 8, 64]), ALU.mult)
        P.tt(y[:, :, :], y[:, :, :], tmpx[q][:, :, :], ALU.add)
        yield
        P.tt(yf, yf, zs[q][:, :], ALU.mult)
        tf = tmpx[q][:, :, :].rr("p k d -> p (k d)")
        P.tt(tf, yf, yf, ALU.mult)
        yield
        P.red(ss, tf, ALU.add)
        P.ts(ss, ss, 1.0 / 512.0, EPS, ALU.mult, ALU.add)
        P.act(ss, ss, AF.Sqrt)
        P.recip(rstd, ss)
        yield
        P.stt(y3[q][:, :], yf, rstd, prm[:, P_NW:P_NW + 512], ALU.mult, ALU.mult)
        psT = P.ps(pool)
        psTb = psT[:, :].bitcast(BF16)
        for c in range(4):
            P.tr(psTb[:, c * 128:(c + 1) * 128], y3[q][:, c * 128:(c + 1) * 128], identb[:, :])
        P.copy(ysTt[:, :, tok], psTb[:, 0:512].rr("p (c t) -> p c t", c=4))

    def run_tile(ygens, bg):
        SKEW = 5
        active = []
        pending = list(ygens)
        bg_done = bg is None
        while pending or active:
            if pending and len(active) < 2 and (not active or active[-1][1] >= SKEW):
                active.append([pending.pop(0), 0])
            for a in list(active):
                try:
                    next(a[0])
                    a[1] += 1
                except StopIteration:
                    active.remove(a)
            if not bg_done:
                try:
                    next(bg)
                except StopIteration:
                    bg_done = True
        while not bg_done:
            try:
                next(bg)
            except StopIteration:
                bg_done = True

    run_tile([], gen_X(0))
    for ti in range(ntiles):
        bg = gen_X(ti + 1) if ti + 1 < ntiles else None
        run_tile([gen_Y(ti, sub) for sub in range(4)], bg)
        yt2 = yT_d.k(ti)[ti, :].rr("(f t) -> f t", t=TT)
        P.dma('sp', yt2[256:768, :].rr("(c p) t -> p c t", p=128), ysT[ti % 2][:, :, :])
        io['tile_done'](ti)


NTOK = 2048
P_BG = 0
NPT = 24


class WStream:
    def __init__(self, P, nslots, cache=None):
        self.P = P
        self.slots = [P.sb([128, 8, 512], BF16) for _ in range(nslots)]
        self.i = 0
        self.cache = cache
        self.ids = {}

    def load(self, wd, r0, nrows, c0, ncols=512, cacheable=True):
        P = self.P
        slot = self.slots[self.i % len(self.slots)]
        self.i += 1
        nkc = nrows // 128
        sv = slot[:, 0:nkc, 0:ncols]
        key = (id(wd), r0, c0)
        if self.cache is not None and cacheable and key in self.ids:
            pid = self.ids[key]
            cv = self.cache.k(pid)[pid, 0:128 * nkc * ncols].rr("(p k n) -> p k n", p=128, k=nkc)
            P.dma('sp', sv, cv)
            return slot
        src = wd[r0:r0 + nrows, c0:c0 + ncols].rr("(kc p) n -> p kc n", p=128)
        P.dma('pool', sv, src)
        if self.cache is not None and cacheable:
            pid = len(self.ids)
            self.ids[key] = pid
            cv = self.cache.k(pid)[pid, 0:128 * nkc * ncols].rr("(p k n) -> p k n", p=128, k=nkc)
            P.dma('sp', cv, sv)
        return slot


def emit_T(P, io, ntiles=NTOK // TT):
    ws = WStream(P, 4, io.get('wcache'))
    prmT = P.sb([128, NPT], F32)
    lnp = P.sb([128, 4, D], F32)
    identf = P.sb([128, 128], F32)
    identb = P.sb([128, 128], BF16)
    P.dma('sp', prmT[:, :], io['prmT'][:, :])
    P.dma('sp', lnp[:, :, :], io['lnp'][:, :].rr("p (a b) -> p a b", a=4))
    P.dma('sp', identf[:, :], io['cst'][:, 0:128])
    P.copy(identb[:, :], identf[:, :])

    memT = P.sb([128, 8, NMEM], BF16)
    P.dma('pool', memT[:, :, :], io['memT'][:, :].rr("(kc p) m -> p kc m", p=128))
    KT = P.sb([128, 8, NMEM], BF16)
    Vt = P.sb([128, 2, 1024], BF16)
    for ph in range(2):
        pan = ws.load(io['wkv'], 0, 1024, ph * 512, cacheable=False)
        for c in range(4):
            ps = P.ps()
            for kc in range(8):
                P.mm(ps[:, 0:NMEM], pan[:, kc, c * 128:(c + 1) * 128], memT[:, kc, :], start=(kc == 0), stop=(kc == 7))
            P.copy(KT.k(ph * 4 + c)[:, ph * 4 + c, :], ps[:, 0:NMEM])
    for ph in range(2):
        pan = ws.load(io['wkv'], 0, 1024, 1024 + ph * 512, cacheable=False)
        for mc in range(2):
            ps = P.ps()
            for kc in range(8):
                P.mm(ps[:, :], memT[:, kc, mc * 128:(mc + 1) * 128], pan[:, kc, :], start=(kc == 0), stop=(kc == 7))
            P.copy(Vt.k((mc, ph))[:, mc, ph * 512:(ph + 1) * 512], ps[:, :])

    xT = P.sb([128, 8, TT], BF16)
    xtok = P.sb([128, 4, D], F32)
    x1 = P.sb([128, 4, D], F32)
    qm = P.sb([128, 8, TT], BF16)
    x1T = P.sb([128, 8, TT], BF16)
    PT = [P.sb([128, 2, TT], BF16) for _ in range(2)]
    yxaT = P.sb([128, 8, TT], BF16)
    ybuf = P.sb([128, 16, TT], BF16)
    acc = P.sb([128, 8, TT], F32)
    hT = P.sb([128, 22, TT], BF16)
    gs = [P.sb([128, TT], F32) for _ in range(2)]
    es_ = [P.sb([128, NMEM], F32) for _ in range(2)]
    pn = [P.sb([128, NMEM], BF16) for _ in range(2)]
    sm = [P.sb([128, 8], F32) for _ in range(2)]
    st = [P.sb([128, 12], F32) for _ in range(2)]
    mv = [P.sb([128, 4], F32) for _ in range(2)]

    lncnt = [0]

    def layer_norm(v, g, b):
        i = lncnt[0] % 2
        lncnt[0] += 1
        P.bnstats(st[i][:, 0:6], v[:, 0:512])
        P.bnstats(st[i][:, 6:12], v[:, 512:1024])
        P.bnaggr(mv[i][:, 0:2], st[i][:, :])
        P.ts(mv[i][:, 2:3], mv[i][:, 1:2], EPS, None, ALU.add)
        P.act(mv[i][:, 2:3], mv[i][:, 2:3], AF.Sqrt)
        P.recip(mv[i][:, 3:4], mv[i][:, 2:3])
        P.ts(v, v, mv[i][:, 0:1], mv[i][:, 3:4], ALU.subtract, ALU.mult)
        P.tt(v, v, g, ALU.mult)
        P.tt(v, v, b, ALU.add)

    for ti in range(ntiles):
        t0 = ti * TT
        io['load_xT'](ti, xT)
        io['load_xtok'](ti, xtok)
        for ph in range(2):
            pan = ws.load(io['wq'], 0, 1024, ph * 512)
            for c in range(4):
                dc = ph * 4 + c
                ps = P.ps()
                for kc in range(8):
                    P.mm(ps[:, :], pan[:, kc, c * 128:(c + 1) * 128], xT[:, kc, :], start=(kc == 0), stop=(kc == 7))
                P.copy(qm.k(dc)[:, dc, :], ps[:, :])
        def gen_A(pool):
            cnt = 0
            for h in range(4):
                PTh = PT[h % 2]
                for s in range(4):
                    i = cnt % 2
                    cnt += 1
                    tok = slice(s * 128, (s + 1) * 128)
                    ps = P.ps(pool)
                    for dd in range(2):
                        P.mm(ps[:, 0:NMEM], qm.k(2 * h + dd)[:, 2 * h + dd, tok], KT.k(2 * h + dd)[:, 2 * h + dd, :], start=(dd == 0), stop=(dd == 1))
                    mx, nb, sm_, rs = sm[i][:, 0:1], sm[i][:, 1:2], sm[i][:, 2:3], sm[i][:, 3:4]
                    P.red(mx, ps[:, 0:NMEM], ALU.max)
                    P.ts(nb, mx, -1.0 / 16.0, None, ALU.mult)
                    yield
                    P.act(es_[i][:, :], ps[:, 0:NMEM], AF.Exp, bias=nb, scale=1.0 / 16.0)
                    P.red(sm_, es_[i][:, :], ALU.add)
                    P.recip(rs, sm_)
                    yield
                    P.ts(pn[i][:, :], es_[i][:, :], rs, None, ALU.mult)
                    pst = P.ps(pool)
                    pstb = pst[:, :].bitcast(BF16)
                    for mc in range(2):
                        P.tr(pstb[:, mc * 128:(mc + 1) * 128], pn[i][:, mc * 128:(mc + 1) * 128], identb[:, :])
                    P.copy(PTh[:, :, tok], pstb[:, 0:256].rr("p (m t) -> p m t", m=2))
                    yield
                for dd in range(2):
                    ps = P.ps(pool)
                    for mc in range(2):
                        P.mm(ps[:, :], Vt[:, mc, h * 256 + dd * 128:h * 256 + (dd + 1) * 128], PTh[:, mc, :], start=(mc == 0), stop=(mc == 1))
                    P.copy(yxaT.k(2 * h + dd)[:, 2 * h + dd, :], ps[:, :])
                yield

        def gen_B(bs, pool):
            for b in bs:
                wname, nkp = [('wbl', 1), ('wbs', 2), ('wbx', 1)][b]
                if b < 2:
                    io['load_y'](ti, b, ybuf)
                ysb = ybuf if b < 2 else yxaT
                for ph in range(2):
                    pans = [ws.load(io[wname], kp * 1024, 1024, ph * 512) for kp in range(nkp)]
                    gp = ws.load(io['wg'], 0, 1024, b * 1024 + ph * 512)
                    for c in range(4):
                        dc = ph * 4 + c
                        psp = P.ps(pool)
                        n = nkp * 8
                        i = 0
                        for kp in range(nkp):
                            for kc in range(8):
                                P.mm(psp[:, :], pans[kp][:, kc, c * 128:(c + 1) * 128], ysb[:, kp * 8 + kc, :], start=(i == 0), stop=(i == n - 1))
                                i += 1
                            yield
                        psg = P.ps(pool)
                        for kc in range(8):
                            P.mm(psg[:, :], gp[:, kc, c * 128:(c + 1) * 128], xT[:, kc, :], start=(kc == 0), stop=(kc == 7))
                        g = gs[dc % 2]
                        P.act(g[:, :], psg[:, :], AF.Sigmoid, bias=prmT[:, P_BG + b * 8 + dc:P_BG + b * 8 + dc + 1])
                        yield
                        if b == 0:
                            P.tt(acc.k(dc)[:, dc, :], g[:, :], psp[:, :], ALU.mult)
                        else:
                            P.tt(g[:, :], g[:, :], psp[:, :], ALU.mult)
                            if b == 1:
                                P.tt(acc.k(dc)[:, dc, :], acc.k(dc)[:, dc, :], g[:, :], ALU.add)
                            else:
                                P.tt(qm.k(dc)[:, dc, :], acc.k(dc)[:, dc, :], g[:, :], ALU.add)
                        yield

        gens = [gen_A([0, 1, 2]), gen_B([0, 1], [3, 4, 5, 6, 7])]
        while gens:
            for g_ in list(gens):
                try:
                    next(g_)
                except StopIteration:
                    gens.remove(g_)
        for _ in gen_B([2], None):
            pass
        for ph in range(2):
            pan = ws.load(io['wo'], 0, 1024, ph * 512)
            for s in range(4):
                tok = slice(s * 128, (s + 1) * 128)
                ps = P.ps()
                for kc in range(8):
                    P.mm(ps[:, :], qm[:, kc, tok], pan[:, kc, :], start=(kc == 0), stop=(kc == 7))
                P.stt(x1.k(s)[:, s, ph * 512:(ph + 1) * 512], xtok.k(s)[:, s, ph * 512:(ph + 1) * 512], ALPHA, ps[:, :], ALU.mult, ALU.add)
        for s in range(4):
            tok = slice(s * 128, (s + 1) * 128)
            layer_norm(x1.k(s)[:, s, :], lnp[:, 0, :], lnp[:, 1, :])
            for g4 in range(2):
                ps = P.ps()
                for j in range(4):
                    dc = g4 * 4 + j
                    P.tr(ps[:, j * 128:(j + 1) * 128], x1.k(s)[:, s, dc * 128:(dc + 1) * 128], identf[:, :])
                P.copy(x1T[:, g4 * 4:(g4 + 1) * 4, tok], ps[:, :].rr("p (j t) -> p j t", j=4))
        for p in range(11):
            pan = ws.load(io['wfi'], 0, 1024, p * 512)
            for c in range(4):
                col = p * 512 + c * 128
                ps = P.ps()
                for kc in range(8):
                    P.mm(ps[:, :], pan[:, kc, c * 128:(c + 1) * 128], x1T[:, kc, :], start=(kc == 0), stop=(kc == 7))
                if col < DFF:
                    ffc = col // 128
                    P.act(hT.k(ffc)[:, ffc, :], ps[:, :], AF.Silu)
                else:
                    ffc = (col - DFF) // 128
                    P.tt(hT.k(ffc)[:, ffc, :], hT.k(ffc)[:, ffc, :], ps[:, :], ALU.mult)
        for ph in range(2):
            pss = [P.ps() for _ in range(4)]
            for (r0, nr) in [(0, 1024), (1024, 1024), (2048, 768)]:
                pan = ws.load(io['wfd'], r0, nr, ph * 512)
                for s in range(4):
                    tok = slice(s * 128, (s + 1) * 128)
                    for kc in range(nr // 128):
                        ffc = r0 // 128 + kc
                        P.mm(pss[s][:, :], hT.k(ffc)[:, ffc, tok], pan[:, kc, :], start=(ffc == 0), stop=(ffc == 21))
            for s in range(4):
                P.stt(xtok.k(s)[:, s, ph * 512:(ph + 1) * 512], x1.k(s)[:, s, ph * 512:(ph + 1) * 512], ALPHA, pss[s][:, :], ALU.mult, ALU.add)
        for s in range(4):
            layer_norm(xtok.k(s)[:, s, :], lnp[:, 2, :], lnp[:, 3, :])
        io['store_out'](ti, xtok)
        if io.get('store_xT') is not None:
            for s in range(4):
                tok = slice(s * 128, (s + 1) * 128)
                for g4 in range(2):
                    ps = P.ps()
                    for j in range(4):
                        dc = g4 * 4 + j
                        P.tr(ps[:, j * 128:(j + 1) * 128], xtok.k(s)[:, s, dc * 128:(dc + 1) * 128], identf[:, :])
                    P.copy(qm[:, g4 * 4:(g4 + 1) * 4, tok], ps[:, :].rr("p (j t) -> p j t", j=4))
            io['store_xT'](ti, qm)


R_W = {'wR': [D, NWR], 'wai': [128, 512], 'prm': [128, NPR]}
T_W = {'wq': [D, 1024], 'wg': [D, 3072], 'wkv': [D, 2048], 'wbl': [1024, D], 'wbs': [2048, D], 'wbx': [1024, D],
       'wo': [D, D], 'wfi': [D, 2 * DFF], 'wfd': [DFF, D], 'prmT': [128, NPT], 'lnp': [128, 4 * D]}


def build_fused(nc, es, stop=None):
    P = Prog(nc, es)

    def ext(n, sh):
        return P.dram(n, sh, F32, "ExternalInput")

    xT0 = ext("xT0", [D, SEQ])
    xTt = ext("xTt", [D, NTOK])
    xtok_d = ext("xtok", [NTOK, D])
    memT = ext("memT", [D, NMEM])
    cst = ext("cst", [128, 1024])
    out_d = P.dram("out", [NTOK, D], F32, "ExternalOutput")
    WR = [{k: ext(f"{k}{l}", sh) for k, sh in R_W.items()} for l in range(2)]
    WT = [{k: ext(f"{k}{l}", sh) for k, sh in T_W.items()} for l in range(2)]
    YT = 768 * TT
    y_loc = [P.dram(f"y_loc{l}", [16, YT], BF16) for l in range(2)]
    y_all = [P.dram(f"y_all{l}", [16, 4 * YT], BF16) for l in range(2)]
    y_mine = [P.dram(f"y_mine{l}", [4, 4 * YT], BF16, "ExternalOutput" if stop == f"R{l}" else None) for l in range(2)]
    x1_d = P.dram("x1_d", [NTOK, D], F32, "ExternalOutput" if stop in ("T0", "DBG") else None)
    wcache = [P.dram(f"wcache{l}", [36, 128 * 8 * 512], BF16) for l in range(2)]
    XT = 512 * TT
    x1T_loc = P.dram("x1T_loc", [8, XT], BF16)
    x1T_all = P.dram("x1T_all", [8, 4 * XT], BF16)

    def tokview(t):
        return t[:, :].rr("(s p) d -> p s d", p=128)

    for l in range(2):
        if l == 0:
            xv = xT0[:, :].rr("(kc p) t -> p kc t", p=128)

            def load_xT_R(ti, xT, xv=xv):
                P.dma('pool', xT[:, :, :], xv[:, :, ti * TT:(ti + 1) * TT])
        else:
            def load_xT_R(ti, xT):
                q, i = ti // 4, ti % 4
                for h in range(2):
                    src = x1T_all[i * 2 + h, q * XT:(q + 1) * XT].rr("(c p t) -> p c t", p=128, t=TT)
                    P.dma('pool', xT[:, 4 * h:4 * h + 4, :], src)
        yl, yal = y_loc[l], y_all[l]

        pend = []

        def tile_done(ti, yl=yl, yal=yal, pend=pend):
            while pend:
                pend.pop(0)()
            pend.append(lambda: P.allgather(yal.k(ti)[ti, :], yl.k(ti)[ti, :]))
        ioR = dict(WR[l])
        ioR.update(cst=cst, yT=y_loc[l], load_xT=load_xT_R, tile_done=tile_done)
        emit_R(P, ioR)
        while pend:
            pend.pop(0)()
        P.barrier()

        ya = y_all[l]

        ym = y_mine[l]

        def fn(e, ym=ym, ya=ya):
            q = e.partition_id() % 4
            yv = ya.base.rearrange("(q i) (r c) -> q (i r) c", q=4, c=16384)
            src = yv[bass.ds(q, 1), :, :].rearrange("o r c -> (o r) c")
            dst = ym.base.rearrange("i (r c) -> (i r) c", c=16384)
            return e.dma_start(out=dst, in_=src)
        op_ = P.dma_fn('pool', ym[:, :], ya[:, :], fn)
        if stop == f"R{l}":
            P.out_dmas.append(op_)
            break

        def load_y(ti, b, ybuf, ym=ym):
            r0, r1, ncc = (0, 256, 2) if b == 0 else (256, 768, 4)
            for j in range(4):
                src = ym[ti, j * YT:(j + 1) * YT].rr("(f t) -> f t", t=TT)[r0:r1, :].rr("(c p) t -> p c t", p=128)
                P.dma('sp', ybuf[:, j * ncc:(j + 1) * ncc, :], src)

        ioT = dict(WT[l])
        ioT.update(cst=cst, memT=memT, load_y=load_y, wcache=wcache[l])
        if l == 0:
            xtv = xTt[:, :].rr("(kc p) t -> p kc t", p=128)
            ioT['load_xT'] = lambda ti, xT, xtv=xtv: P.dma('pool', xT[:, :, :], xtv[:, :, ti * TT:(ti + 1) * TT])
            ioT['load_xtok'] = lambda ti, xt: P.dma('sp', xt[:, :, :], tokview(xtok_d)[:, ti * 4:(ti + 1) * 4, :])
            ioT['store_out'] = lambda ti, xt: P.dma('sp', tokview(x1_d)[:, ti * 4:(ti + 1) * 4, :], xt[:, :, :])
            pend2 = []

            def store_xT(ti, qm):
                while pend2:
                    pend2.pop(0)()
                for h in range(2):
                    n = ti * 2 + h
                    dst = x1T_loc.k(n)[n, :].rr("(c p t) -> p c t", p=128, t=TT)
                    P.dma('sp', dst, qm[:, 4 * h:4 * h + 4, :])
                    pend2.append(lambda n=n: P.allgather(x1T_all.k(n)[n, :], x1T_loc.k(n)[n, :]))
            ioT['store_xT'] = store_xT
        else:
            def load_xT_T(ti, xT):
                for h in range(2):
                    src = x1T_loc[ti * 2 + h, :].rr("(c p t) -> p c t", p=128, t=TT)
                    P.dma('pool', xT[:, 4 * h:4 * h + 4, :], src)
            ioT['load_xT'] = load_xT_T
            ioT['load_xtok'] = lambda ti, xt: P.dma('sp', xt[:, :, :], tokview(x1_d)[:, ti * 4:(ti + 1) * 4, :])
            ioT['store_out'] = lambda ti, xt: P.dma('sp', tokview(out_d)[:, ti * 4:(ti + 1) * 4, :], xt[:, :, :], is_out=True)
            ioT['store_xT'] = None
        emit_T(P, ioT)
        if l == 1 and stop == "DBG":
            dbg = P.dram("dbg_x1T", [8, XT], BF16, "ExternalOutput")
            P.dma('sp', dbg[:, :], x1T_loc[:, :], is_out=True)
        if l == 0:
            while pend2:
                pend2.pop(0)()
            P.barrier()
            if stop == "T0":
                break
    P.run()


OFF = {'lrux': 0, 'lrug': 1024, 'z': 2048, 'xbc': 4096, 'dt': 7168, 'q': 7200, 'g': 8224}
_CST = make_consts()
_prog = []


def get_prog():
    if not _prog:
        nc = bass.Bass("TRN2", target_bir_lowering=False)
        es = ExitStack()
        build_fused(nc, es)
        _prog.append((nc, es))
    return _prog[0][0]


def rep(v, n=128):
    v = np.asarray(v, np.float32)
    return np.ascontiguousarray(np.broadcast_to(v.reshape(1, -1), (n, v.size)))


def pcol(v):
    v = np.asarray(v, np.float32)
    return np.ascontiguousarray(v.reshape(-1, 128).T)


def r_inputs(inp, l, j):
    w_in = inp['w_in'][l]
    lr = slice(256 * j, 256 * (j + 1))
    gr = slice(512 * j, 512 * (j + 1))
    hr = slice(8 * j, 8 * (j + 1))
    xbc0 = OFF['xbc']
    bsl = slice(2048 + 128 * j, 2048 + 128 * (j + 1))
    csl = slice(2560 + 128 * j, 2560 + 128 * (j + 1))
    wR = np.concatenate([
        w_in[:, OFF['lrux'] + 256 * j:OFF['lrux'] + 256 * (j + 1)],
        w_in[:, OFF['lrug'] + 256 * j:OFF['lrug'] + 256 * (j + 1)],
        w_in[:, xbc0 + 512 * j:xbc0 + 512 * (j + 1)],
        w_in[:, xbc0 + bsl.start:xbc0 + bsl.stop],
        w_in[:, xbc0 + csl.start:xbc0 + csl.stop],
        w_in[:, OFF['z'] + 512 * j:OFF['z'] + 512 * (j + 1)],
        w_in[:, OFF['dt'] + 8 * j:OFF['dt'] + 8 * (j + 1)],
    ], axis=1)
    lcw = inp['lru_conv_w'][l][:, lr]
    scw = inp['ssd_conv_w'][l]
    scb = inp['ssd_conv_b'][l]
    cw_cols = np.concatenate([lcw, scw[:, gr], scw[:, bsl], scw[:, csl]], axis=1)
    cb_cols = np.concatenate([inp['lru_conv_b'][l][lr], scb[gr], scb[bsl], scb[csl]])
    cw = cw_cols.reshape(4, 8, 128).transpose(2, 1, 0).reshape(128, 32)
    prm = np.concatenate([
        cw, pcol(cb_cols), pcol(inp['lru_b_a'][l][lr]), pcol(inp['lru_b_i'][l][lr]), pcol(inp['lru_lambda'][l][lr]),
        rep(inp['ssd_dt_bias'][l][hr]), rep(inp['ssd_a_log'][l][hr]), rep(inp['ssd_d'][l][hr]),
        rep(inp['ssd_norm_w'][l][gr]),
    ], axis=1).astype(np.float32)
    wa = inp['lru_w_a'][l][2 * j:2 * j + 2]
    wi = inp['lru_w_i'][l][2 * j:2 * j + 2]
    wai = np.concatenate([wa, wi], axis=0).transpose(1, 0, 2).reshape(128, 512)
    return {f"wR{l}": np.ascontiguousarray(wR, dtype=np.float32), f"wai{l}": np.ascontiguousarray(wai, dtype=np.float32),
            f"prm{l}": np.ascontiguousarray(prm)}


def t_inputs(inp, l):
    w_in = inp['w_in'][l]
    bg = inp['b_gate'][l]
    prmT = bg.reshape(3, 8, 128).transpose(2, 0, 1).reshape(128, 24)
    lnp = np.concatenate([rep(inp['ln1_g'][l]), rep(inp['ln1_b'][l]), rep(inp['ln2_g'][l]), rep(inp['ln2_b'][l])], axis=1)
    c = np.ascontiguousarray
    return {
        f"wq{l}": c(w_in[:, OFF['q']:OFF['q'] + 1024]), f"wg{l}": c(w_in[:, OFF['g']:OFF['g'] + 3072]),
        f"wkv{l}": c(inp['mem_w_kv'][l]), f"wbl{l}": c(inp['w_br_lru'][l]), f"wbs{l}": c(inp['w_br_ssd'][l]),
        f"wbx{l}": c(inp['w_br_xa'][l]), f"wo{l}": c(inp['w_out'][l]), f"wfi{l}": c(inp['ffn_w_in'][l]),
        f"wfd{l}": c(inp['ffn_w_down'][l]), f"prmT{l}": c(prmT.astype(np.float32)), f"lnp{l}": c(lnp),
    }


def kernel(**inputs):
    inp = {k: np.asarray(v, dtype=np.float32) for k, v in inputs.items()}
    x = inp['x']
    nc = get_prog()
    xTs = [np.ascontiguousarray(x[b].T) for b in range(2)]
    memTs = [np.ascontiguousarray(inp['mem'][b].T) for b in range(2)]
    tw = {}
    for l in range(2):
        tw.update(t_inputs(inp, l))
    rw = [{} for _ in range(4)]
    for j in range(4):
        for l in range(2):
            rw[j].update(r_inputs(inp, l, j))
    in_maps = []
    for c in range(8):
        b, q = c // 4, c % 4
        m = {"xT0": xTs[b], "xTt": np.ascontiguousarray(xTs[b][:, q * NTOK:(q + 1) * NTOK]),
             "xtok": np.ascontiguousarray(x[b, q * NTOK:(q + 1) * NTOK]), "memT": memTs[b], "cst": _CST}
        m.update(tw)
        m.update(rw[q])
        in_maps.append(m)
    res = run_bass_kernel_spmd(nc, in_maps, core_ids=list(range(8)))
    out = np.empty((2, SEQ, D), np.float32)
    for c in range(8):
        b, q = c // 4, c % 4
        out[b, q * NTOK:(q + 1) * NTOK] = np.asarray(res.results[c]["out"])
    return out
```

```python
import numpy as np
from contextlib import ExitStack
import ml_dtypes
import concourse.bass as bass
import concourse.mybir as mybir
from concourse.bass_utils import run_bass_kernel_spmd

F32 = mybir.dt.float32
BF16 = mybir.dt.bfloat16
AF = mybir.ActivationFunctionType
ALU = mybir.AluOpType
AX = mybir.AxisListType

D = 1024
SEQ = 8192
NMEM = 256
DFF = 2816
ALPHA = 4.0 ** 0.25
EPS = 1e-5
TT = 512
SAME_ENG_SYNC = True
ARENA_KB = 196
GROUPS = [[0, 1, 2, 3], [4, 5, 6, 7]]
BLK = {'pe': 'tensor', 'act': 'scalar', 'dve': 'vector', 'pool': 'gpsimd', 'sp': 'sync'}


class Buf:
    __slots__ = ("w", "r", "wsig")

    def __init__(self):
        self.w = []
        self.r = []
        self.wsig = None


def apsig(v):
    a = v.ap
    n = 1
    for d in a.shape[1:]:
        n *= d
    return (n, tuple(map(tuple, a.ap)), a.offset, str(a.dtype))


class V:
    __slots__ = ("ap", "tile", "key")

    def __init__(self, ap, tile, key):
        self.ap = ap
        self.tile = tile
        self.key = key

    def __getitem__(self, idx):
        return V(self.ap[idx], self.tile, self.key)

    def rr(self, pat, **kw):
        return V(self.ap.rearrange(pat, **kw), self.tile, self.key)

    def bc(self, shape):
        return V(self.ap.to_broadcast(list(shape)), self.tile, self.key)

    def unsq(self, ax):
        return V(self.ap.unsqueeze(ax), self.tile, self.key)

    def bitcast(self, dt):
        return V(self.ap.bitcast(dt), self.tile, self.key)


class TK:
    def __init__(self, tile, key):
        self.tile = tile
        self.key = key

    def __getitem__(self, idx):
        return V(self.tile.base[idx], self.tile, self.key)


class Tile:
    def __init__(self, base, is_psum=False):
        self.base = base
        self.whole = Buf()
        self.kids = {}
        self.is_psum = is_psum

    def __getitem__(self, idx):
        return V(self.base[idx], self, None)

    def k(self, key):
        return TK(self, key)

    def chk(self, key):
        if key is None:
            return [self.whole] + list(self.kids.values())
        if key not in self.kids:
            self.kids[key] = Buf()
        return [self.whole, self.kids[key]]

    def upd(self, key):
        if key is None:
            return self.whole
        if key not in self.kids:
            self.kids[key] = Buf()
        return self.kids[key]


class Op:
    __slots__ = ("eng", "fn", "deps", "marked", "dma", "sem", "val", "inc")


class Prog:
    def __init__(self, nc, es):
        self.nc = nc
        self.es = es
        self.ops = {e: [] for e in BLK}
        self.esem = {e: es.enter_context(nc.semaphore(f"s_{e}")) for e in ('pe', 'act', 'dve', 'pool')}
        self.dsem = {e: [es.enter_context(nc.semaphore(f"d_{e}{i}")) for i in range(8)] for e in ('sp', 'pool', 'act')}
        self.dcnt = {e: 0 for e in self.dsem}
        self.nt = 0
        self.psb = []
        for i in range(8):
            t = es.enter_context(nc.psum_tensor(f"psb{i}", [128, 512], F32))
            self.psb.append(Tile(t, is_psum=True))
        self.psi = 0
        self.pcnt = {}
        self.out_dmas = []
        self.phase_dmas = []
        self.cache = {}
        self.arena = es.enter_context(nc.sbuf_tensor("arena", [128, ARENA_KB * 512], BF16))
        self.aoff = 0

    def sb(self, shape, dt, name=None):
        n = 1
        for d in shape[1:]:
            n *= d
        nb = n * (4 if dt == F32 else 2)
        nb = (nb + 63) // 64 * 64
        off = self.aoff
        self.aoff += nb
        assert self.aoff <= ARENA_KB * 1024, f"arena overflow {self.aoff}"
        ap = self.arena[:, off // 2:(off + nb) // 2]
        if dt == F32:
            ap = ap.bitcast(F32)
        ap = ap[:, 0:n]
        if len(shape) == 3:
            ap = ap.rearrange("p (a b) -> p a b", a=shape[1])
        elif len(shape) == 4:
            ap = ap.rearrange("p (a b c) -> p a b c", a=shape[1], b=shape[2])
        return Tile(ap)

    def barrier(self):
        lasts = []
        for eng in BLK:
            for op in reversed(self.ops[eng]):
                if op.fn is not None and not op.dma:
                    op.marked = True
                    lasts.append(op)
                    break
        dmas = list(self.phase_dmas)
        self.phase_dmas = []
        for eng in BLK:
            w = Op()
            w.eng = eng
            w.fn = None
            w.marked = False
            w.dma = False
            w.sem = None
            w.val = 0
            w.deps = list(lasts) + dmas
            self.ops[eng].append(w)
        self.aoff = 0

    def allgather(self, out_v, in_v):
        self.nt += 1
        sem = self.es.enter_context(self.nc.semaphore(f"cc{self.nt}"))
        out_ap, in_ap = out_v.ap.opt(), in_v.ap.opt()
        op = self._op('pool', lambda e: e.collective_compute("AllGather", ALU.bypass, replica_groups=GROUPS, ins=[in_ap], outs=[out_ap]),
                      [out_v], [in_v], dma=True)
        self.dcnt['pool'] -= 1
        op.sem = sem
        op.val = 1
        op.inc = 1
        return op

    def dma_fn(self, q, out, in_, fn):
        return self._op(q, fn, [out], [in_], dma=True)

    def ps(self, pool=None):
        if pool is None:
            t = self.psb[self.psi % 8]
            self.psi += 1
            return t
        key = tuple(pool)
        n = self.pcnt.get(key, 0)
        self.pcnt[key] = n + 1
        return self.psb[pool[n % len(pool)]]

    def dram(self, name, shape, dt, kind=None):
        if kind is None:
            t = self.nc.dram_tensor(name, list(shape), dt)
        else:
            t = self.nc.dram_tensor(name, list(shape), dt, kind=kind)
        return Tile(t.ap())

    def _op(self, eng, fn, outs, ins, dma=False):
        op = Op()
        op.eng = eng
        op.fn = fn
        op.marked = False
        op.dma = dma
        op.sem = None
        op.val = 0
        op.inc = 16
        deps = set()
        raw_hard = set()
        for v in ins:
            sig = apsig(v)
            for b in v.tile.chk(v.key):
                deps.update(b.w)
                for p in b.w:
                    if p.eng == eng and not p.dma and not (b.wsig is not None and b.wsig == sig and sig[0] >= 256):
                        raw_hard.add(p)
                if v.tile.is_psum:
                    deps.update(o for o in b.r if o.eng != eng)
        for v in outs:
            for b in v.tile.chk(v.key):
                deps.update(b.w)
                deps.update(b.r)
        deps.discard(op)
        keep = []
        for p in deps:
            if p.dma:
                keep.append(p)
            elif p.eng == eng:
                if eng == 'pe':
                    continue
                if dma or (SAME_ENG_SYNC and p in raw_hard):
                    p.marked = True
                    keep.append(p)
            else:
                p.marked = True
                keep.append(p)
        op.deps = keep
        for v in ins:
            b = v.tile.upd(v.key)
            if not dma:
                b.r = [o for o in b.r if o.dma or o.eng != eng]
            b.r.append(op)
        for v in outs:
            b = v.tile.upd(v.key)
            b.w = [op]
            b.r = []
            b.wsig = apsig(v)
        if dma:
            n = self.dcnt[eng]
            self.dcnt[eng] = n + 1
            op.sem = self.dsem[eng][n % 8]
            op.val = 16 * (n // 8 + 1)
            self.phase_dmas.append(op)
        self.ops[eng].append(op)
        return op

    def mm(self, out, lhsT, rhs, start=True, stop=True):
        self._op('pe', lambda e: e.matmul(out.ap, lhsT.ap, rhs.ap, start=start, stop=stop), [out], [lhsT, rhs])

    def tr(self, out, in_, ident):
        self._op('pe', lambda e: e.transpose(out.ap, in_.ap, ident.ap), [out], [in_, ident])

    def act(self, out, in_, func, bias=None, scale=None):
        ins = [in_]
        kw = {}
        if bias is not None:
            if isinstance(bias, V):
                ins.append(bias)
                kw['bias'] = bias.ap
            else:
                kw['bias'] = float(bias)
        if scale is not None:
            if isinstance(scale, V):
                ins.append(scale)
                kw['scale'] = scale.ap
            else:
                kw['scale'] = float(scale)
        self._op('act', lambda e: e.activation(out.ap, in_.ap, func, **kw), [out], ins)

    def copy(self, out, in_, eng='act'):
        if eng == 'act':
            self._op('act', lambda e: e.copy(out.ap, in_.ap), [out], [in_])
        else:
            self._op(eng, lambda e: e.tensor_copy(out.ap, in_.ap), [out], [in_])

    def tt(self, out, in0, in1, op, eng='dve'):
        self._op(eng, lambda e: e.tensor_tensor(out.ap, in0.ap, in1.ap, op), [out], [in0, in1])

    def ts(self, out, in0, s1, s2, op0, op1=None, eng='dve'):
        ins = [in0]
        a1 = s1
        a2 = s2
        if isinstance(s1, V):
            ins.append(s1)
            a1 = s1.ap
        if isinstance(s2, V):
            ins.append(s2)
            a2 = s2.ap
        if op1 is None:
            self._op(eng, lambda e: e.tensor_scalar(out.ap, in0.ap, a1, a2, op0), [out], ins)
        else:
            self._op(eng, lambda e: e.tensor_scalar(out.ap, in0.ap, a1, a2, op0, op1), [out], ins)

    def stt(self, out, in0, scalar, in1, op0, op1):
        ins = [in0, in1]
        a = scalar
        if isinstance(scalar, V):
            ins.append(scalar)
            a = scalar.ap
        self._op('dve', lambda e: e.scalar_tensor_tensor(out.ap, in0.ap, a, in1.ap, op0, op1), [out], ins)

    def scan(self, out, d0, d1, init, op0, op1):
        ins = [d0, d1]
        a = init
        if isinstance(init, V):
            ins.append(init)
            a = init.ap
        self._op('dve', lambda e: e.tensor_tensor_scan(out.ap, d0.ap, d1.ap, a, op0, op1), [out], ins)

    def red(self, out, in_, op):
        self._op('dve', lambda e: e.tensor_reduce(out.ap, in_.ap, AX.X, op), [out], [in_])

    def recip(self, out, in_):
        self._op('dve', lambda e: e.reciprocal(out.ap, in_.ap), [out], [in_])

    def bnstats(self, out, in_):
        self._op('dve', lambda e: e.bn_stats(out.ap, in_.ap), [out], [in_])

    def bnaggr(self, out, in_):
        self._op('dve', lambda e: e.bn_aggr(out.ap, in_.ap), [out], [in_])

    def memset(self, out, val, eng='dve'):
        self._op(eng, lambda e: e.memset(out.ap, val), [out], [])

    def dma(self, q, out, in_, is_out=False):
        op = self._op(q, lambda e: e.dma_start(out=out.ap, in_=in_.ap), [out], [in_], dma=True)
        if is_out:
            self.out_dmas.append(op)
        return op

    def run(self):
        fin = Op()
        fin.eng = 'sp'
        fin.fn = None
        fin.marked = False
        fin.dma = False
        fin.deps = list(self.out_dmas)
        fin.sem = None
        fin.val = 0
        self.ops['sp'].append(fin)
        for eng, lst in self.ops.items():
            c = 0
            for op in lst:
                if op.dma:
                    continue
                if op.marked:
                    c += 1
                    op.val = c
                    op.sem = self.esem[eng]
        with self.nc.Block() as block:
            for eng in BLK:
                if not self.ops[eng]:
                    continue

                def body(e, eng=eng):
                    seen = {}
                    for op in self.ops[eng]:
                        need = {}
                        for p in op.deps:
                            key = id(p.sem)
                            if seen.get(key, 0) >= p.val:
                                continue
                            if key not in need or need[key][1] < p.val:
                                need[key] = (p.sem, p.val)
                        for key, (sem_, val_) in need.items():
                            e.wait_ge(sem_, val_)
                            seen[key] = val_
                        if op.fn is None:
                            continue
                        ins = op.fn(e)
                        if op.dma:
                            ins.then_inc(op.sem, op.inc)
                        elif op.marked:
                            ins.then_inc(op.sem, 1)

                getattr(block, BLK[eng])(body)


def make_consts():
    j = np.arange(128)[:, None]
    l = np.arange(128)[None, :]
    same = (j // 64) == (l // 64)
    c = np.zeros((128, 8, 128), np.float32)
    c[:, 0] = (j == l)
    c[:, 1] = same & (j <= l)
    c[:, 2] = same
    c[:, 3] = (j < 64)
    c[:, 4] = (j >= 64)
    c[:, 5] = same & (j > l)
    c[:, 6] = (j <= l)
    c[:, 7] = same & (l >= j)
    return np.ascontiguousarray(c.reshape(128, 1024))


def softplus(P, out, x, tmp, neg=False):
    ax, y, w, q = tmp
    P.ts(ax, x, -1.0, None, ALU.mult)
    P.tt(ax, ax, x, ALU.max)
    P.act(y, ax, AF.Exp, scale=-1.0)
    P.ts(w, y, 2.0, None, ALU.add)
    P.recip(w, w)
    P.tt(w, w, y, ALU.mult)
    P.tt(y, w, w, ALU.mult)
    P.ts(q, y, 1.0 / 13.0, None, ALU.mult)
    for c in (1.0 / 11.0, 1.0 / 9.0, 1.0 / 7.0, 1.0 / 5.0, 1.0 / 3.0):
        P.stt(q, q, c, y, ALU.add, ALU.mult)
    P.stt(q, q, 1.0, w, ALU.add, ALU.mult)
    P.ts(ax, x, (-1.0 if neg else 1.0), 0.0, ALU.mult, ALU.max)
    P.stt(out, q, 2.0, ax, ALU.mult, ALU.add)

C_LRUX, C_LRUG, C_XS, C_B, C_C, C_Z, C_DT = 0, 256, 512, 1024, 1152, 1280, 1792
NWR = 1800
P_CW, P_CB, P_BA, P_BI, P_LAM, P_DTB, P_ALOG, P_DSK, P_NW = 0, 32, 40, 42, 44, 46, 54, 62, 70
NPR = 70 + 512


def emit_R(P, io, ntiles=SEQ // TT):
    wR_d, wai_d, prm_d, cst_d, yT_d = io['wR'], io['wai'], io['prm'], io['cst'], io['yT']
    prm = P.sb([128, NPR], F32)
    cst = P.sb([128, 8, 128], F32)
    identb = P.sb([128, 128], BF16)
    wR = P.sb([128, 8, NWR], BF16)
    wai = P.sb([128, 4, 128], BF16)
    P.dma('sp', prm[:, :], prm_d[:, :])
    P.dma('sp', cst[:, :, :], cst_d[:, :].rr("p (a b) -> p a b", a=8))
    P.dma('pool', wai[:, :, :], wai_d[:, :].rr("p (a b) -> p a b", a=4))
    wRv = wR_d[:, :].rr("(kc p) n -> p kc n", p=128)
    for kc in range(8):
        P.dma('pool', wR[:, kc, :], wRv[:, kc, :])
    P.copy(identb[:, :], cst[:, 0, :])
    TriBD, OnesBD, Half0, Half1, GtBD, mask2, maskCB = (cst[:, i, :] for i in range(1, 8))

    clru = P.sb([128, 2], F32)
    spw = P.sb([128, 8], F32)
    softplus(P, clru[:, :], prm[:, P_LAM:P_LAM + 2], [spw[:, 2 * i:2 * i + 2] for i in range(4)], neg=True)
    P.ts(clru[:, :], clru[:, :], -8.0, None, ALU.mult)
    Abc = P.sb([128, 8], F32)
    P.act(Abc[:, :], prm[:, P_ALOG:P_ALOG + 8], AF.Exp)
    P.ts(Abc[:, :], Abc[:, :], -1.0, None, ALU.mult)

    cbuf = [P.sb([128, 4 + TT], BF16) for _ in range(8)]
    for c in range(8):
        P.memset(cbuf[c][:, 0:3], 0.0)
    dg = P.sb([128, 32, 128], BF16)
    for c in range(8):
        for j in range(4):
            P.ts(dg[:, c * 4 + j, :], cst[:, 0, :], prm[:, P_CW + c * 4 + j:P_CW + c * 4 + j + 1], None, ALU.mult)
    hst = P.sb([128, 2], F32)
    P.memset(hst[:, :], 0.0)
    state = P.sb([128, 8, 64], F32)
    P.memset(state[:, :, :], 0.0)
    CTp = [P.sb([128, 4, 2, 128], BF16) for _ in range(2)]
    for t in CTp:
        P.memset(t[:, :, :, :], 0.0)

    xTs = [P.sb([128, 8, TT], BF16) for _ in range(2)]
    accs = [P.sb([128, TT], F32) for _ in range(2)]
    xc = [P.sb([128, TT], F32) for _ in range(2)]
    xcb = [P.sb([128, TT], BF16) for _ in range(2)]
    xsT = [P.sb([128, 4, TT], BF16) for _ in range(2)]
    BT = [P.sb([128, TT], BF16) for _ in range(2)]
    CT = [P.sb([128, TT], BF16) for _ in range(2)]
    lr = [P.sb([128, TT], F32) for _ in range(6)]
    ylT = [P.sb([128, 2, TT], BF16) for _ in range(2)]
    ysT = [P.sb([128, 4, TT], BF16) for _ in range(2)]
    zs = [P.sb([128, 512], F32) for _ in range(2)]
    sm = [P.sb([128, 96], F32) for _ in range(2)]
    Rm = [P.sb([128, 8, 128], F32) for _ in range(2)]
    dec = [P.sb([128, 8, 128], F32) for _ in range(2)]
    CBm = [P.sb([128, 128], F32) for _ in range(2)]
    MT = [P.sb([128, 8, 128], BF16) for _ in range(2)]
    xs_tok = [P.sb([128, 8, 64], BF16) for _ in range(2)]
    B_tok = [P.sb([128, 128], BF16) for _ in range(2)]
    xdt = [P.sb([128, 8, 64], BF16) for _ in range(2)]
    xdec = [P.sb([128, 8, 64], BF16) for _ in range(2)]
    prevb = [P.sb([128, 512], BF16) for _ in range(4)]
    yb = [P.sb([128, 8, 64], F32) for _ in range(2)]
    tmpx = [P.sb([128, 8, 64], F32) for _ in range(2)]
    y3 = [P.sb([128, 512], BF16) for _ in range(2)]
    dtall = [P.sb([128, 32], F32) for _ in range(2)]
    dtw = [P.sb([128, 32], F32) for _ in range(5)]

    ccol = [C_LRUX, C_LRUX + 128, C_XS, C_XS + 128, C_XS + 256, C_XS + 384, C_B, C_C]

    PX, PY = [6, 7], [[0, 1, 2], [3, 4, 5]]

    def gen_X(ti):
        xT = xTs[ti % 2]
        io['load_xT'](ti, xT)
        xsTt, BTt, CTt, CTpt = xsT[ti % 2], BT[ti % 2], CT[ti % 2], CTp[ti % 2]

        def fm_chunk(col):
            ps = P.ps(PX)
            for kc in range(8):
                P.mm(ps[:, :], wR[:, kc, col:col + 128], xT[:, kc, :], start=(kc == 0), stop=(kc == 7))
            return ps

        for c in range(8):
            ps = fm_chunk(ccol[c])
            cb = cbuf[c]
            P.copy(cb[:, 3:3 + TT], ps[:, :])
            yield
            acc = P.ps(PX)
            for j in range(4):
                P.mm(acc[:, :], dg[:, c * 4 + j, :], cb[:, j:j + TT], start=(j == 0), stop=(j == 3))
            P.copy(cb[:, 0:3], cb[:, TT:TT + 3], eng='dve')
            yield
            bias = prm[:, P_CB + c:P_CB + c + 1]
            if c < 2:
                P.act(xc[c][:, :], acc[:, :], AF.Identity, bias=bias)
                P.act(xcb[c][:, :], acc[:, :], AF.Identity, bias=bias)
            elif c < 6:
                P.act(xsTt[:, c - 2, :], acc[:, :], AF.Silu, bias=bias)
            elif c == 6:
                P.act(BTt[:, :], acc[:, :], AF.Silu, bias=bias)
            else:
                P.act(CTt[:, :], acc[:, :], AF.Silu, bias=bias)
                a4 = acc[:, :].rr("p (s h t) -> p s h t", s=4, h=2)
                for h in range(2):
                    P.act(CTpt[:, :, h, h * 64:(h + 1) * 64], a4[:, :, h, :], AF.Silu, bias=bias)
            yield
        psd = P.ps(PX)
        for sub in range(4):
            for kc in range(8):
                P.mm(psd[:, sub * 8:(sub + 1) * 8], xT[:, kc, sub * 128:(sub + 1) * 128], wR[:, kc, C_DT:C_DT + 8], start=(kc == 0), stop=(kc == 7))
        dtx = dtw[0][:, :]
        P.tt(dtx.rr("p (s k) -> p s k", s=4), psd[:, 0:32].rr("p (s k) -> p s k", s=4), prm[:, P_DTB:P_DTB + 8].unsq(1).bc([128, 4, 8]), ALU.add)
        yield
        softplus(P, dtall[ti % 2][:, :], dtx, [t[:, :] for t in dtw[1:5]])
        yield
        ylTt = ylT[ti % 2]
        for n in range(2):
            r_, i_, a_, s_, u_, h_ = lr
            ps = P.ps(PX)
            P.mm(ps[:, :], wai[:, n, :], xcb[n][:, :])
            P.act(r_[:, :], ps[:, :], AF.Sigmoid, bias=prm[:, P_BA + n:P_BA + n + 1])
            ps = P.ps(PX)
            P.mm(ps[:, :], wai[:, 2 + n, :], xcb[n][:, :])
            P.act(i_[:, :], ps[:, :], AF.Sigmoid, bias=prm[:, P_BI + n:P_BI + n + 1])
            yield
            P.act(a_[:, :], r_[:, :], AF.Exp, scale=clru[:, n:n + 1])
            P.tt(s_[:, :], a_[:, :], a_[:, :], ALU.mult)
            P.act(s_[:, :], s_[:, :], AF.Sqrt, bias=1.0, scale=-1.0)
            yield
            P.tt(u_[:, :], i_[:, :], xc[n][:, :], ALU.mult)
            P.tt(u_[:, :], u_[:, :], s_[:, :], ALU.mult)
            yield
            P.scan(h_[:, :], a_[:, :], u_[:, :], hst[:, n:n + 1], ALU.mult, ALU.add)
            P.copy(hst[:, n:n + 1], h_[:, TT - 1:TT], eng='dve')
            yield
            psg = fm_chunk(C_LRUG + n * 128)
            P.act(r_[:, :], psg[:, :], AF.Square)
            P.ts(r_[:, :], r_[:, :], 0.044715, 1.0, ALU.mult, ALU.add)
            yield
            P.tt(r_[:, :], r_[:, :], psg[:, :], ALU.mult)
            P.act(r_[:, :], r_[:, :], AF.Sigmoid, scale=1.5957691216057308)
            yield
            P.tt(r_[:, :], r_[:, :], psg[:, :], ALU.mult)
            P.tt(ylTt[:, n, :], r_[:, :], h_[:, :], ALU.mult)
            yield
        yt2 = yT_d.k(ti)[ti, :].rr("(f t) -> f t", t=TT)
        P.dma('sp', yt2[0:256, :].rr("(c p) t -> p c t", p=128), ylTt[:, :, :])

    def gen_Y(ti, sub):
        xT = xTs[ti % 2]
        xsTt, BTt, CTt, CTpt = xsT[ti % 2], BT[ti % 2], CT[ti % 2], CTp[ti % 2]
        ysTt = ysT[ti % 2]
        q = sub % 2
        pool = PY[q]
        tok = slice(sub * 128, (sub + 1) * 128)
        smq = sm[q]
        da = smq[:, 24:32]
        cs_sb, expcs, dend, cdec = smq[:, 32:40], smq[:, 40:48], smq[:, 48:56], smq[:, 56:72]
        tmp8, ss, rstd = smq[:, 72:80], smq[:, 80:81], smq[:, 81:82]
        psz = P.ps(pool)
        for kc in range(8):
            P.mm(psz[:, :], xT[:, kc, tok], wR[:, kc, C_Z:C_Z + 512], start=(kc == 0), stop=(kc == 7))
        P.act(zs[q][:, :], psz[:, :], AF.Silu)
        dt = dtall[ti % 2][:, sub * 8:(sub + 1) * 8]
        P.tt(da, dt, Abc[:, :], ALU.mult)
        yield
        pst = P.ps(pool)
        pstb = pst[:, :].bitcast(BF16)
        for c in range(4):
            P.tr(pstb[:, c * 128:(c + 1) * 128], xsTt[:, c, tok], identb[:, :])
        P.copy(xs_tok[q][:, :, :], pstb[:, 0:512].rr("p (k d) -> p k d", k=8))
        psb_ = P.ps(pool)
        psbb = psb_[:, :].bitcast(BF16)
        P.tr(psbb[:, 0:128], BTt[:, tok], identb[:, :])
        P.copy(B_tok[q][:, :], psbb[:, 0:128])
        yield
        psc = P.ps(pool)
        P.mm(psc[:, 0:8], TriBD, da)
        P.mm(psc[:, 8:16], OnesBD, da)
        P.mm(psc[:, 16:24], Half0, da)
        P.mm(psc[:, 24:32], Half1, da)
        P.copy(cs_sb, psc[:, 0:8])
        P.act(expcs, psc[:, 0:8], AF.Exp)
        yield
        P.tt(tmp8, psc[:, 8:16], cs_sb, ALU.subtract)
        P.act(dend, tmp8, AF.Exp)
        P.act(cdec, psc[:, 16:32], AF.Exp)
        yield
        P.tt(Rm[q][:, :, :], da.unsq(2).bc([128, 8, 128]), mask2.unsq(1).bc([128, 8, 128]), ALU.mult)
        yield
        Rf = Rm[q][:, :, :].rr("p k l -> p (k l)")
        psA = P.ps(pool)
        P.mm(psA[:, :], GtBD, Rf[:, 0:512])
        psB = P.ps(pool)
        P.mm(psB[:, :], GtBD, Rf[:, 512:1024])
        df = dec[q][:, :, :].rr("p k l -> p (k l)")
        P.act(df[:, 0:512], psA[:, :], AF.Exp)
        P.act(df[:, 512:1024], psB[:, :], AF.Exp)
        yield
        psC = P.ps(pool)
        P.mm(psC[:, 0:128], BTt[:, tok], CTt[:, tok])
        P.tt(CBm[q][:, :], psC[:, 0:128], maskCB, ALU.mult)
        yield
        P.tt(MT[q][:, :, :], dec[q][:, :, :], CBm[q][:, :].unsq(1).bc([128, 8, 128]), ALU.mult)
        yield
        P.tt(xdt[q][:, :, :], xs_tok[q][:, :, :], dt.unsq(2).bc([128, 8, 64]), ALU.mult)
        P.tt(xdec[q][:, :, :], xdt[q][:, :, :], dend.unsq(2).bc([128, 8, 64]), ALU.mult)
        yield
        psO = P.ps(pool)
        xdf = xdec[q][:, :, :].rr("p k d -> p (k d)")
        sf = state[:, :, :].rr("p k d -> p (k d)")
        for h in range(2):
            psS = P.ps(pool)
            P.mm(psS[:, :], B_tok[q][64 * h:64 * h + 64, :], xdf[64 * h:64 * h + 64, :])
            pv = prevb[(2 * q + h)]
            P.copy(pv[:, :], sf)
            P.tt(state[:, :, :], state[:, :, :], cdec[:, 8 * h:8 * h + 8].unsq(2).bc([128, 8, 64]), ALU.mult)
            P.tt(sf, sf, psS[:, :], ALU.add)
            P.mm(psO[:, :], CTpt[:, sub, h, :], pv[:, :], start=(h == 0), stop=(h == 1))
            yield
        y = yb[q]
        yf = y[:, :, :].rr("p k d -> p (k d)")
        P.tt(y[:, :, :], psO[:, :].rr("p (k d) -> p k d", k=8), expcs.unsq(2).bc([128, 8, 64]), ALU.mult)
        yield
        psY = P.ps(pool)
        for k in range(8):
            P.mm(psY[:, k * 64:(k + 1) * 64], MT[q][:, k, :], xdt[q][:, k, :])
        P.tt(yf, yf, psY[:, :], ALU.add)
        yield
        P.tt(tmpx[q][:, :, :], xs_tok[q][:, :, :], prm[:, P_DSK:P_DSK + 8].unsq(2).bc([128, 8, 64]), ALU.mult)
        P.tt(y[:, :, :], y[:, :, :], tmpx[q][:, :, :], ALU.add)
        yield
        P.tt(yf, yf, zs[q][:, :], ALU.mult)
        tf = tmpx[q][:, :, :].rr("p k d -> p (k d)")
        P.tt(tf, yf, yf, ALU.mult)
        yield
        P.red(ss, tf, ALU.add)
        P.ts(ss, ss, 1.0 / 512.0, EPS, ALU.mult, ALU.add)
        P.act(ss, ss, AF.Sqrt)
        P.recip(rstd, ss)
        yield
        P.stt(y3[q][:, :], yf, rstd, prm[:, P_NW:P_NW + 512], ALU.mult, ALU.mult)
        psT = P.ps(pool)
        psTb = psT[:, :].bitcast(BF16)
        for c in range(4):
            P.tr(psTb[:, c * 128:(c + 1) * 128], y3[q][:, c * 128:(c + 1) * 128], identb[:, :])
        P.copy(ysTt[:, :, tok], psTb[:, 0:512].rr("p (c t) -> p c t", c=4))

    def run_tile(ygens, bg):
        SKEW = 8
        active = []
        pending = list(ygens)
        bg_done = bg is None
        while pending or active:
            if pending and len(active) < 2 and (not active or active[-1][1] >= SKEW):
                active.append([pending.pop(0), 0])
            for a in list(active):
                try:
                    next(a[0])
                    a[1] += 1
                except StopIteration:
                    active.remove(a)
            if not bg_done:
                try:
                    next(bg)
                except StopIteration:
                    bg_done = True
        while not bg_done:
            try:
                next(bg)
            except StopIteration:
                bg_done = True

    run_tile([], gen_X(0))
    for ti in range(ntiles):
        bg = gen_X(ti + 1) if ti + 1 < ntiles else None
        run_tile([gen_Y(ti, sub) for sub in range(4)], bg)
        yt2 = yT_d.k(ti)[ti, :].rr("(f t) -> f t", t=TT)
        P.dma('sp', yt2[256:768, :].rr("(c p) t -> p c t", p=128), ysT[ti % 2][:, :, :])
        io['tile_done'](ti)


NTOK = 2048
P_BG = 0
NPT = 24


class WStream:
    def __init__(self, P, nslots, cache=None):
        self.P = P
        self.slots = [P.sb([128, 8, 512], BF16) for _ in range(nslots)]
        self.i = 0
        self.cache = cache
        self.ids = {}

    def load(self, wd, r0, nrows, c0, ncols=512, cacheable=True):
        P = self.P
        slot = self.slots[self.i % len(self.slots)]
        self.i += 1
        nkc = nrows // 128
        sv = slot[:, 0:nkc, 0:ncols]
        key = (id(wd), r0, c0)
        if self.cache is not None and cacheable and key in self.ids:
            pid = self.ids[key]
            cv = self.cache.k(pid)[pid, 0:128 * nkc * ncols].rr("(p k n) -> p k n", p=128, k=nkc)
            P.dma('sp', sv, cv)
            return slot
        src = wd[r0:r0 + nrows, c0:c0 + ncols].rr("(kc p) n -> p kc n", p=128)
        P.dma('pool', sv, src)
        if self.cache is not None and cacheable:
            pid = len(self.ids)
            self.ids[key] = pid
            cv = self.cache.k(pid)[pid, 0:128 * nkc * ncols].rr("(p k n) -> p k n", p=128, k=nkc)
            P.dma('sp', cv, sv)
        return slot


def emit_T(P, io, ntiles=NTOK // TT):
    ws = WStream(P, 4, io.get('wcache'))
    prmT = P.sb([128, NPT], F32)
    lnp = P.sb([128, 4, D], F32)
    identf = P.sb([128, 128], F32)
    identb = P.sb([128, 128], BF16)
    P.dma('sp', prmT[:, :], io['prmT'][:, :])
    P.dma('sp', lnp[:, :, :], io['lnp'][:, :].rr("p (a b) -> p a b", a=4))
    P.dma('sp', identf[:, :], io['cst'][:, 0:128])
    P.copy(identb[:, :], identf[:, :])

    memT = P.sb([128, 8, NMEM], BF16)
    P.dma('pool', memT[:, :, :], io['memT'][:, :].rr("(kc p) m -> p kc m", p=128))
    KT = P.sb([128, 8, NMEM], BF16)
    Vt = P.sb([128, 2, 1024], BF16)
    for ph in range(2):
        pan = ws.load(io['wkv'], 0, 1024, ph * 512, cacheable=False)
        for c in range(4):
            ps = P.ps()
            for kc in range(8):
                P.mm(ps[:, 0:NMEM], pan[:, kc, c * 128:(c + 1) * 128], memT[:, kc, :], start=(kc == 0), stop=(kc == 7))
            P.copy(KT.k(ph * 4 + c)[:, ph * 4 + c, :], ps[:, 0:NMEM])
    for ph in range(2):
        pan = ws.load(io['wkv'], 0, 1024, 1024 + ph * 512, cacheable=False)
        for mc in range(2):
            ps = P.ps()
            for kc in range(8):
                P.mm(ps[:, :], memT[:, kc, mc * 128:(mc + 1) * 128], pan[:, kc, :], start=(kc == 0), stop=(kc == 7))
            P.copy(Vt.k((mc, ph))[:, mc, ph * 512:(ph + 1) * 512], ps[:, :])

    xT = P.sb([128, 8, TT], BF16)
    xtok = P.sb([128, 4, D], F32)
    x1 = P.sb([128, 4, D], F32)
    qm = P.sb([128, 8, TT], BF16)
    x1T = P.sb([128, 8, TT], BF16)
    PT = [P.sb([128, 2, TT], BF16) for _ in range(2)]
    yxaT = P.sb([128, 8, TT], BF16)
    ybuf = P.sb([128, 16, TT], BF16)
    acc = P.sb([128, 8, TT], F32)
    hT = P.sb([128, 22, TT], BF16)
    gs = [P.sb([128, TT], F32) for _ in range(2)]
    es_ = [P.sb([128, NMEM], F32) for _ in range(2)]
    pn = [P.sb([128, NMEM], BF16) for _ in range(2)]
    sm = [P.sb([128, 8], F32) for _ in range(2)]
    st = [P.sb([128, 12], F32) for _ in range(2)]
    mv = [P.sb([128, 4], F32) for _ in range(2)]

    lncnt = [0]

    def layer_norm(v, g, b):
        i = lncnt[0] % 2
        lncnt[0] += 1
        P.bnstats(st[i][:, 0:6], v[:, 0:512])
        P.bnstats(st[i][:, 6:12], v[:, 512:1024])
        P.bnaggr(mv[i][:, 0:2], st[i][:, :])
        P.ts(mv[i][:, 2:3], mv[i][:, 1:2], EPS, None, ALU.add)
        P.act(mv[i][:, 2:3], mv[i][:, 2:3], AF.Sqrt)
        P.recip(mv[i][:, 3:4], mv[i][:, 2:3])
        P.ts(v, v, mv[i][:, 0:1], mv[i][:, 3:4], ALU.subtract, ALU.mult)
        P.tt(v, v, g, ALU.mult)
        P.tt(v, v, b, ALU.add)

    for ti in range(ntiles):
        t0 = ti * TT
        io['load_xT'](ti, xT)
        io['load_xtok'](ti, xtok)
        for ph in range(2):
            pan = ws.load(io['wq'], 0, 1024, ph * 512)
            for c in range(4):
                dc = ph * 4 + c
                ps = P.ps()
                for kc in range(8):
                    P.mm(ps[:, :], pan[:, kc, c * 128:(c + 1) * 128], xT[:, kc, :], start=(kc == 0), stop=(kc == 7))
                P.copy(qm.k(dc)[:, dc, :], ps[:, :])
        def gen_A(pool):
            cnt = 0
            for h in range(4):
                PTh = PT[h % 2]
                for s in range(4):
                    i = cnt % 2
                    cnt += 1
                    tok = slice(s * 128, (s + 1) * 128)
                    ps = P.ps(pool)
                    for dd in range(2):
                        P.mm(ps[:, 0:NMEM], qm.k(2 * h + dd)[:, 2 * h + dd, tok], KT.k(2 * h + dd)[:, 2 * h + dd, :], start=(dd == 0), stop=(dd == 1))
                    mx, nb, sm_, rs = sm[i][:, 0:1], sm[i][:, 1:2], sm[i][:, 2:3], sm[i][:, 3:4]
                    P.red(mx, ps[:, 0:NMEM], ALU.max)
                    P.ts(nb, mx, -1.0 / 16.0, None, ALU.mult)
                    yield
                    P.act(es_[i][:, :], ps[:, 0:NMEM], AF.Exp, bias=nb, scale=1.0 / 16.0)
                    P.red(sm_, es_[i][:, :], ALU.add)
                    P.recip(rs, sm_)
                    yield
                    P.ts(pn[i][:, :], es_[i][:, :], rs, None, ALU.mult)
                    pst = P.ps(pool)
                    pstb = pst[:, :].bitcast(BF16)
                    for mc in range(2):
                        P.tr(pstb[:, mc * 128:(mc + 1) * 128], pn[i][:, mc * 128:(mc + 1) * 128], identb[:, :])
                    P.copy(PTh[:, :, tok], pstb[:, 0:256].rr("p (m t) -> p m t", m=2))
                    yield
                for dd in range(2):
                    ps = P.ps(pool)
                    for mc in range(2):
                        P.mm(ps[:, :], Vt[:, mc, h * 256 + dd * 128:h * 256 + (dd + 1) * 128], PTh[:, mc, :], start=(mc == 0), stop=(mc == 1))
                    P.copy(yxaT.k(2 * h + dd)[:, 2 * h + dd, :], ps[:, :])
                yield

        def gen_B(bs, pool):
            for b in bs:
                wname, nkp = [('wbl', 1), ('wbs', 2), ('wbx', 1)][b]
                if b < 2:
                    io['load_y'](ti, b, ybuf)
                ysb = ybuf if b < 2 else yxaT
                for ph in range(2):
                    pans = [ws.load(io[wname], kp * 1024, 1024, ph * 512) for kp in range(nkp)]
                    gp = ws.load(io['wg'], 0, 1024, b * 1024 + ph * 512)
                    for c in range(4):
                        dc = ph * 4 + c
                        psp = P.ps(pool)
                        n = nkp * 8
                        i = 0
                        for kp in range(nkp):
                            for kc in range(8):
                                P.mm(psp[:, :], pans[kp][:, kc, c * 128:(c + 1) * 128], ysb[:, kp * 8 + kc, :], start=(i == 0), stop=(i == n - 1))
                                i += 1
                            yield
                        psg = P.ps(pool)
                        for kc in range(8):
                            P.mm(psg[:, :], gp[:, kc, c * 128:(c + 1) * 128], xT[:, kc, :], start=(kc == 0), stop=(kc == 7))
                        g = gs[dc % 2]
                        P.act(g[:, :], psg[:, :], AF.Sigmoid, bias=prmT[:, P_BG + b * 8 + dc:P_BG + b * 8 + dc + 1])
                        yield
                        if b == 0:
                            P.tt(acc.k(dc)[:, dc, :], g[:, :], psp[:, :], ALU.mult)
                        else:
                            P.tt(g[:, :], g[:, :], psp[:, :], ALU.mult)
                            if b == 1:
                                P.tt(acc.k(dc)[:, dc, :], acc.k(dc)[:, dc, :], g[:, :], ALU.add)
                            else:
                                P.tt(qm.k(dc)[:, dc, :], acc.k(dc)[:, dc, :], g[:, :], ALU.add)
                        yield

        gens = [gen_A([0, 1, 2]), gen_B([0, 1], [3, 4, 5, 6, 7])]
        while gens:
            for g_ in list(gens):
                try:
                    next(g_)
                except StopIteration:
                    gens.remove(g_)
        for _ in gen_B([2], None):
            pass
        for ph in range(2):
            pan = ws.load(io['wo'], 0, 1024, ph * 512)
            for s in range(4):
                tok = slice(s * 128, (s + 1) * 128)
                ps = P.ps()
                for kc in range(8):
                    P.mm(ps[:, :], qm[:, kc, tok], pan[:, kc, :], start=(kc == 0), stop=(kc == 7))
                P.stt(x1.k(s)[:, s, ph * 512:(ph + 1) * 512], xtok.k(s)[:, s, ph * 512:(ph + 1) * 512], ALPHA, ps[:, :], ALU.mult, ALU.add)
        for s in range(4):
            tok = slice(s * 128, (s + 1) * 128)
            layer_norm(x1.k(s)[:, s, :], lnp[:, 0, :], lnp[:, 1, :])
            for g4 in range(2):
                ps = P.ps()
                for j in range(4):
                    dc = g4 * 4 + j
                    P.tr(ps[:, j * 128:(j + 1) * 128], x1.k(s)[:, s, dc * 128:(dc + 1) * 128], identf[:, :])
                P.copy(x1T[:, g4 * 4:(g4 + 1) * 4, tok], ps[:, :].rr("p (j t) -> p j t", j=4))
        for p in range(11):
            pan = ws.load(io['wfi'], 0, 1024, p * 512)
            for c in range(4):
                col = p * 512 + c * 128
                ps = P.ps()
                for kc in range(8):
                    P.mm(ps[:, :], pan[:, kc, c * 128:(c + 1) * 128], x1T[:, kc, :], start=(kc == 0), stop=(kc == 7))
                if col < DFF:
                    ffc = col // 128
                    P.act(hT.k(ffc)[:, ffc, :], ps[:, :], AF.Silu)
                else:
                    ffc = (col - DFF) // 128
                    P.tt(hT.k(ffc)[:, ffc, :], hT.k(ffc)[:, ffc, :], ps[:, :], ALU.mult)
        for ph in range(2):
            pss = [P.ps() for _ in range(4)]
            for (r0, nr) in [(0, 1024), (1024, 1024), (2048, 768)]:
                pan = ws.load(io['wfd'], r0, nr, ph * 512)
                for s in range(4):
                    tok = slice(s * 128, (s + 1) * 128)
                    for kc in range(nr // 128):
                        ffc = r0 // 128 + kc
                        P.mm(pss[s][:, :], hT.k(ffc)[:, ffc, tok], pan[:, kc, :], start=(ffc == 0), stop=(ffc == 21))
            for s in range(4):
                P.stt(xtok.k(s)[:, s, ph * 512:(ph + 1) * 512], x1.k(s)[:, s, ph * 512:(ph + 1) * 512], ALPHA, pss[s][:, :], ALU.mult, ALU.add)
        for s in range(4):
            layer_norm(xtok.k(s)[:, s, :], lnp[:, 2, :], lnp[:, 3, :])
        io['store_out'](ti, xtok)
        if io.get('store_xT') is not None:
            for s in range(4):
                tok = slice(s * 128, (s + 1) * 128)
                for g4 in range(2):
                    ps = P.ps()
                    for j in range(4):
                        dc = g4 * 4 + j
                        P.tr(ps[:, j * 128:(j + 1) * 128], xtok.k(s)[:, s, dc * 128:(dc + 1) * 128], identf[:, :])
                    P.copy(qm[:, g4 * 4:(g4 + 1) * 4, tok], ps[:, :].rr("p (j t) -> p j t", j=4))
            io['store_xT'](ti, qm)


R_W = {'wR': [D, NWR], 'wai': [128, 512], 'prm': [128, NPR]}
T_W = {'wq': [D, 1024], 'wg': [D, 3072], 'wkv': [D, 2048], 'wbl': [1024, D], 'wbs': [2048, D], 'wbx': [1024, D],
       'wo': [D, D], 'wfi': [D, 2 * DFF], 'wfd': [DFF, D], 'prmT': [128, NPT], 'lnp': [128, 4 * D]}


def build_fused(nc, es, stop=None):
    P = Prog(nc, es)

    def ext(n, sh):
        return P.dram(n, sh, F32, "ExternalInput")

    xT0 = ext("xT0", [D, SEQ])
    xTt = ext("xTt", [D, NTOK])
    xtok_d = ext("xtok", [NTOK, D])
    memT = ext("memT", [D, NMEM])
    cst = ext("cst", [128, 1024])
    out_d = P.dram("out", [NTOK, D], F32, "ExternalOutput")
    WR = [{k: ext(f"{k}{l}", sh) for k, sh in R_W.items()} for l in range(2)]
    WT = [{k: ext(f"{k}{l}", sh) for k, sh in T_W.items()} for l in range(2)]
    YT = 768 * TT
    y_loc = [P.dram(f"y_loc{l}", [16, YT], BF16) for l in range(2)]
    y_all = [P.dram(f"y_all{l}", [16, 4 * YT], BF16) for l in range(2)]
    y_mine = [P.dram(f"y_mine{l}", [4, 4 * YT], BF16, "ExternalOutput" if stop == f"R{l}" else None) for l in range(2)]
    x1_d = P.dram("x1_d", [NTOK, D], F32, "ExternalOutput" if stop in ("T0", "DBG") else None)
    wcache = [P.dram(f"wcache{l}", [36, 128 * 8 * 512], BF16) for l in range(2)]
    XT = 512 * TT
    x1T_loc = P.dram("x1T_loc", [8, XT], BF16)
    x1T_all = P.dram("x1T_all", [8, 4 * XT], BF16)

    def tokview(t):
        return t[:, :].rr("(s p) d -> p s d", p=128)

    for l in range(2):
        if l == 0:
            xv = xT0[:, :].rr("(kc p) t -> p kc t", p=128)

            def load_xT_R(ti, xT, xv=xv):
                P.dma('pool', xT[:, :, :], xv[:, :, ti * TT:(ti + 1) * TT])
        else:
            def load_xT_R(ti, xT):
                q, i = ti // 4, ti % 4
                for h in range(2):
                    src = x1T_all[i * 2 + h, q * XT:(q + 1) * XT].rr("(c p t) -> p c t", p=128, t=TT)
                    P.dma('pool', xT[:, 4 * h:4 * h + 4, :], src)
        yl, yal = y_loc[l], y_all[l]

        pend = []

        def tile_done(ti, yl=yl, yal=yal, pend=pend):
            while pend:
                pend.pop(0)()
            pend.append(lambda: P.allgather(yal.k(ti)[ti, :], yl.k(ti)[ti, :]))
        ioR = dict(WR[l])
        ioR.update(cst=cst, yT=y_loc[l], load_xT=load_xT_R, tile_done=tile_done)
        emit_R(P, ioR)
        while pend:
            pend.pop(0)()
        P.barrier()

        ya = y_all[l]

        ym = y_mine[l]

        def fn(e, ym=ym, ya=ya):
            q = e.partition_id() % 4
            yv = ya.base.rearrange("(q i) (r c) -> q (i r) c", q=4, c=16384)
            src = yv[bass.ds(q, 1), :, :].rearrange("o r c -> (o r) c")
            dst = ym.base.rearrange("i (r c) -> (i r) c", c=16384)
            return e.dma_start(out=dst, in_=src)
        op_ = P.dma_fn('pool', ym[:, :], ya[:, :], fn)
        if stop == f"R{l}":
            P.out_dmas.append(op_)
            break

        def load_y(ti, b, ybuf, ym=ym):
            r0, r1, ncc = (0, 256, 2) if b == 0 else (256, 768, 4)
            for j in range(4):
                src = ym[ti, j * YT:(j + 1) * YT].rr("(f t) -> f t", t=TT)[r0:r1, :].rr("(c p) t -> p c t", p=128)
                P.dma('sp', ybuf[:, j * ncc:(j + 1) * ncc, :], src)

        ioT = dict(WT[l])
        ioT.update(cst=cst, memT=memT, load_y=load_y, wcache=wcache[l])
        if l == 0:
            xtv = xTt[:, :].rr("(kc p) t -> p kc t", p=128)
            ioT['load_xT'] = lambda ti, xT, xtv=xtv: P.dma('pool', xT[:, :, :], xtv[:, :, ti * TT:(ti + 1) * TT])
            ioT['load_xtok'] = lambda ti, xt: P.dma('sp', xt[:, :, :], tokview(xtok_d)[:, ti * 4:(ti + 1) * 4, :])
            ioT['store_out'] = lambda ti, xt: P.dma('sp', tokview(x1_d)[:, ti * 4:(ti + 1) * 4, :], xt[:, :, :])
            pend2 = []

            def store_xT(ti, qm):
                while pend2:
                    pend2.pop(0)()
                for h in range(2):
                    n = ti * 2 + h
                    dst = x1T_loc.k(n)[n, :].rr("(c p t) -> p c t", p=128, t=TT)
                    P.dma('sp', dst, qm[:, 4 * h:4 * h + 4, :])
                    pend2.append(lambda n=n: P.allgather(x1T_all.k(n)[n, :], x1T_loc.k(n)[n, :]))
            ioT['store_xT'] = store_xT
        else:
            def load_xT_T(ti, xT):
                for h in range(2):
                    src = x1T_loc[ti * 2 + h, :].rr("(c p t) -> p c t", p=128, t=TT)
                    P.dma('pool', xT[:, 4 * h:4 * h + 4, :], src)
            ioT['load_xT'] = load_xT_T
            ioT['load_xtok'] = lambda ti, xt: P.dma('sp', xt[:, :, :], tokview(x1_d)[:, ti * 4:(ti + 1) * 4, :])
            ioT['store_out'] = lambda ti, xt: P.dma('sp', tokview(out_d)[:, ti * 4:(ti + 1) * 4, :], xt[:, :, :], is_out=True)
            ioT['store_xT'] = None
        emit_T(P, ioT)
        if l == 1 and stop == "DBG":
            dbg = P.dram("dbg_x1T", [8, XT], BF16, "ExternalOutput")
            P.dma('sp', dbg[:, :], x1T_loc[:, :], is_out=True)
        if l == 0:
            while pend2:
                pend2.pop(0)()
            P.barrier()
            if stop == "T0":
                break
    P.run()


OFF = {'lrux': 0, 'lrug': 1024, 'z': 2048, 'xbc': 4096, 'dt': 7168, 'q': 7200, 'g': 8224}
_CST = make_consts()
_prog = []


def get_prog():
    if not _prog:
        nc = bass.Bass("TRN2", target_bir_lowering=False)
        es = ExitStack()
        build_fused(nc, es)
        _prog.append((nc, es))
    return _prog[0][0]


def rep(v, n=128):
    v = np.asarray(v, np.float32)
    return np.ascontiguousarray(np.broadcast_to(v.reshape(1, -1), (n, v.size)))


def pcol(v):
    v = np.asarray(v, np.float32)
    return np.ascontiguousarray(v.reshape(-1, 128).T)


def r_inputs(inp, l, j):
    w_in = inp['w_in'][l]
    lr = slice(256 * j, 256 * (j + 1))
    gr = slice(512 * j, 512 * (j + 1))
    hr = slice(8 * j, 8 * (j + 1))
    xbc0 = OFF['xbc']
    bsl = slice(2048 + 128 * j, 2048 + 128 * (j + 1))
    csl = slice(2560 + 128 * j, 2560 + 128 * (j + 1))
    wR = np.concatenate([
        w_in[:, OFF['lrux'] + 256 * j:OFF['lrux'] + 256 * (j + 1)],
        w_in[:, OFF['lrug'] + 256 * j:OFF['lrug'] + 256 * (j + 1)],
        w_in[:, xbc0 + 512 * j:xbc0 + 512 * (j + 1)],
        w_in[:, xbc0 + bsl.start:xbc0 + bsl.stop],
        w_in[:, xbc0 + csl.start:xbc0 + csl.stop],
        w_in[:, OFF['z'] + 512 * j:OFF['z'] + 512 * (j + 1)],
        w_in[:, OFF['dt'] + 8 * j:OFF['dt'] + 8 * (j + 1)],
    ], axis=1)
    lcw = inp['lru_conv_w'][l][:, lr]
    scw = inp['ssd_conv_w'][l]
    scb = inp['ssd_conv_b'][l]
    cw_cols = np.concatenate([lcw, scw[:, gr], scw[:, bsl], scw[:, csl]], axis=1)
    cb_cols = np.concatenate([inp['lru_conv_b'][l][lr], scb[gr], scb[bsl], scb[csl]])
    cw = cw_cols.reshape(4, 8, 128).transpose(2, 1, 0).reshape(128, 32)
    prm = np.concatenate([
        cw, pcol(cb_cols), pcol(inp['lru_b_a'][l][lr]), pcol(inp['lru_b_i'][l][lr]), pcol(inp['lru_lambda'][l][lr]),
        rep(inp['ssd_dt_bias'][l][hr]), rep(inp['ssd_a_log'][l][hr]), rep(inp['ssd_d'][l][hr]),
        rep(inp['ssd_norm_w'][l][gr]),
    ], axis=1).astype(np.float32)
    wa = inp['lru_w_a'][l][2 * j:2 * j + 2]
    wi = inp['lru_w_i'][l][2 * j:2 * j + 2]
    wai = np.concatenate([wa, wi], axis=0).transpose(1, 0, 2).reshape(128, 512)
    return {f"wR{l}": np.ascontiguousarray(wR, dtype=np.float32), f"wai{l}": np.ascontiguousarray(wai, dtype=np.float32),
            f"prm{l}": np.ascontiguousarray(prm)}


def t_inputs(inp, l):
    w_in = inp['w_in'][l]
    bg = inp['b_gate'][l]
    prmT = bg.reshape(3, 8, 128).transpose(2, 0, 1).reshape(128, 24)
    lnp = np.concatenate([rep(inp['ln1_g'][l]), rep(inp['ln1_b'][l]), rep(inp['ln2_g'][l]), rep(inp['ln2_b'][l])], axis=1)
    c = np.ascontiguousarray
    return {
        f"wq{l}": c(w_in[:, OFF['q']:OFF['q'] + 1024]), f"wg{l}": c(w_in[:, OFF['g']:OFF['g'] + 3072]),
        f"wkv{l}": c(inp['mem_w_kv'][l]), f"wbl{l}": c(inp['w_br_lru'][l]), f"wbs{l}": c(inp['w_br_ssd'][l]),
        f"wbx{l}": c(inp['w_br_xa'][l]), f"wo{l}": c(inp['w_out'][l]), f"wfi{l}": c(inp['ffn_w_in'][l]),
        f"wfd{l}": c(inp['ffn_w_down'][l]), f"prmT{l}": c(prmT.astype(np.float32)), f"lnp{l}": c(lnp),
    }


def kernel(**inputs):
    inp = {k: np.asarray(v, dtype=np.float32) for k, v in inputs.items()}
    x = inp['x']
    nc = get_prog()
    xTs = [np.ascontiguousarray(x[b].T) for b in range(2)]
    memTs = [np.ascontiguousarray(inp['mem'][b].T) for b in range(2)]
    tw = {}
    for l in range(2):
        tw.update(t_inputs(inp, l))
    rw = [{} for _ in range(4)]
    for j in range(4):
        for l in range(2):
            rw[j].update(r_inputs(inp, l, j))
    in_maps = []
    for c in range(8):
        b, q = c // 4, c % 4
        m = {"xT0": xTs[b], "xTt": np.ascontiguousarray(xTs[b][:, q * NTOK:(q + 1) * NTOK]),
             "xtok": np.ascontiguousarray(x[b, q * NTOK:(q + 1) * NTOK]), "memT": memTs[b], "cst": _CST}
        m.update(tw)
        m.update(rw[q])
        in_maps.append(m)
    res = run_bass_kernel_spmd(nc, in_maps, core_ids=list(range(8)))
    out = np.empty((2, SEQ, D), np.float32)
    for c in range(8):
        b, q = c // 4, c % 4
        out[b, q * NTOK:(q + 1) * NTOK] = np.asarray(res.results[c]["out"])
    return out
```

```python
import numpy as np
from contextlib import ExitStack
import ml_dtypes
import concourse.bass as bass
import concourse.mybir as mybir
from concourse.bass_utils import run_bass_kernel_spmd

F32 = mybir.dt.float32
BF16 = mybir.dt.bfloat16
AF = mybir.ActivationFunctionType
ALU = mybir.AluOpType
AX = mybir.AxisListType

D = 1024
SEQ = 8192
NMEM = 256
DFF = 2816
ALPHA = 4.0 ** 0.25
EPS = 1e-5
TT = 512
SAME_ENG_SYNC = True
ARENA_KB = 196
GROUPS = [[0, 1, 2, 3], [4, 5, 6, 7]]
BLK = {'pe': 'tensor', 'act': 'scalar', 'dve': 'vector', 'pool': 'gpsimd', 'sp': 'sync'}


class Buf:
    __slots__ = ("w", "r", "wsig")

    def __init__(self):
        self.w = []
        self.r = []
        self.wsig = None


def apsig(v):
    a = v.ap
    n = 1
    for d in a.shape[1:]:
        n *= d
    return (n, tuple(map(tuple, a.ap)), a.offset, str(a.dtype))


class V:
    __slots__ = ("ap", "tile", "key")

    def __init__(self, ap, tile, key):
        self.ap = ap
        self.tile = tile
        self.key = key

    def __getitem__(self, idx):
        return V(self.ap[idx], self.tile, self.key)

    def rr(self, pat, **kw):
        return V(self.ap.rearrange(pat, **kw), self.tile, self.key)

    def bc(self, shape):
        return V(self.ap.to_broadcast(list(shape)), self.tile, self.key)

    def unsq(self, ax):
        return V(self.ap.unsqueeze(ax), self.tile, self.key)

    def bitcast(self, dt):
        return V(self.ap.bitcast(dt), self.tile, self.key)


class TK:
    def __init__(self, tile, key):
        self.tile = tile
        self.key = key

    def __getitem__(self, idx):
        return V(self.tile.base[idx], self.tile, self.key)


class Tile:
    def __init__(self, base, is_psum=False):
        self.base = base
        self.whole = Buf()
        self.kids = {}
        self.is_psum = is_psum

    def __getitem__(self, idx):
        return V(self.base[idx], self, None)

    def k(self, key):
        return TK(self, key)

    def chk(self, key):
        if key is None:
            return [self.whole] + list(self.kids.values())
        if key not in self.kids:
            self.kids[key] = Buf()
        return [self.whole, self.kids[key]]

    def upd(self, key):
        if key is None:
            return self.whole
        if key not in self.kids:
            self.kids[key] = Buf()
        return self.kids[key]


class Op:
    __slots__ = ("eng", "fn", "deps", "marked", "dma", "sem", "val", "inc")


class Prog:
    def __init__(self, nc, es):
        self.nc = nc
        self.es = es
        self.ops = {e: [] for e in BLK}
        self.esem = {e: es.enter_context(nc.semaphore(f"s_{e}")) for e in ('pe', 'act', 'dve', 'pool')}
        self.dsem = {e: [es.enter_context(nc.semaphore(f"d_{e}{i}")) for i in range(8)] for e in ('sp', 'pool', 'act')}
        self.dcnt = {e: 0 for e in self.dsem}
        self.nt = 0
        self.psb = []
        for i in range(8):
            t = es.enter_context(nc.psum_tensor(f"psb{i}", [128, 512], F32))
            self.psb.append(Tile(t, is_psum=True))
        self.psi = 0
        self.pcnt = {}
        self.out_dmas = []
        self.phase_dmas = []
        self.cache = {}
        self.arena = es.enter_context(nc.sbuf_tensor("arena", [128, ARENA_KB * 512], BF16))
        self.aoff = 0

    def sb(self, shape, dt, name=None):
        n = 1
        for d in shape[1:]:
            n *= d
        nb = n * (4 if dt == F32 else 2)
        nb = (nb + 63) // 64 * 64
        off = self.aoff
        self.aoff += nb
        assert self.aoff <= ARENA_KB * 1024, f"arena overflow {self.aoff}"
        ap = self.arena[:, off // 2:(off + nb) // 2]
        if dt == F32:
            ap = ap.bitcast(F32)
        ap = ap[:, 0:n]
        if len(shape) == 3:
            ap = ap.rearrange("p (a b) -> p a b", a=shape[1])
        elif len(shape) == 4:
            ap = ap.rearrange("p (a b c) -> p a b c", a=shape[1], b=shape[2])
        return Tile(ap)

    def barrier(self):
        lasts = []
        for eng in BLK:
            for op in reversed(self.ops[eng]):
                if op.fn is not None and not op.dma:
                    op.marked = True
                    lasts.append(op)
                    break
        dmas = list(self.phase_dmas)
        self.phase_dmas = []
        for eng in BLK:
            w = Op()
            w.eng = eng
            w.fn = None
            w.marked = False
            w.dma = False
            w.sem = None
            w.val = 0
            w.deps = list(lasts) + dmas
            self.ops[eng].append(w)
        self.aoff = 0

    def allgather(self, out_v, in_v):
        self.nt += 1
        sem = self.es.enter_context(self.nc.semaphore(f"cc{self.nt}"))
        out_ap, in_ap = out_v.ap.opt(), in_v.ap.opt()
        op = self._op('pool', lambda e: e.collective_compute("AllGather", ALU.bypass, replica_groups=GROUPS, ins=[in_ap], outs=[out_ap]),
                      [out_v], [in_v], dma=True)
        self.dcnt['pool'] -= 1
        op.sem = sem
        op.val = 1
        op.inc = 1
        return op

    def dma_fn(self, q, out, in_, fn):
        return self._op(q, fn, [out], [in_], dma=True)

    def ps(self, pool=None):
        if pool is None:
            t = self.psb[self.psi % 8]
            self.psi += 1
            return t
        key = tuple(pool)
        n = self.pcnt.get(key, 0)
        self.pcnt[key] = n + 1
        return self.psb[pool[n % len(pool)]]

    def dram(self, name, shape, dt, kind=None):
        if kind is None:
            t = self.nc.dram_tensor(name, list(shape), dt)
        else:
            t = self.nc.dram_tensor(name, list(shape), dt, kind=kind)
        return Tile(t.ap())

    def _op(self, eng, fn, outs, ins, dma=False):
        op = Op()
        op.eng = eng
        op.fn = fn
        op.marked = False
        op.dma = dma
        op.sem = None
        op.val = 0
        op.inc = 16
        deps = set()
        raw_hard = set()
        for v in ins:
            sig = apsig(v)
            for b in v.tile.chk(v.key):
                deps.update(b.w)
                for p in b.w:
                    if p.eng == eng and not p.dma and not (b.wsig is not None and b.wsig == sig and sig[0] >= 256):
                        raw_hard.add(p)
                if v.tile.is_psum:
                    deps.update(o for o in b.r if o.eng != eng)
        for v in outs:
            for b in v.tile.chk(v.key):
                deps.update(b.w)
                deps.update(b.r)
        deps.discard(op)
        keep = []
        for p in deps:
            if p.dma:
                keep.append(p)
            elif p.eng == eng:
                if eng == 'pe':
                    continue
                if dma or (SAME_ENG_SYNC and p in raw_hard):
                    p.marked = True
                    keep.append(p)
            else:
                p.marked = True
                keep.append(p)
        op.deps = keep
        for v in ins:
            b = v.tile.upd(v.key)
            if not dma:
                b.r = [o for o in b.r if o.dma or o.eng != eng]
            b.r.append(op)
        for v in outs:
            b = v.tile.upd(v.key)
            b.w = [op]
            b.r = []
            b.wsig = apsig(v)
        if dma:
            n = self.dcnt[eng]
            self.dcnt[eng] = n + 1
            op.sem = self.dsem[eng][n % 8]
            op.val = 16 * (n // 8 + 1)
            self.phase_dmas.append(op)
        self.ops[eng].append(op)
        return op

    def mm(self, out, lhsT, rhs, start=True, stop=True):
        self._op('pe', lambda e: e.matmul(out.ap, lhsT.ap, rhs.ap, start=start, stop=stop), [out], [lhsT, rhs])

    def tr(self, out, in_, ident):
        self._op('pe', lambda e: e.transpose(out.ap, in_.ap, ident.ap), [out], [in_, ident])

    def act(self, out, in_, func, bias=None, scale=None):
        ins = [in_]
        kw = {}
        if bias is not None:
            if isinstance(bias, V):
                ins.append(bias)
                kw['bias'] = bias.ap
            else:
                kw['bias'] = float(bias)
        if scale is not None:
            if isinstance(scale, V):
                ins.append(scale)
                kw['scale'] = scale.ap
            else:
                kw['scale'] = float(scale)
        self._op('act', lambda e: e.activation(out.ap, in_.ap, func, **kw), [out], ins)

    def copy(self, out, in_, eng='act'):
        if eng == 'act':
            self._op('act', lambda e: e.copy(out.ap, in_.ap), [out], [in_])
        else:
            self._op(eng, lambda e: e.tensor_copy(out.ap, in_.ap), [out], [in_])

    def tt(self, out, in0, in1, op, eng='dve'):
        self._op(eng, lambda e: e.tensor_tensor(out.ap, in0.ap, in1.ap, op), [out], [in0, in1])

    def ts(self, out, in0, s1, s2, op0, op1=None, eng='dve'):
        ins = [in0]
        a1 = s1
        a2 = s2
        if isinstance(s1, V):
            ins.append(s1)
            a1 = s1.ap
        if isinstance(s2, V):
            ins.append(s2)
            a2 = s2.ap
        if op1 is None:
            self._op(eng, lambda e: e.tensor_scalar(out.ap, in0.ap, a1, a2, op0), [out], ins)
        else:
            self._op(eng, lambda e: e.tensor_scalar(out.ap, in0.ap, a1, a2, op0, op1), [out], ins)

    def stt(self, out, in0, scalar, in1, op0, op1):
        ins = [in0, in1]
        a = scalar
        if isinstance(scalar, V):
            ins.append(scalar)
            a = scalar.ap
        self._op('dve', lambda e: e.scalar_tensor_tensor(out.ap, in0.ap, a, in1.ap, op0, op1), [out], ins)

    def scan(self, out, d0, d1, init, op0, op1):
        ins = [d0, d1]
        a = init
        if isinstance(init, V):
            ins.append(init)
            a = init.ap
        self._op('dve', lambda e: e.tensor_tensor_scan(out.ap, d0.ap, d1.ap, a, op0, op1), [out], ins)

    def red(self, out, in_, op):
        self._op('dve', lambda e: e.tensor_reduce(out.ap, in_.ap, AX.X, op), [out], [in_])

    def recip(self, out, in_):
        self._op('dve', lambda e: e.reciprocal(out.ap, in_.ap), [out], [in_])

    def bnstats(self, out, in_):
        self._op('dve', lambda e: e.bn_stats(out.ap, in_.ap), [out], [in_])

    def bnaggr(self, out, in_):
        self._op('dve', lambda e: e.bn_aggr(out.ap, in_.ap), [out], [in_])

    def memset(self, out, val, eng='dve'):
        self._op(eng, lambda e: e.memset(out.ap, val), [out], [])

    def dma(self, q, out, in_, is_out=False):
        op = self._op(q, lambda e: e.dma_start(out=out.ap, in_=in_.ap), [out], [in_], dma=True)
        if is_out:
            self.out_dmas.append(op)
        return op

    def run(self):
        fin = Op()
        fin.eng = 'sp'
        fin.fn = None
        fin.marked = False
        fin.dma = False
        fin.deps = list(self.out_dmas)
        fin.sem = None
        fin.val = 0
        self.ops['sp'].append(fin)
        for eng, lst in self.ops.items():
            c = 0
            for op in lst:
                if op.dma:
                    continue
                if op.marked:
                    c += 1
                    op.val = c
                    op.sem = self.esem[eng]
        with self.nc.Block() as block:
            for eng in BLK:
                if not self.ops[eng]:
                    continue

                def body(e, eng=eng):
                    seen = {}
                    for op in self.ops[eng]:
                        need = {}
                        for p in op.deps:
                            key = id(p.sem)
                            if seen.get(key, 0) >= p.val:
                                continue
                            if key not in need or need[key][1] < p.val:
                                need[key] = (p.sem, p.val)
                        for key, (sem_, val_) in need.items():
                            e.wait_ge(sem_, val_)
                            seen[key] = val_
                        if op.fn is None:
                            continue
                        ins = op.fn(e)
                        if op.dma:
                            ins.then_inc(op.sem, op.inc)
                        elif op.marked:
                            ins.then_inc(op.sem, 1)

                getattr(block, BLK[eng])(body)


def make_consts():
    j = np.arange(128)[:, None]
    l = np.arange(128)[None, :]
    same = (j // 64) == (l // 64)
    c = np.zeros((128, 8, 128), np.float32)
    c[:, 0] = (j == l)
    c[:, 1] = same & (j <= l)
    c[:, 2] = same
    c[:, 3] = (j < 64)
    c[:, 4] = (j >= 64)
    c[:, 5] = same & (j > l)
    c[:, 6] = (j <= l)
    c[:, 7] = same & (l >= j)
    return np.ascontiguousarray(c.reshape(128, 1024))


def softplus(P, out, x, tmp, neg=False):
    ax, y, w, q = tmp
    P.ts(ax, x, -1.0, None, ALU.mult)
    P.tt(ax, ax, x, ALU.max)
    P.act(y, ax, AF.Exp, scale=-1.0)
    P.ts(w, y, 2.0, None, ALU.add)
    P.recip(w, w)
    P.tt(w, w, y, ALU.mult)
    P.tt(y, w, w, ALU.mult)
    P.ts(q, y, 1.0 / 13.0, None, ALU.mult)
    for c in (1.0 / 11.0, 1.0 / 9.0, 1.0 / 7.0, 1.0 / 5.0, 1.0 / 3.0):
        P.stt(q, q, c, y, ALU.add, ALU.mult)
    P.stt(q, q, 1.0, w, ALU.add, ALU.mult)
    P.ts(ax, x, (-1.0 if neg else 1.0), 0.0, ALU.mult, ALU.max)
    P.stt(out, q, 2.0, ax, ALU.mult, ALU.add)

C_LRUX, C_LRUG, C_XS, C_B, C_C, C_Z, C_DT = 0, 256, 512, 1024, 1152, 1280, 1792
NWR = 1800
P_CW, P_CB, P_BA, P_BI, P_LAM, P_DTB, P_ALOG, P_DSK, P_NW = 0, 32, 40, 42, 44, 46, 54, 62, 70
NPR = 70 + 512


def emit_R(P, io, ntiles=SEQ // TT):
    wR_d, wai_d, prm_d, cst_d, yT_d = io['wR'], io['wai'], io['prm'], io['cst'], io['yT']
    prm = P.sb([128, NPR], F32)
    cst = P.sb([128, 8, 128], F32)
    identb = P.sb([128, 128], BF16)
    wR = P.sb([128, 8, NWR], BF16)
    wai = P.sb([128, 4, 128], BF16)
    P.dma('sp', prm[:, :], prm_d[:, :])
    P.dma('sp', cst[:, :, :], cst_d[:, :].rr("p (a b) -> p a b", a=8))
    P.dma('pool', wai[:, :, :], wai_d[:, :].rr("p (a b) -> p a b", a=4))
    wRv = wR_d[:, :].rr("(kc p) n -> p kc n", p=128)
    for kc in range(8):
        P.dma('pool', wR[:, kc, :], wRv[:, kc, :])
    P.copy(identb[:, :], cst[:, 0, :])
    TriBD, OnesBD, Half0, Half1, GtBD, mask2, maskCB = (cst[:, i, :] for i in range(1, 8))

    clru = P.sb([128, 2], F32)
    spw = P.sb([128, 8], F32)
    softplus(P, clru[:, :], prm[:, P_LAM:P_LAM + 2], [spw[:, 2 * i:2 * i + 2] for i in range(4)], neg=True)
    P.ts(clru[:, :], clru[:, :], -8.0, None, ALU.mult)
    Abc = P.sb([128, 8], F32)
    P.act(Abc[:, :], prm[:, P_ALOG:P_ALOG + 8], AF.Exp)
    P.ts(Abc[:, :], Abc[:, :], -1.0, None, ALU.mult)

    cbuf = [P.sb([128, 4 + TT], BF16) for _ in range(8)]
    for c in range(8):
        P.memset(cbuf[c][:, 0:3], 0.0)
    dg = P.sb([128, 32, 128], BF16)
    for c in range(8):
        for j in range(4):
            P.ts(dg[:, c * 4 + j, :], cst[:, 0, :], prm[:, P_CW + c * 4 + j:P_CW + c * 4 + j + 1], None, ALU.mult)
    hst = P.sb([128, 2], F32)
    P.memset(hst[:, :], 0.0)
    state = P.sb([128, 8, 64], F32)
    P.memset(state[:, :, :], 0.0)
    CTp = [P.sb([128, 4, 2, 128], BF16) for _ in range(2)]
    for t in CTp:
        P.memset(t[:, :, :, :], 0.0)

    xTs = [P.sb([128, 8, TT], BF16) for _ in range(2)]
    accs = [P.sb([128, TT], F32) for _ in range(2)]
    xc = [P.sb([128, TT], F32) for _ in range(2)]
    xcb = [P.sb([128, TT], BF16) for _ in range(2)]
    xsT = [P.sb([128, 4, TT], BF16) for _ in range(2)]
    BT = [P.sb([128, TT], BF16) for _ in range(2)]
    CT = [P.sb([128, TT], BF16) for _ in range(2)]
    lr = [P.sb([128, TT], F32) for _ in range(6)]
    ylT = [P.sb([128, 2, TT], BF16) for _ in range(2)]
    ysT = [P.sb([128, 4, TT], BF16) for _ in range(2)]
    zs = [P.sb([128, 512], F32) for _ in range(2)]
    sm = [P.sb([128, 96], F32) for _ in range(2)]
    Rm = [P.sb([128, 8, 128], F32) for _ in range(2)]
    dec = [P.sb([128, 8, 128], F32) for _ in range(2)]
    CBm = [P.sb([128, 128], F32) for _ in range(2)]
    MT = [P.sb([128, 8, 128], BF16) for _ in range(2)]
    xs_tok = [P.sb([128, 8, 64], BF16) for _ in range(2)]
    B_tok = [P.sb([128, 128], BF16) for _ in range(2)]
    xdt = [P.sb([128, 8, 64], BF16) for _ in range(2)]
    xdec = [P.sb([128, 8, 64], BF16) for _ in range(2)]
    prevb = [P.sb([128, 512], BF16) for _ in range(4)]
    yb = [P.sb([128, 8, 64], F32) for _ in range(2)]
    tmpx = [P.sb([128, 8, 64], F32) for _ in range(2)]
    y3 = [P.sb([128, 512], BF16) for _ in range(2)]
    dtall = [P.sb([128, 32], F32) for _ in range(2)]
    dtw = [P.sb([128, 32], F32) for _ in range(5)]

    ccol = [C_LRUX, C_LRUX + 128, C_XS, C_XS + 128, C_XS + 256, C_XS + 384, C_B, C_C]

    PX, PY = [6, 7], [[0, 1, 2], [3, 4, 5]]

    def gen_X(ti):
        xT = xTs[ti % 2]
        io['load_xT'](ti, xT)
        xsTt, BTt, CTt, CTpt = xsT[ti % 2], BT[ti % 2], CT[ti % 2], CTp[ti % 2]

        def fm_chunk(col):
            ps = P.ps(PX)
            for kc in range(8):
                P.mm(ps[:, :], wR[:, kc, col:col + 128], xT[:, kc, :], start=(kc == 0), stop=(kc == 7))
            return ps

        for c in range(8):
            ps = fm_chunk(ccol[c])
            cb = cbuf[c]
            P.copy(cb[:, 3:3 + TT], ps[:, :])
            yield
            acc = P.ps(PX)
            for j in range(4):
                P.mm(acc[:, :], dg[:, c * 4 + j, :], cb[:, j:j + TT], start=(j == 0), stop=(j == 3))
            P.copy(cb[:, 0:3], cb[:, TT:TT + 3], eng='dve')
            yield
            bias = prm[:, P_CB + c:P_CB + c + 1]
            if c < 2:
                P.act(xc[c][:, :], acc[:, :], AF.Identity, bias=bias)
                P.act(xcb[c][:, :], acc[:, :], AF.Identity, bias=bias)
            elif c < 6:
                P.act(xsTt[:, c - 2, :], acc[:, :], AF.Silu, bias=bias)
            elif c == 6:
                P.act(BTt[:, :], acc[:, :], AF.Silu, bias=bias)
            else:
                P.act(CTt[:, :], acc[:, :], AF.Silu, bias=bias)
                a4 = acc[:, :].rr("p (s h t) -> p s h t", s=4, h=2)
                for h in range(2):
                    P.act(CTpt[:, :, h, h * 64:(h + 1) * 64], a4[:, :, h, :], AF.Silu, bias=bias)
            yield
        psd = P.ps(PX)
        for sub in range(4):
            for kc in range(8):
                P.mm(psd[:, sub * 8:(sub + 1) * 8], xT[:, kc, sub * 128:(sub + 1) * 128], wR[:, kc, C_DT:C_DT + 8], start=(kc == 0), stop=(kc == 7))
        dtx = dtw[0][:, :]
        P.tt(dtx.rr("p (s k) -> p s k", s=4), psd[:, 0:32].rr("p (s k) -> p s k", s=4), prm[:, P_DTB:P_DTB + 8].unsq(1).bc([128, 4, 8]), ALU.add)
        yield
        softplus(P, dtall[ti % 2][:, :], dtx, [t[:, :] for t in dtw[1:5]])
        yield
        ylTt = ylT[ti % 2]
        for n in range(2):
            r_, i_, a_, s_, u_, h_ = lr
            ps = P.ps(PX)
            P.mm(ps[:, :], wai[:, n, :], xcb[n][:, :])
            P.act(r_[:, :], ps[:, :], AF.Sigmoid, bias=prm[:, P_BA + n:P_BA + n + 1])
            ps = P.ps(PX)
            P.mm(ps[:, :], wai[:, 2 + n, :], xcb[n][:, :])
            P.act(i_[:, :], ps[:, :], AF.Sigmoid, bias=prm[:, P_BI + n:P_BI + n + 1])
            yield
            P.act(a_[:, :], r_[:, :], AF.Exp, scale=clru[:, n:n + 1])
            P.tt(s_[:, :], a_[:, :], a_[:, :], ALU.mult)
            P.act(s_[:, :], s_[:, :], AF.Sqrt, bias=1.0, scale=-1.0)
            yield
            P.tt(u_[:, :], i_[:, :], xc[n][:, :], ALU.mult)
            P.tt(u_[:, :], u_[:, :], s_[:, :], ALU.mult)
            yield
            P.scan(h_[:, :], a_[:, :], u_[:, :], hst[:, n:n + 1], ALU.mult, ALU.add)
            P.copy(hst[:, n:n + 1], h_[:, TT - 1:TT], eng='dve')
            yield
            psg = fm_chunk(C_LRUG + n * 128)
            P.act(r_[:, :], psg[:, :], AF.Square)
            P.ts(r_[:, :], r_[:, :], 0.044715, 1.0, ALU.mult, ALU.add)
            yield
            P.tt(r_[:, :], r_[:, :], psg[:, :], ALU.mult)
            P.act(r_[:, :], r_[:, :], AF.Sigmoid, scale=1.5957691216057308)
            yield
            P.tt(r_[:, :], r_[:, :], psg[:, :], ALU.mult)
            P.tt(ylTt[:, n, :], r_[:, :], h_[:, :], ALU.mult)
            yield
        yt2 = yT_d.k(ti)[ti, :].rr("(f t) -> f t", t=TT)
        P.dma('sp', yt2[0:256, :].rr("(c p) t -> p c t", p=128), ylTt[:, :, :])

    def gen_Y(ti, sub):
        xT = xTs[ti % 2]
        xsTt, BTt, CTt, CTpt = xsT[ti % 2], BT[ti % 2], CT[ti % 2], CTp[ti % 2]
        ysTt = ysT[ti % 2]
        q = sub % 2
        pool = PY[q]
        tok = slice(sub * 128, (sub + 1) * 128)
        smq = sm[q]
        da = smq[:, 24:32]
        cs_sb, expcs, dend, cdec = smq[:, 32:40], smq[:, 40:48], smq[:, 48:56], smq[:, 56:72]
        tmp8, ss, rstd = smq[:, 72:80], smq[:, 80:81], smq[:, 81:82]
        psz = P.ps(pool)
        for kc in range(8):
            P.mm(psz[:, :], xT[:, kc, tok], wR[:, kc, C_Z:C_Z + 512], start=(kc == 0), stop=(kc == 7))
        P.act(zs[q][:, :], psz[:, :], AF.Silu)
        dt = dtall[ti % 2][:, sub * 8:(sub + 1) * 8]
        P.tt(da, dt, Abc[:, :], ALU.mult)
        yield
        pst = P.ps(pool)
        pstb = pst[:, :].bitcast(BF16)
        for c in range(4):
            P.tr(pstb[:, c * 128:(c + 1) * 128], xsTt[:, c, tok], identb[:, :])
        P.copy(xs_tok[q][:, :, :], pstb[:, 0:512].rr("p (k d) -> p k d", k=8))
        psb_ = P.ps(pool)
        psbb = psb_[:, :].bitcast(BF16)
        P.tr(psbb[:, 0:128], BTt[:, tok], identb[:, :])
        P.copy(B_tok[q][:, :], psbb[:, 0:128])
        yield
        psc = P.ps(pool)
        P.mm(psc[:, 0:8], TriBD, da)
        P.mm(psc[:, 8:16], OnesBD, da)
        P.mm(psc[:, 16:24], Half0, da)
        P.mm(psc[:, 24:32], Half1, da)
        P.copy(cs_sb, psc[:, 0:8])
        P.act(expcs, psc[:, 0:8], AF.Exp)
        yield
        P.tt(tmp8, psc[:, 8:16], cs_sb, ALU.subtract)
        P.act(dend, tmp8, AF.Exp)
        P.act(cdec, psc[:, 16:32], AF.Exp)
        yield
        P.tt(Rm[q][:, :, :], da.unsq(2).bc([128, 8, 128]), mask2.unsq(1).bc([128, 8, 128]), ALU.mult)
        yield
        Rf = Rm[q][:, :, :].rr("p k l -> p (k l)")
        psA = P.ps(pool)
        P.mm(psA[:, :], GtBD, Rf[:, 0:512])
        psB = P.ps(pool)
        P.mm(psB[:, :], GtBD, Rf[:, 512:1024])
        df = dec[q][:, :, :].rr("p k l -> p (k l)")
        P.act(df[:, 0:512], psA[:, :], AF.Exp)
        P.act(df[:, 512:1024], psB[:, :], AF.Exp)
        yield
        psC = P.ps(pool)
        P.mm(psC[:, 0:128], BTt[:, tok], CTt[:, tok])
        P.tt(CBm[q][:, :], psC[:, 0:128], maskCB, ALU.mult)
        yield
        P.tt(MT[q][:, :, :], dec[q][:, :, :], CBm[q][:, :].unsq(1).bc([128, 8, 128]), ALU.mult)
        yield
        P.tt(xdt[q][:, :, :], xs_tok[q][:, :, :], dt.unsq(2).bc([128, 8, 64]), ALU.mult)
        P.tt(xdec[q][:, :, :], xdt[q][:, :, :], dend.unsq(2).bc([128, 8, 64]), ALU.mult)
        yield
        psO = P.ps(pool)
        xdf = xdec[q][:, :, :].rr("p k d -> p (k d)")
        sf = state[:, :, :].rr("p k d -> p (k d)")
        for h in range(2):
            psS = P.ps(pool)
            P.mm(psS[:, :], B_tok[q][64 * h:64 * h + 64, :], xdf[64 * h:64 * h + 64, :])
            pv = prevb[(2 * q + h)]
            P.copy(pv[:, :], sf)
            P.tt(state[:, :, :], state[:, :, :], cdec[:, 8 * h:8 * h + 8].unsq(2).bc([128, 8, 64]), ALU.mult)
            P.tt(sf, sf, psS[:, :], ALU.add)
            P.mm(psO[:, :], CTpt[:, sub, h, :], pv[:, :], start=(h == 0), stop=(h == 1))
            yield
        y = yb[q]
        yf = y[:, :, :].rr("p k d -> p (k d)")
        P.tt(y[:, :, :], psO[:, :].rr("p (k d) -> p k d", k=8), expcs.unsq(2).bc([128, 8, 64]), ALU.mult)
        yield
        psY = P.ps(pool)
        for k in range(8):
            P.mm(psY[:, k * 64:(k + 1) * 64], MT[q][:, k, :], xdt[q][:, k, :])
        P.tt(yf, yf, psY[:, :], ALU.add)
        yield
        P.tt(tmpx[q][:, :, :], xs_tok[q][:, :, :], prm[:, P_DSK:P_DSK + 8].unsq(2).bc([128, 8, 64]), ALU.mult)
        P.tt(y[:, :, :], y[:, :, :], tmpx[q][:, :, :], ALU.add)
        yield
        P.tt(yf, yf, zs[q][:, :], ALU.mult)
        tf = tmpx[q][:, :, :].rr("p k d -> p (k d)")
        P.tt(tf, yf, yf, ALU.mult)
        yield
        P.red(ss, tf, ALU.add)
        P.ts(ss, ss, 1.0 / 512.0, EPS, ALU.mult, ALU.add)
        P.act(ss, ss, AF.Sqrt)
        P.recip(rstd, ss)
        yield
        P.stt(y3[q][:, :], yf, rstd, prm[:, P_NW:P_NW + 512], ALU.mult, ALU.mult)
        psT = P.ps(pool)
        psTb = psT[:, :].bitcast(BF16)
        for c in range(4):
            P.tr(psTb[:, c * 128:(c + 1) * 128], y3[q][:, c * 128:(c + 1) * 128], identb[:, :])
        P.copy(ysTt[:, :, tok], psTb[:, 0:512].rr("p (c t) -> p c t", c=4))

    def run_tile(ygens, bg):
        SKEW = 5
        active = []
        pending = list(ygens)
        bg_done = bg is None
        while pending or active:
            if pending and len(active) < 2 and (not active or active[-1][1] >= SKEW):
                active.append([pending.pop(0), 0])
            for a in list(active):
                try:
                    next(a[0])
                    a[1] += 1
                except StopIteration:
                    active.remove(a)
            if not bg_done:
                try:
                    next(bg)
                except StopIteration:
                    bg_done = True
        while not bg_done:
            try:
                next(bg)
            except StopIteration:
                bg_done = True

    run_tile([], gen_X(0))
    for ti in range(ntiles):
        bg = gen_X(ti + 1) if ti + 1 < ntiles else None
        run_tile([gen_Y(ti, sub) for sub in range(4)], bg)
        yt2 = yT_d.k(ti)[ti, :].rr("(f t) -> f t", t=TT)
        P.dma('sp', yt2[256:768, :].rr("(c p) t -> p c t", p=128), ysT[ti % 2][:, :, :])
        io['tile_done'](ti)


NTOK = 2048
P_BG = 0
NPT = 24


class WStream:
    def __init__(self, P, nslots, cache=None):
        self.P = P
        self.slots = [P.sb([128, 8, 512], BF16) for _ in range(nslots)]
        self.i = 0
        self.cache = cache
        self.ids = {}

    def load(self, wd, r0, nrows, c0, ncols=512, cacheable=True):
        P = self.P
        slot = self.slots[self.i % len(self.slots)]
        self.i += 1
        nkc = nrows // 128
        sv = slot[:, 0:nkc, 0:ncols]
        key = (id(wd), r0, c0)
        if self.cache is not None and cacheable and key in self.ids:
            pid = self.ids[key]
            cv = self.cache.k(pid)[pid, 0:128 * nkc * ncols].rr("(p k n) -> p k n", p=128, k=nkc)
            P.dma('sp', sv, cv)
            return slot
        src = wd[r0:r0 + nrows, c0:c0 + ncols].rr("(kc p) n -> p kc n", p=128)
        P.dma('pool', sv, src)
        if self.cache is not None and cacheable:
            pid = len(self.ids)
            self.ids[key] = pid
            cv = self.cache.k(pid)[pid, 0:128 * nkc * ncols].rr("(p k n) -> p k n", p=128, k=nkc)
            P.dma('sp', cv, sv)
        return slot


def emit_T(P, io, ntiles=NTOK // TT):
    ws = WStream(P, 4, io.get('wcache'))
    prmT = P.sb([128, NPT], F32)
    lnp = P.sb([128, 4, D], F32)
    identf = P.sb([128, 128], F32)
    identb = P.sb([128, 128], BF16)
    P.dma('sp', prmT[:, :], io['prmT'][:, :])
    P.dma('sp', lnp[:, :, :], io['lnp'][:, :].rr("p (a b) -> p a b", a=4))
    P.dma('sp', identf[:, :], io['cst'][:, 0:128])
    P.copy(identb[:, :], identf[:, :])

    memT = P.sb([128, 8, NMEM], BF16)
    P.dma('pool', memT[:, :, :], io['memT'][:, :].rr("(kc p) m -> p kc m", p=128))
    KT = P.sb([128, 8, NMEM], BF16)
    Vt = P.sb([128, 2, 1024], BF16)
    for ph in range(2):
        pan = ws.load(io['wkv'], 0, 1024, ph * 512, cacheable=False)
        for c in range(4):
            ps = P.ps()
            for kc in range(8):
                P.mm(ps[:, 0:NMEM], pan[:, kc, c * 128:(c + 1) * 128], memT[:, kc, :], start=(kc == 0), stop=(kc == 7))
            P.copy(KT.k(ph * 4 + c)[:, ph * 4 + c, :], ps[:, 0:NMEM])
    for ph in range(2):
        pan = ws.load(io['wkv'], 0, 1024, 1024 + ph * 512, cacheable=False)
        for mc in range(2):
            ps = P.ps()
            for kc in range(8):
                P.mm(ps[:, :], memT[:, kc, mc * 128:(mc + 1) * 128], pan[:, kc, :], start=(kc == 0), stop=(kc == 7))
            P.copy(Vt.k((mc, ph))[:, mc, ph * 512:(ph + 1) * 512], ps[:, :])

    xT = P.sb([128, 8, TT], BF16)
    xtok = P.sb([128, 4, D], F32)
    x1 = P.sb([128, 4, D], F32)
    qm = P.sb([128, 8, TT], BF16)
    x1T = P.sb([128, 8, TT], BF16)
    PT = [P.sb([128, 2, TT], BF16) for _ in range(2)]
    yxaT = P.sb([128, 8, TT], BF16)
    ybuf = P.sb([128, 16, TT], BF16)
    acc = P.sb([128, 8, TT], F32)
    hT = P.sb([128, 22, TT], BF16)
    gs = [P.sb([128, TT], F32) for _ in range(2)]
    es_ = [P.sb([128, NMEM], F32) for _ in range(2)]
    pn = [P.sb([128, NMEM], BF16) for _ in range(2)]
    sm = [P.sb([128, 8], F32) for _ in range(2)]
    st = [P.sb([128, 12], F32) for _ in range(2)]
    mv = [P.sb([128, 4], F32) for _ in range(2)]

    lncnt = [0]

    def layer_norm(v, g, b):
        i = lncnt[0] % 2
        lncnt[0] += 1
        P.bnstats(st[i][:, 0:6], v[:, 0:512])
        P.bnstats(st[i][:, 6:12], v[:, 512:1024])
        P.bnaggr(mv[i][:, 0:2], st[i][:, :])
        P.ts(mv[i][:, 2:3], mv[i][:, 1:2], EPS, None, ALU.add)
        P.act(mv[i][:, 2:3], mv[i][:, 2:3], AF.Sqrt)
        P.recip(mv[i][:, 3:4], mv[i][:, 2:3])
        P.ts(v, v, mv[i][:, 0:1], mv[i][:, 3:4], ALU.subtract, ALU.mult)
        P.tt(v, v, g, ALU.mult)
        P.tt(v, v, b, ALU.add)

    for ti in range(ntiles):
        t0 = ti * TT
        io['load_xT'](ti, xT)
        io['load_xtok'](ti, xtok)
        for ph in range(2):
            pan = ws.load(io['wq'], 0, 1024, ph * 512)
            for c in range(4):
                dc = ph * 4 + c
                ps = P.ps()
                for kc in range(8):
                    P.mm(ps[:, :], pan[:, kc, c * 128:(c + 1) * 128], xT[:, kc, :], start=(kc == 0), stop=(kc == 7))
                P.copy(qm.k(dc)[:, dc, :], ps[:, :])
        def gen_A(pool):
            cnt = 0
            for h in range(4):
                PTh = PT[h % 2]
                for s in range(4):
                    i = cnt % 2
                    cnt += 1
                    tok = slice(s * 128, (s + 1) * 128)
                    ps = P.ps(pool)
                    for dd in range(2):
                        P.mm(ps[:, 0:NMEM], qm.k(2 * h + dd)[:, 2 * h + dd, tok], KT.k(2 * h + dd)[:, 2 * h + dd, :], start=(dd == 0), stop=(dd == 1))
                    mx, nb, sm_, rs = sm[i][:, 0:1], sm[i][:, 1:2], sm[i][:, 2:3], sm[i][:, 3:4]
                    P.red(mx, ps[:, 0:NMEM], ALU.max)
                    P.ts(nb, mx, -1.0 / 16.0, None, ALU.mult)
                    yield
                    P.act(es_[i][:, :], ps[:, 0:NMEM], AF.Exp, bias=nb, scale=1.0 / 16.0)
                    P.red(sm_, es_[i][:, :], ALU.add)
                    P.recip(rs, sm_)
                    yield
                    P.ts(pn[i][:, :], es_[i][:, :], rs, None, ALU.mult)
                    pst = P.ps(pool)
                    pstb = pst[:, :].bitcast(BF16)
                    for mc in range(2):
                        P.tr(pstb[:, mc * 128:(mc + 1) * 128], pn[i][:, mc * 128:(mc + 1) * 128], identb[:, :])
                    P.copy(PTh[:, :, tok], pstb[:, 0:256].rr("p (m t) -> p m t", m=2))
                    yield
                for dd in range(2):
                    ps = P.ps(pool)
                    for mc in range(2):
                        P.mm(ps[:, :], Vt[:, mc, h * 256 + dd * 128:h * 256 + (dd + 1) * 128], PTh[:, mc, :], start=(mc == 0), stop=(mc == 1))
                    P.copy(yxaT.k(2 * h + dd)[:, 2 * h + dd, :], ps[:, :])
                yield

        def gen_B(bs, pool):
            for b in bs:
                wname, nkp = [('wbl', 1), ('wbs', 2), ('wbx', 1)][b]
                if b < 2:
                    io['load_y'](ti, b, ybuf)
                ysb = ybuf if b < 2 else yxaT
                for ph in range(2):
                    pans = [ws.load(io[wname], kp * 1024, 1024, ph * 512) for kp in range(nkp)]
                    gp = ws.load(io['wg'], 0, 1024, b * 1024 + ph * 512)
                    for c in range(4):
                        dc = ph * 4 + c
                        psp = P.ps(pool)
                        n = nkp * 8
                        i = 0
                        for kp in range(nkp):
                            for kc in range(8):
                                P.mm(psp[:, :], pans[kp][:, kc, c * 128:(c + 1) * 128], ysb[:, kp * 8 + kc, :], start=(i == 0), stop=(i == n - 1))
                                i += 1
                            yield
                        psg = P.ps(pool)
                        for kc in range(8):
                            P.mm(psg[:, :], gp[:, kc, c * 128:(c + 1) * 128], xT[:, kc, :], start=(kc == 0), stop=(kc == 7))
                        g = gs[dc % 2]
                        P.act(g[:, :], psg[:, :], AF.Sigmoid, bias=prmT[:, P_BG + b * 8 + dc:P_BG + b * 8 + dc + 1])
                        yield
                        if b == 0:
                            P.tt(acc.k(dc)[:, dc, :], g[:, :], psp[:, :], ALU.mult)
                        else:
                            P.tt(g[:, :], g[:, :], psp[:, :], ALU.mult)
                            if b == 1:
                                P.tt(acc.k(dc)[:, dc, :], acc.k(dc)[:, dc, :], g[:, :], ALU.add)
                            else:
                                P.tt(qm.k(dc)[:, dc, :], acc.k(dc)[:, dc, :], g[:, :], ALU.add)
                        yield

        gens = [gen_A([0, 1, 2]), gen_B([0, 1], [3, 4, 5, 6, 7])]
        while gens:
            for g_ in list(gens):
                try:
                    next(g_)
                except StopIteration:
                    gens.remove(g_)
        for _ in gen_B([2], None):
            pass
        for ph in range(2):
            pan = ws.load(io['wo'], 0, 1024, ph * 512)
            for s in range(4):
                tok = slice(s * 128, (s + 1) * 128)
                ps = P.ps()
                for kc in range(8):
                    P.mm(ps[:, :], qm[:, kc, tok], pan[:, kc, :], start=(kc == 0), stop=(kc == 7))
                P.stt(x1.k(s)[:, s, ph * 512:(ph + 1) * 512], xtok.k(s)[:, s, ph * 512:(ph + 1) * 512], ALPHA, ps[:, :], ALU.mult, ALU.add)
        for s in range(4):
            tok = slice(s * 128, (s + 1) * 128)
            layer_norm(x1.k(s)[:, s, :], lnp[:, 0, :], lnp[:, 1, :])
            for g4 in range(2):
                ps = P.ps()
                for j in range(4):
                    dc = g4 * 4 + j
                    P.tr(ps[:, j * 128:(j + 1) * 128], x1.k(s)[:, s, dc * 128:(dc + 1) * 128], identf[:, :])
                P.copy(x1T[:, g4 * 4:(g4 + 1) * 4, tok], ps[:, :].rr("p (j t) -> p j t", j=4))
        for p in range(11):
            pan = ws.load(io['wfi'], 0, 1024, p * 512)
            for c in range(4):
                col = p * 512 + c * 128
                ps = P.ps()
                for kc in range(8):
                    P.mm(ps[:, :], pan[:, kc, c * 128:(c + 1) * 128], x1T[:, kc, :], start=(kc == 0), stop=(kc == 7))
                if col < DFF:
                    ffc = col // 128
                    P.act(hT.k(ffc)[:, ffc, :], ps[:, :], AF.Silu)
                else:
                    ffc = (col - DFF) // 128
                    P.tt(hT.k(ffc)[:, ffc, :], hT.k(ffc)[:, ffc, :], ps[:, :], ALU.mult)
        for ph in range(2):
            pss = [P.ps() for _ in range(4)]
            for (r0, nr) in [(0, 1024), (1024, 1024), (2048, 768)]:
                pan = ws.load(io['wfd'], r0, nr, ph * 512)
                for s in range(4):
                    tok = slice(s * 128, (s + 1) * 128)
                    for kc in range(nr // 128):
                        ffc = r0 // 128 + kc
                        P.mm(pss[s][:, :], hT.k(ffc)[:, ffc, tok], pan[:, kc, :], start=(ffc == 0), stop=(ffc == 21))
            for s in range(4):
                P.stt(xtok.k(s)[:, s, ph * 512:(ph + 1) * 512], x1.k(s)[:, s, ph * 512:(ph + 1) * 512], ALPHA, pss[s][:, :], ALU.mult, ALU.add)
        for s in range(4):
            layer_norm(xtok.k(s)[:, s, :], lnp[:, 2, :], lnp[:, 3, :])
        io['store_out'](ti, xtok)
        if io.get('store_xT') is not None:
            for s in range(4):
                tok = slice(s * 128, (s + 1) * 128)
                for g4 in range(2):
                    ps = P.ps()
                    for j in range(4):
                        dc = g4 * 4 + j
                        P.tr(ps[:, j * 128:(j + 1) * 128], xtok.k(s)[:, s, dc * 128:(dc + 1) * 128], identf[:, :])
                    P.copy(qm[:, g4 * 4:(g4 + 1) * 4, tok], ps[:, :].rr("p (j t) -> p j t", j=4))
            io['store_xT'](ti, qm)


R_W = {'wR': [D, NWR], 'wai': [128, 512], 'prm': [128, NPR]}
T_W = {'wq': [D, 1024], 'wg': [D, 3072], 'wkv': [D, 2048], 'wbl': [1024, D], 'wbs': [2048, D], 'wbx': [1024, D],
       'wo': [D, D], 'wfi': [D, 2 * DFF], 'wfd': [DFF, D], 'prmT': [128, NPT], 'lnp': [128, 4 * D]}


def build_fused(nc, es, stop=None):
    P = Prog(nc, es)

    def ext(n, sh):
        return P.dram(n, sh, F32, "ExternalInput")

    xT0 = ext("xT0", [D, SEQ])
    xTt = ext("xTt", [D, NTOK])
    xtok_d = ext("xtok", [NTOK, D])
    memT = ext("memT", [D, NMEM])
    cst = ext("cst", [128, 1024])
    out_d = P.dram("out", [NTOK, D], F32, "ExternalOutput")
    WR = [{k: ext(f"{k}{l}", sh) for k, sh in R_W.items()} for l in range(2)]
    WT = [{k: ext(f"{k}{l}", sh) for k, sh in T_W.items()} for l in range(2)]
    YT = 768 * TT
    y_loc = [P.dram(f"y_loc{l}", [16, YT], BF16) for l in range(2)]
    y_all = [P.dram(f"y_all{l}", [16, 4 * YT], BF16) for l in range(2)]
    y_mine = [P.dram(f"y_mine{l}", [4, 4 * YT], BF16, "ExternalOutput" if stop == f"R{l}" else None) for l in range(2)]
    x1_d = P.dram("x1_d", [NTOK, D], F32, "ExternalOutput" if stop in ("T0", "DBG") else None)
    wcache = [P.dram(f"wcache{l}", [36, 128 * 8 * 512], BF16) for l in range(2)]
    XT = 512 * TT
    x1T_loc = P.dram("x1T_loc", [8, XT], BF16)
    x1T_all = P.dram("x1T_all", [8, 4 * XT], BF16)

    def tokview(t):
        return t[:, :].rr("(s p) d -> p s d", p=128)

    for l in range(2):
        if l == 0:
            xv = xT0[:, :].rr("(kc p) t -> p kc t", p=128)

            def load_xT_R(ti, xT, xv=xv):
                P.dma('pool', xT[:, :, :], xv[:, :, ti * TT:(ti + 1) * TT])
        else:
            def load_xT_R(ti, xT):
                q, i = ti // 4, ti % 4
                for h in range(2):
                    src = x1T_all[i * 2 + h, q * XT:(q + 1) * XT].rr("(c p t) -> p c t", p=128, t=TT)
                    P.dma('pool', xT[:, 4 * h:4 * h + 4, :], src)
        yl, yal = y_loc[l], y_all[l]

        pend = []

        def tile_done(ti, yl=yl, yal=yal, pend=pend):
            while pend:
                pend.pop(0)()
            pend.append(lambda: P.allgather(yal.k(ti)[ti, :], yl.k(ti)[ti, :]))
        ioR = dict(WR[l])
        ioR.update(cst=cst, yT=y_loc[l], load_xT=load_xT_R, tile_done=tile_done)
        emit_R(P, ioR)
        while pend:
            pend.pop(0)()
        P.barrier()

        ya = y_all[l]

        ym = y_mine[l]

        def fn(e, ym=ym, ya=ya):
            q = e.partition_id() % 4
            yv = ya.base.rearrange("(q i) (r c) -> q (i r) c", q=4, c=16384)
            src = yv[bass.ds(q, 1), :, :].rearrange("o r c -> (o r) c")
            dst = ym.base.rearrange("i (r c) -> (i r) c", c=16384)
            return e.dma_start(out=dst, in_=src)
        op_ = P.dma_fn('pool', ym[:, :], ya[:, :], fn)
        if stop == f"R{l}":
            P.out_dmas.append(op_)
            break

        def load_y(ti, b, ybuf, ym=ym):
            r0, r1, ncc = (0, 256, 2) if b == 0 else (256, 768, 4)
            for j in range(4):
                src = ym[ti, j * YT:(j + 1) * YT].rr("(f t) -> f t", t=TT)[r0:r1, :].rr("(c p) t -> p c t", p=128)
                P.dma('sp', ybuf[:, j * ncc:(j + 1) * ncc, :], src)

        ioT = dict(WT[l])
        ioT.update(cst=cst, memT=memT, load_y=load_y, wcache=wcache[l])
        if l == 0:
            xtv = xTt[:, :].rr("(kc p) t -> p kc t", p=128)
            ioT['load_xT'] = lambda ti, xT, xtv=xtv: P.dma('pool', xT[:, :, :], xtv[:, :, ti * TT:(ti + 1) * TT])
            ioT['load_xtok'] = lambda ti, xt: P.dma('sp', xt[:, :, :], tokview(xtok_d)[:, ti * 4:(ti + 1) * 4, :])
            ioT['store_out'] = lambda ti, xt: P.dma('sp', tokview(x1_d)[:, ti * 4:(ti + 1) * 4, :], xt[:, :, :])
            pend2 = []

            def store_xT(ti, qm):
                while pend2:
                    pend2.pop(0)()
                for h in range(2):
                    n = ti * 2 + h
                    dst = x1T_loc.k(n)[n, :].rr("(c p t) -> p c t", p=128, t=TT)
                    P.dma('sp', dst, qm[:, 4 * h:4 * h + 4, :])
                    pend2.append(lambda n=n: P.allgather(x1T_all.k(n)[n, :], x1T_loc.k(n)[n, :]))
            ioT['store_xT'] = store_xT
        else:
            def load_xT_T(ti, xT):
                for h in range(2):
                    src = x1T_loc[ti * 2 + h, :].rr("(c p t) -> p c t", p=128, t=TT)
                    P.dma('pool', xT[:, 4 * h:4 * h + 4, :], src)
            ioT['load_xT'] = load_xT_T
            ioT['load_xtok'] = lambda ti, xt: P.dma('sp', xt[:, :, :], tokview(x1_d)[:, ti * 4:(ti + 1) * 4, :])
            ioT['store_out'] = lambda ti, xt: P.dma('sp', tokview(out_d)[:, ti * 4:(ti + 1) * 4, :], xt[:, :, :], is_out=True)
            ioT['store_xT'] = None
        emit_T(P, ioT)
        if l == 1 and stop == "DBG":
            dbg = P.dram("dbg_x1T", [8, XT], BF16, "ExternalOutput")
            P.dma('sp', dbg[:, :], x1T_loc[:, :], is_out=True)
        if l == 0:
            while pend2:
                pend2.pop(0)()
            P.barrier()
            if stop == "T0":
                break
    P.run()


OFF = {'lrux': 0, 'lrug': 1024, 'z': 2048, 'xbc': 4096, 'dt': 7168, 'q': 7200, 'g': 8224}
_CST = make_consts()
_prog = []


def get_prog():
    if not _prog:
        nc = bass.Bass("TRN2", target_bir_lowering=False)
        es = ExitStack()
        build_fused(nc, es)
        _prog.append((nc, es))
    return _prog[0][0]


def rep(v, n=128):
    v = np.asarray(v, np.float32)
    return np.ascontiguousarray(np.broadcast_to(v.reshape(1, -1), (n, v.size)))


def pcol(v):
    v = np.asarray(v, np.float32)
    return np.ascontiguousarray(v.reshape(-1, 128).T)


def r_inputs(inp, l, j):
    w_in = inp['w_in'][l]
    lr = slice(256 * j, 256 * (j + 1))
    gr = slice(512 * j, 512 * (j + 1))
    hr = slice(8 * j, 8 * (j + 1))
    xbc0 = OFF['xbc']
    bsl = slice(2048 + 128 * j, 2048 + 128 * (j + 1))
    csl = slice(2560 + 128 * j, 2560 + 128 * (j + 1))
    wR = np.concatenate([
        w_in[:, OFF['lrux'] + 256 * j:OFF['lrux'] + 256 * (j + 1)],
        w_in[:, OFF['lrug'] + 256 * j:OFF['lrug'] + 256 * (j + 1)],
        w_in[:, xbc0 + 512 * j:xbc0 + 512 * (j + 1)],
        w_in[:, xbc0 + bsl.start:xbc0 + bsl.stop],
        w_in[:, xbc0 + csl.start:xbc0 + csl.stop],
        w_in[:, OFF['z'] + 512 * j:OFF['z'] + 512 * (j + 1)],
        w_in[:, OFF['dt'] + 8 * j:OFF['dt'] + 8 * (j + 1)],
    ], axis=1)
    lcw = inp['lru_conv_w'][l][:, lr]
    scw = inp['ssd_conv_w'][l]
    scb = inp['ssd_conv_b'][l]
    cw_cols = np.concatenate([lcw, scw[:, gr], scw[:, bsl], scw[:, csl]], axis=1)
    cb_cols = np.concatenate([inp['lru_conv_b'][l][lr], scb[gr], scb[bsl], scb[csl]])
    cw = cw_cols.reshape(4, 8, 128).transpose(2, 1, 0).reshape(128, 32)
    prm = np.concatenate([
        cw, pcol(cb_cols), pcol(inp['lru_b_a'][l][lr]), pcol(inp['lru_b_i'][l][lr]), pcol(inp['lru_lambda'][l][lr]),
        rep(inp['ssd_dt_bias'][l][hr]), rep(inp['ssd_a_log'][l][hr]), rep(inp['ssd_d'][l][hr]),
        rep(inp['ssd_norm_w'][l][gr]),
    ], axis=1).astype(np.float32)
    wa = inp['lru_w_a'][l][2 * j:2 * j + 2]
    wi = inp['lru_w_i'][l][2 * j:2 * j + 2]
    wai = np.concatenate([wa, wi], axis=0).transpose(1, 0, 2).reshape(128, 512)
    return {f"wR{l}": np.ascontiguousarray(wR, dtype=np.float32), f"wai{l}": np.ascontiguousarray(wai, dtype=np.float32),
            f"prm{l}": np.ascontiguousarray(prm)}


def t_inputs(inp, l):
    w_in = inp['w_in'][l]
    bg = inp['b_gate'][l]
    prmT = bg.reshape(3, 8, 128).transpose(2, 0, 1).reshape(128, 24)
    lnp = np.concatenate([rep(inp['ln1_g'][l]), rep(inp['ln1_b'][l]), rep(inp['ln2_g'][l]), rep(inp['ln2_b'][l])], axis=1)
    c = np.ascontiguousarray
    return {
        f"wq{l}": c(w_in[:, OFF['q']:OFF['q'] + 1024]), f"wg{l}": c(w_in[:, OFF['g']:OFF['g'] + 3072]),
        f"wkv{l}": c(inp['mem_w_kv'][l]), f"wbl{l}": c(inp['w_br_lru'][l]), f"wbs{l}": c(inp['w_br_ssd'][l]),
        f"wbx{l}": c(inp['w_br_xa'][l]), f"wo{l}": c(inp['w_out'][l]), f"wfi{l}": c(inp['ffn_w_in'][l]),
        f"wfd{l}": c(inp['ffn_w_down'][l]), f"prmT{l}": c(prmT.astype(np.float32)), f"lnp{l}": c(lnp),
    }


def kernel(**inputs):
    inp = {k: np.asarray(v, dtype=np.float32) for k, v in inputs.items()}
    x = inp['x']
    nc = get_prog()
    xTs = [np.ascontiguousarray(x[b].T) for b in range(2)]
    memTs = [np.ascontiguousarray(inp['mem'][b].T) for b in range(2)]
    tw = {}
    for l in range(2):
        tw.update(t_inputs(inp, l))
    rw = [{} for _ in range(4)]
    for j in range(4):
        for l in range(2):
            rw[j].update(r_inputs(inp, l, j))
    in_maps = []
    for c in range(8):
        b, q = c // 4, c % 4
        m = {"xT0": xTs[b], "xTt": np.ascontiguousarray(xTs[b][:, q * NTOK:(q + 1) * NTOK]),
             "xtok": np.ascontiguousarray(x[b, q * NTOK:(q + 1) * NTOK]), "memT": memTs[b], "cst": _CST}
        m.update(tw)
        m.update(rw[q])
        in_maps.append(m)
    res = run_bass_kernel_spmd(nc, in_maps, core_ids=list(range(8)))
    out = np.empty((2, SEQ, D), np.float32)
    for c in range(8):
        b, q = c // 4, c % 4
        out[b, q * NTOK:(q + 1) * NTOK] = np.asarray(res.results[c]["out"])
    return out
```

```python
import numpy as np
from contextlib import ExitStack
import ml_dtypes
import concourse.bass as bass
import concourse.mybir as mybir
from concourse.bass_utils import run_bass_kernel_spmd

F32 = mybir.dt.float32
BF16 = mybir.dt.bfloat16
AF = mybir.ActivationFunctionType
ALU = mybir.AluOpType
AX = mybir.AxisListType

D = 1024
SEQ = 8192
NMEM = 256
DFF = 2816
ALPHA = 4.0 ** 0.25
EPS = 1e-5
TT = 512
SAME_ENG_SYNC = True
ARENA_KB = 196
GROUPS = [[0, 1, 2, 3], [4, 5, 6, 7]]
BLK = {'pe': 'tensor', 'act': 'scalar', 'dve': 'vector', 'pool': 'gpsimd', 'sp': 'sync'}


class Buf:
    __slots__ = ("w", "r", "wsig")

    def __init__(self):
        self.w = []
        self.r = []
        self.wsig = None


def apsig(v):
    a = v.ap
    n = 1
    for d in a.shape[1:]:
        n *= d
    return (n, tuple(map(tuple, a.ap)), a.offset, str(a.dtype))


class V:
    __slots__ = ("ap", "tile", "key")

    def __init__(self, ap, tile, key):
        self.ap = ap
        self.tile = tile
        self.key = key

    def __getitem__(self, idx):
        return V(self.ap[idx], self.tile, self.key)

    def rr(self, pat, **kw):
        return V(self.ap.rearrange(pat, **kw), self.tile, self.key)

    def bc(self, shape):
        return V(self.ap.to_broadcast(list(shape)), self.tile, self.key)

    def unsq(self, ax):
        return V(self.ap.unsqueeze(ax), self.tile, self.key)

    def bitcast(self, dt):
        return V(self.ap.bitcast(dt), self.tile, self.key)


class TK:
    def __init__(self, tile, key):
        self.tile = tile
        self.key = key

    def __getitem__(self, idx):
        return V(self.tile.base[idx], self.tile, self.key)


class Tile:
    def __init__(self, base, is_psum=False):
        self.base = base
        self.whole = Buf()
        self.kids = {}
        self.is_psum = is_psum

    def __getitem__(self, idx):
        return V(self.base[idx], self, None)

    def k(self, key):
        return TK(self, key)

    def chk(self, key):
        if key is None:
            return [self.whole] + list(self.kids.values())
        if key not in self.kids:
            self.kids[key] = Buf()
        return [self.whole, self.kids[key]]

    def upd(self, key):
        if key is None:
            return self.whole
        if key not in self.kids:
            self.kids[key] = Buf()
        return self.kids[key]


class Op:
    __slots__ = ("eng", "fn", "deps", "marked", "dma", "sem", "val", "inc")


class Prog:
    def __init__(self, nc, es):
        self.nc = nc
        self.es = es
        self.ops = {e: [] for e in BLK}
        self.esem = {e: es.enter_context(nc.semaphore(f"s_{e}")) for e in ('pe', 'act', 'dve', 'pool')}
        self.dsem = {e: [es.enter_context(nc.semaphore(f"d_{e}{i}")) for i in range(8)] for e in ('sp', 'pool', 'act')}
        self.dcnt = {e: 0 for e in self.dsem}
        self.nt = 0
        self.psb = []
        for i in range(8):
            t = es.enter_context(nc.psum_tensor(f"psb{i}", [128, 512], F32))
            self.psb.append(Tile(t, is_psum=True))
        self.psi = 0
        self.pcnt = {}
        self.out_dmas = []
        self.phase_dmas = []
        self.cache = {}
        self.arena = es.enter_context(nc.sbuf_tensor("arena", [128, ARENA_KB * 512], BF16))
        self.aoff = 0

    def sb(self, shape, dt, name=None):
        n = 1
        for d in shape[1:]:
            n *= d
        nb = n * (4 if dt == F32 else 2)
        nb = (nb + 63) // 64 * 64
        off = self.aoff
        self.aoff += nb
        assert self.aoff <= ARENA_KB * 1024, f"arena overflow {self.aoff}"
        ap = self.arena[:, off // 2:(off + nb) // 2]
        if dt == F32:
            ap = ap.bitcast(F32)
        ap = ap[:, 0:n]
        if len(shape) == 3:
            ap = ap.rearrange("p (a b) -> p a b", a=shape[1])
        elif len(shape) == 4:
            ap = ap.rearrange("p (a b c) -> p a b c", a=shape[1], b=shape[2])
        return Tile(ap)

    def barrier(self):
        lasts = []
        for eng in BLK:
            for op in reversed(self.ops[eng]):
                if op.fn is not None and not op.dma:
                    op.marked = True
                    lasts.append(op)
                    break
        dmas = list(self.phase_dmas)
        self.phase_dmas = []
        for eng in BLK:
            w = Op()
            w.eng = eng
            w.fn = None
            w.marked = False
            w.dma = False
            w.sem = None
            w.val = 0
            w.deps = list(lasts) + dmas
            self.ops[eng].append(w)
        self.aoff = 0

    def allgather(self, out_v, in_v):
        self.nt += 1
        sem = self.es.enter_context(self.nc.semaphore(f"cc{self.nt}"))
        out_ap, in_ap = out_v.ap.opt(), in_v.ap.opt()
        op = self._op('pool', lambda e: e.collective_compute("AllGather", ALU.bypass, replica_groups=GROUPS, ins=[in_ap], outs=[out_ap]),
                      [out_v], [in_v], dma=True)
        self.dcnt['pool'] -= 1
        op.sem = sem
        op.val = 1
        op.inc = 1
        return op

    def dma_fn(self, q, out, in_, fn):
        return self._op(q, fn, [out], [in_], dma=True)

    def ps(self, pool=None):
        if pool is None:
            t = self.psb[self.psi % 8]
            self.psi += 1
            return t
        key = tuple(pool)
        n = self.pcnt.get(key, 0)
        self.pcnt[key] = n + 1
        return self.psb[pool[n % len(pool)]]

    def dram(self, name, shape, dt, kind=None):
        if kind is None:
            t = self.nc.dram_tensor(name, list(shape), dt)
        else:
            t = self.nc.dram_tensor(name, list(shape), dt, kind=kind)
        return Tile(t.ap())

    def _op(self, eng, fn, outs, ins, dma=False):
        op = Op()
        op.eng = eng
        op.fn = fn
        op.marked = False
        op.dma = dma
        op.sem = None
        op.val = 0
        op.inc = 16
        deps = set()
        raw_hard = set()
        for v in ins:
            sig = apsig(v)
            for b in v.tile.chk(v.key):
                deps.update(b.w)
                for p in b.w:
                    if p.eng == eng and not p.dma and not (b.wsig is not None and b.wsig == sig and sig[0] >= 256):
                        raw_hard.add(p)
                if v.tile.is_psum:
                    deps.update(o for o in b.r if o.eng != eng)
        for v in outs:
            for b in v.tile.chk(v.key):
                deps.update(b.w)
                deps.update(b.r)
        deps.discard(op)
        keep = []
        for p in deps:
            if p.dma:
                keep.append(p)
            elif p.eng == eng:
                if eng == 'pe':
                    continue
                if dma or (SAME_ENG_SYNC and p in raw_hard):
                    p.marked = True
                    keep.append(p)
            else:
                p.marked = True
                keep.append(p)
        op.deps = keep
        for v in ins:
            b = v.tile.upd(v.key)
            if not dma:
                b.r = [o for o in b.r if o.dma or o.eng != eng]
            b.r.append(op)
        for v in outs:
            b = v.tile.upd(v.key)
            b.w = [op]
            b.r = []
            b.wsig = apsig(v)
        if dma:
            n = self.dcnt[eng]
            self.dcnt[eng] = n + 1
            op.sem = self.dsem[eng][n % 8]
            op.val = 16 * (n // 8 + 1)
            self.phase_dmas.append(op)
        self.ops[eng].append(op)
        return op

    def mm(self, out, lhsT, rhs, start=True, stop=True):
        self._op('pe', lambda e: e.matmul(out.ap, lhsT.ap, rhs.ap, start=start, stop=stop), [out], [lhsT, rhs])

    def tr(self, out, in_, ident):
        self._op('pe', lambda e: e.transpose(out.ap, in_.ap, ident.ap), [out], [in_, ident])

    def act(self, out, in_, func, bias=None, scale=None):
        ins = [in_]
        kw = {}
        if bias is not None:
            if isinstance(bias, V):
                ins.append(bias)
                kw['bias'] = bias.ap
            else:
                kw['bias'] = float(bias)
        if scale is not None:
            if isinstance(scale, V):
                ins.append(scale)
                kw['scale'] = scale.ap
            else:
                kw['scale'] = float(scale)
        self._op('act', lambda e: e.activation(out.ap, in_.ap, func, **kw), [out], ins)

    def copy(self, out, in_, eng='act'):
        if eng == 'act':
            self._op('act', lambda e: e.copy(out.ap, in_.ap), [out], [in_])
        else:
            self._op(eng, lambda e: e.tensor_copy(out.ap, in_.ap), [out], [in_])

    def tt(self, out, in0, in1, op, eng='dve'):
        self._op(eng, lambda e: e.tensor_tensor(out.ap, in0.ap, in1.ap, op), [out], [in0, in1])

    def ts(self, out, in0, s1, s2, op0, op1=None, eng='dve'):
        ins = [in0]
        a1 = s1
        a2 = s2
        if isinstance(s1, V):
            ins.append(s1)
            a1 = s1.ap
        if isinstance(s2, V):
            ins.append(s2)
            a2 = s2.ap
        if op1 is None:
            self._op(eng, lambda e: e.tensor_scalar(out.ap, in0.ap, a1, a2, op0), [out], ins)
        else:
            self._op(eng, lambda e: e.tensor_scalar(out.ap, in0.ap, a1, a2, op0, op1), [out], ins)

    def stt(self, out, in0, scalar, in1, op0, op1):
        ins = [in0, in1]
        a = scalar
        if isinstance(scalar, V):
            ins.append(scalar)
            a = scalar.ap
        self._op('dve', lambda e: e.scalar_tensor_tensor(out.ap, in0.ap, a, in1.ap, op0, op1), [out], ins)

    def scan(self, out, d0, d1, init, op0, op1):
        ins = [d0, d1]
        a = init
        if isinstance(init, V):
            ins.append(init)
            a = init.ap
        self._op('dve', lambda e: e.tensor_tensor_scan(out.ap, d0.ap, d1.ap, a, op0, op1), [out], ins)

    def red(self, out, in_, op):
        self._op('dve', lambda e: e.tensor_reduce(out.ap, in_.ap, AX.X, op), [out], [in_])

    def recip(self, out, in_):
        self._op('dve', lambda e: e.reciprocal(out.ap, in_.ap), [out], [in_])

    def bnstats(self, out, in_):
        self._op('dve', lambda e: e.bn_stats(out.ap, in_.ap), [out], [in_])

    def bnaggr(self, out, in_):
        self._op('dve', lambda e: e.bn_aggr(out.ap, in_.ap), [out], [in_])

    def memset(self, out, val, eng='dve'):
        self._op(eng, lambda e: e.memset(out.ap, val), [out], [])

    def dma(self, q, out, in_, is_out=False):
        op = self._op(q, lambda e: e.dma_start(out=out.ap, in_=in_.ap), [out], [in_], dma=True)
        if is_out:
            self.out_dmas.append(op)
        return op

    def run(self):
        fin = Op()
        fin.eng = 'sp'
        fin.fn = None
        fin.marked = False
        fin.dma = False
        fin.deps = list(self.out_dmas)
        fin.sem = None
        fin.val = 0
        self.ops['sp'].append(fin)
        for eng, lst in self.ops.items():
            c = 0
            for op in lst:
                if op.dma:
                    continue
                if op.marked:
                    c += 1
                    op.val = c
                    op.sem = self.esem[eng]
        with self.nc.Block() as block:
            for eng in BLK:
                if not self.ops[eng]:
                    continue

                def body(e, eng=eng):
                    seen = {}
                    for op in self.ops[eng]:
                        need = {}
                        for p in op.deps:
                            key = id(p.sem)
                            if seen.get(key, 0) >= p.val:
                                continue
                            if key not in need or need[key][1] < p.val:
                                need[key] = (p.sem, p.val)
                        for key, (sem_, val_) in need.items():
                            e.wait_ge(sem_, val_)
                            seen[key] = val_
                        if op.fn is None:
                            continue
                        ins = op.fn(e)
                        if op.dma:
                            ins.then_inc(op.sem, op.inc)
                        elif op.marked:
                            ins.then_inc(op.sem, 1)

                getattr(block, BLK[eng])(body)


def make_consts():
    j = np.arange(128)[:, None]
    l = np.arange(128)[None, :]
    same = (j // 64) == (l // 64)
    c = np.zeros((128, 8, 128), np.float32)
    c[:, 0] = (j == l)
    c[:, 1] = same & (j <= l)
    c[:, 2] = same
    c[:, 3] = (j < 64)
    c[:, 4] = (j >= 64)
    c[:, 5] = same & (j > l)
    c[:, 6] = (j <= l)
    c[:, 7] = same & (l >= j)
    return np.ascontiguousarray(c.reshape(128, 1024))


def softplus(P, out, x, tmp, neg=False):
    ax, y, w, q = tmp
    P.ts(ax, x, -1.0, None, ALU.mult)
    P.tt(ax, ax, x, ALU.max)
    P.act(y, ax, AF.Exp, scale=-1.0)
    P.ts(w, y, 2.0, None, ALU.add)
    P.recip(w, w)
    P.tt(w, w, y, ALU.mult)
    P.tt(y, w, w, ALU.mult)
    P.ts(q, y, 1.0 / 13.0, None, ALU.mult)
    for c in (1.0 / 11.0, 1.0 / 9.0, 1.0 / 7.0, 1.0 / 5.0, 1.0 / 3.0):
        P.stt(q, q, c, y, ALU.add, ALU.mult)
    P.stt(q, q, 1.0, w, ALU.add, ALU.mult)
    P.ts(ax, x, (-1.0 if neg else 1.0), 0.0, ALU.mult, ALU.max)
    P.stt(out, q, 2.0, ax, ALU.mult, ALU.add)

C_LRUX, C_LRUG, C_XS, C_B, C_C, C_Z, C_DT = 0, 256, 512, 1024, 1152, 1280, 1792
NWR = 1800
P_CW, P_CB, P_BA, P_BI, P_LAM, P_DTB, P_ALOG, P_DSK, P_NW = 0, 32, 40, 42, 44, 46, 54, 62, 70
NPR = 70 + 512


def emit_R(P, io, ntiles=SEQ // TT):
    wR_d, wai_d, prm_d, cst_d, yT_d = io['wR'], io['wai'], io['prm'], io['cst'], io['yT']
    prm = P.sb([128, NPR], F32)
    cst = P.sb([128, 8, 128], F32)
    identb = P.sb([128, 128], BF16)
    wR = P.sb([128, 8, NWR], BF16)
    wai = P.sb([128, 4, 128], BF16)
    P.dma('sp', prm[:, :], prm_d[:, :])
    P.dma('sp', cst[:, :, :], cst_d[:, :].rr("p (a b) -> p a b", a=8))
    P.dma('pool', wai[:, :, :], wai_d[:, :].rr("p (a b) -> p a b", a=4))
    wRv = wR_d[:, :].rr("(kc p) n -> p kc n", p=128)
    for kc in range(8):
        P.dma('pool', wR[:, kc, :], wRv[:, kc, :])
    P.copy(identb[:, :], cst[:, 0, :])
    TriBD, OnesBD, Half0, Half1, GtBD, mask2, maskCB = (cst[:, i, :] for i in range(1, 8))

    clru = P.sb([128, 2], F32)
    spw = P.sb([128, 8], F32)
    softplus(P, clru[:, :], prm[:, P_LAM:P_LAM + 2], [spw[:, 2 * i:2 * i + 2] for i in range(4)], neg=True)
    P.ts(clru[:, :], clru[:, :], -8.0, None, ALU.mult)
    Abc = P.sb([128, 8], F32)
    P.act(Abc[:, :], prm[:, P_ALOG:P_ALOG + 8], AF.Exp)
    P.ts(Abc[:, :], Abc[:, :], -1.0, None, ALU.mult)

    cbuf = [P.sb([128, 4 + TT], BF16) for _ in range(8)]
    for c in range(8):
        P.memset(cbuf[c][:, 0:3], 0.0)
    dg = P.sb([128, 32, 128], BF16)
    for c in range(8):
        for j in range(4):
            P.ts(dg[:, c * 4 + j, :], cst[:, 0, :], prm[:, P_CW + c * 4 + j:P_CW + c * 4 + j + 1], None, ALU.mult)
    hst = P.sb([128, 2], F32)
    P.memset(hst[:, :], 0.0)
    state = P.sb([128, 8, 64], F32)
    P.memset(state[:, :, :], 0.0)
    CTp = [P.sb([128, 4, 2, 128], BF16) for _ in range(2)]
    for t in CTp:
        P.memset(t[:, :, :, :], 0.0)

    xTs = [P.sb([128, 8, TT], BF16) for _ in range(2)]
    accs = [P.sb([128, TT], F32) for _ in range(2)]
    xc = [P.sb([128, TT], F32) for _ in range(2)]
    xcb = [P.sb([128, TT], BF16) for _ in range(2)]
    xsT = [P.sb([128, 4, TT], BF16) for _ in range(2)]
    BT = [P.sb([128, TT], BF16) for _ in range(2)]
    CT = [P.sb([128, TT], BF16) for _ in range(2)]
    lr = [P.sb([128, TT], F32) for _ in range(6)]
    ylT = [P.sb([128, 2, TT], BF16) for _ in range(2)]
    ysT = [P.sb([128, 4, TT], BF16) for _ in range(2)]
    zs = [P.sb([128, 512], F32) for _ in range(2)]
    sm = [P.sb([128, 96], F32) for _ in range(2)]
    Rm = [P.sb([128, 8, 128], F32) for _ in range(2)]
    dec = [P.sb([128, 8, 128], F32) for _ in range(2)]
    CBm = [P.sb([128, 128], F32) for _ in range(2)]
    MT = [P.sb([128, 8, 128], BF16) for _ in range(2)]
    xs_tok = [P.sb([128, 8, 64], BF16) for _ in range(2)]
    B_tok = [P.sb([128, 128], BF16) for _ in range(2)]
    xdt = [P.sb([128, 8, 64], BF16) for _ in range(2)]
    xdec = [P.sb([128, 8, 64], BF16) for _ in range(2)]
    prevb = [P.sb([128, 512], BF16) for _ in range(4)]
    yb = [P.sb([128, 8, 64], F32) for _ in range(2)]
    tmpx = [P.sb([128, 8, 64], F32) for _ in range(2)]
    y3 = [P.sb([128, 512], BF16) for _ in range(2)]
    dtall = [P.sb([128, 32], F32) for _ in range(2)]
    dtw = [P.sb([128, 32], F32) for _ in range(5)]

    ccol = [C_LRUX, C_LRUX + 128, C_XS, C_XS + 128, C_XS + 256, C_XS + 384, C_B, C_C]

    PX, PY = [6, 7], [[0, 1, 2], [3, 4, 5]]

    def gen_X(ti):
        xT = xTs[ti % 2]
        io['load_xT'](ti, xT)
        xsTt, BTt, CTt, CTpt = xsT[ti % 2], BT[ti % 2], CT[ti % 2], CTp[ti % 2]

        def fm_chunk(col):
            ps = P.ps(PX)
            for kc in range(8):
                P.mm(ps[:, :], wR[:, kc, col:col + 128], xT[:, kc, :], start=(kc == 0), stop=(kc == 7))
            return ps

        for c in range(8):
            ps = fm_chunk(ccol[c])
            cb = cbuf[c]
            P.copy(cb[:, 3:3 + TT], ps[:, :])
            yield
            acc = P.ps(PX)
            for j in range(4):
                P.mm(acc[:, :], dg[:, c * 4 + j, :], cb[:, j:j + TT], start=(j == 0), stop=(j == 3))
            P.copy(cb[:, 0:3], cb[:, TT:TT + 3], eng='dve')
            yield
            bias = prm[:, P_CB + c:P_CB + c + 1]
            if c < 2:
                P.act(xc[c][:, :], acc[:, :], AF.Identity, bias=bias)
                P.act(xcb[c][:, :], acc[:, :], AF.Identity, bias=bias)
            elif c < 6:
                P.act(xsTt[:, c - 2, :], acc[:, :], AF.Silu, bias=bias)
            elif c == 6:
                P.act(BTt[:, :], acc[:, :], AF.Silu, bias=bias)
            else:
                P.act(CTt[:, :], acc[:, :], AF.Silu, bias=bias)
                a4 = acc[:, :].rr("p (s h t) -> p s h t", s=4, h=2)
                for h in range(2):
                    P.act(CTpt[:, :, h, h * 64:(h + 1) * 64], a4[:, :, h, :], AF.Silu, bias=bias)
            yield
        psd = P.ps(PX)
        for sub in range(4):
            for kc in range(8):
                P.mm(psd[:, sub * 8:(sub + 1) * 8], xT[:, kc, sub * 128:(sub + 1) * 128], wR[:, kc, C_DT:C_DT + 8], start=(kc == 0), stop=(kc == 7))
        dtx = dtw[0][:, :]
        P.tt(dtx.rr("p (s k) -> p s k", s=4), psd[:, 0:32].rr("p (s k) -> p s k", s=4), prm[:, P_DTB:P_DTB + 8].unsq(1).bc([128, 4, 8]), ALU.add)
        yield
        softplus(P, dtall[ti % 2][:, :], dtx, [t[:, :] for t in dtw[1:5]])
        yield
        ylTt = ylT[ti % 2]
        for n in range(2):
            r_, i_, a_, s_, u_, h_ = lr
            ps = P.ps(PX)
            P.mm(ps[:, :], wai[:, n, :], xcb[n][:, :])
            P.act(r_[:, :], ps[:, :], AF.Sigmoid, bias=prm[:, P_BA + n:P_BA + n + 1])
            ps = P.ps(PX)
            P.mm(ps[:, :], wai[:, 2 + n, :], xcb[n][:, :])
            P.act(i_[:, :], ps[:, :], AF.Sigmoid, bias=prm[:, P_BI + n:P_BI + n + 1])
            yield
            P.act(a_[:, :], r_[:, :], AF.Exp, scale=clru[:, n:n + 1])
            P.tt(s_[:, :], a_[:, :], a_[:, :], ALU.mult)
            P.act(s_[:, :], s_[:, :], AF.Sqrt, bias=1.0, scale=-1.0)
            yield
            P.tt(u_[:, :], i_[:, :], xc[n][:, :], ALU.mult)
            P.tt(u_[:, :], u_[:, :], s_[:, :], ALU.mult)
            yield
            P.scan(h_[:, :], a_[:, :], u_[:, :], hst[:, n:n + 1], ALU.mult, ALU.add)
            P.copy(hst[:, n:n + 1], h_[:, TT - 1:TT], eng='dve')
            yield
            psg = fm_chunk(C_LRUG + n * 128)
            P.act(r_[:, :], psg[:, :], AF.Square)
            P.ts(r_[:, :], r_[:, :], 0.044715, 1.0, ALU.mult, ALU.add)
            yield
            P.tt(r_[:, :], r_[:, :], psg[:, :], ALU.mult)
            P.act(r_[:, :], r_[:, :], AF.Sigmoid, scale=1.5957691216057308)
            yield
            P.tt(r_[:, :], r_[:, :], psg[:, :], ALU.mult)
            P.tt(ylTt[:, n, :], r_[:, :], h_[:, :], ALU.mult)
            yield
        yt2 = yT_d.k(ti)[ti, :].rr("(f t) -> f t", t=TT)
        P.dma('sp', yt2[0:256, :].rr("(c p) t -> p c t", p=128), ylTt[:, :, :])

    def gen_Y(ti, sub):
        xT = xTs[ti % 2]
        xsTt, BTt, CTt, CTpt = xsT[ti % 2], BT[ti % 2], CT[ti % 2], CTp[ti % 2]
        ysTt = ysT[ti % 2]
        q = sub % 2
        pool = PY[q]
        tok = slice(sub * 128, (sub + 1) * 128)
        smq = sm[q]
        da = smq[:, 24:32]
        cs_sb, expcs, dend, cdec = smq[:, 32:40], smq[:, 40:48], smq[:, 48:56], smq[:, 56:72]
        tmp8, ss, rstd = smq[:, 72:80], smq[:, 80:81], smq[:, 81:82]
        psz = P.ps(pool)
        for kc in range(8):
            P.mm(psz[:, :], xT[:, kc, tok], wR[:, kc, C_Z:C_Z + 512], start=(kc == 0), stop=(kc == 7))
        P.act(zs[q][:, :], psz[:, :], AF.Silu)
        dt = dtall[ti % 2][:, sub * 8:(sub + 1) * 8]
        P.tt(da, dt, Abc[:, :], ALU.mult)
        yield
        pst = P.ps(pool)
        pstb = pst[:, :].bitcast(BF16)
        for c in range(4):
            P.tr(pstb[:, c * 128:(c + 1) * 128], xsTt[:, c, tok], identb[:, :])
        P.copy(xs_tok[q][:, :, :], pstb[:, 0:512].rr("p (k d) -> p k d", k=8))
        psb_ = P.ps(pool)
        psbb = psb_[:, :].bitcast(BF16)
        P.tr(psbb[:, 0:128], BTt[:, tok], identb[:, :])
        P.copy(B_tok[q][:, :], psbb[:, 0:128])
        yield
        psc = P.ps(pool)
        P.mm(psc[:, 0:8], TriBD, da)
        P.mm(psc[:, 8:16], OnesBD, da)
        P.mm(psc[:, 16:24], Half0, da)
        P.mm(psc[:, 24:32], Half1, da)
        P.copy(cs_sb, psc[:, 0:8])
        P.act(expcs, psc[:, 0:8], AF.Exp)
        yield
        P.tt(tmp8, psc[:, 8:16], cs_sb, ALU.subtract)
        P.act(dend, tmp8, AF.Exp)
        P.act(cdec, psc[:, 16:32], AF.Exp)
        yield
        P.tt(Rm[q][:, :, :], da.unsq(2).bc([128, 8, 128]), mask2.unsq(1).bc([128, 8, 128]), ALU.mult)
        yield
        Rf = Rm[q][:, :, :].rr("p k l -> p (k l)")
        psA = P.ps(pool)
        P.mm(psA[:, :], GtBD, Rf[:, 0:512])
        psB = P.ps(pool)
        P.mm(psB[:, :], GtBD, Rf[:, 512:1024])
        df = dec[q][:, :, :].rr("p k l -> p (k l)")
        P.act(df[:, 0:512], psA[:, :], AF.Exp)
        P.act(df[:, 512:1024], psB[:, :], AF.Exp)
        yield
        psC = P.ps(pool)
        P.mm(psC[:, 0:128], BTt[:, tok], CTt[:, tok])
        P.tt(CBm[q][:, :], psC[:, 0:128], maskCB, ALU.mult)
        yield
        P.tt(MT[q][:, :, :], dec[q][:, :, :], CBm[q][:, :].unsq(1).bc([128, 8, 128]), ALU.mult)
        yield
        P.tt(xdt[q][:, :, :], xs_tok[q][:, :, :], dt.unsq(2).bc([128, 8, 64]), ALU.mult)
        P.tt(xdec[q][:, :, :], xdt[q][:, :, :], dend.unsq(2).bc([128, 8, 64]), ALU.mult)
        yield
        psO = P.ps(pool)
        xdf = xdec[q][:, :, :].rr("p k d -> p (k d)")
        sf = state[:, :, :].rr("p k d -> p (k d)")
        for h in range(2):
            psS = P.ps(pool)
            P.mm(psS[:, :], B_tok[q][64 * h:64 * h + 64, :], xdf[64 * h:64 * h + 64, :])
            pv = prevb[(2 * q + h)]
            P.copy(pv[:, :], sf)
            P.tt(state[:, :, :], state[:, :, :], cdec[:, 8 * h:8 * h + 8].unsq(2).bc([128, 8, 64]), ALU.mult)
            P.tt(sf, sf, psS[:, :], ALU.add)
            P.mm(psO[:, :], CTpt[:, sub, h, :], pv[:, :], start=(h == 0), stop=(h == 1))
            yield
        y = yb[q]
        yf = y[:, :, :].rr("p k d -> p (k d)")
        P.tt(y[:, :, :], psO[:, :].rr("p (k d) -> p k d", k=8), expcs.unsq(2).bc([128, 8, 64]), ALU.mult)
        yield
        psY = P.ps(pool)
        for k in range(8):
            P.mm(psY[:, k * 64:(k + 1) * 64], MT[q][:, k, :], xdt[q][:, k, :])
        P.tt(yf, yf, psY[:, :], ALU.add)
        yield
        P.tt(tmpx[q][:, :, :], xs_tok[q][:, :, :], prm[:, P_DSK:P_DSK + 8].unsq(2).bc([128, 8, 64]), ALU.mult)
        P.tt(y[:, :, :], y[:, :, :], tmpx[q][:, :, :], ALU.add)
        yield
        P.tt(yf, yf, zs[q][:, :], ALU.mult)
        tf = tmpx[q][:, :, :].rr("p k d -> p (k d)")
        P.tt(tf, yf, yf, ALU.mult)
        yield
        P.red(ss, tf, ALU.add)
        P.ts(ss, ss, 1.0 / 512.0, EPS, ALU.mult, ALU.add)
        P.act(ss, ss, AF.Sqrt)
        P.recip(rstd, ss)
        yield
        P.stt(y3[q][:, :], yf, rstd, prm[:, P_NW:P_NW + 512], ALU.mult, ALU.mult)
        psT = P.ps(pool)
        psTb = psT[:, :].bitcast(BF16)
        for c in range(4):
            P.tr(psTb[:, c * 128:(c + 1) * 128], y3[q][:, c * 128:(c + 1) * 128], identb[:, :])
        P.copy(ysTt[:, :, tok], psTb[:, 0:512].rr("p (c t) -> p c t", c=4))

    def run_tile(ygens, bg):
        SKEW = 3
        active = []
        pending = list(ygens)
        bg_done = bg is None
        while pending or active:
            if pending and len(active) < 2 and (not active or active[-1][1] >= SKEW):
                active.append([pending.pop(0), 0])
            for a in list(active):
                try:
                    next(a[0])
                    a[1] += 1
                except StopIteration:
                    active.remove(a)
            if not bg_done:
                try:
                    next(bg)
                except StopIteration:
                    bg_done = True
        while not bg_done:
            try:
                next(bg)
            except StopIteration:
                bg_done = True

    run_tile([], gen_X(0))
    for ti in range(ntiles):
        bg = gen_X(ti + 1) if ti + 1 < ntiles else None
        run_tile([gen_Y(ti, sub) for sub in range(4)], bg)
        yt2 = yT_d.k(ti)[ti, :].rr("(f t) -> f t", t=TT)
        P.dma('sp', yt2[256:768, :].rr("(c p) t -> p c t", p=128), ysT[ti % 2][:, :, :])
        io['tile_done'](ti)


NTOK = 2048
P_BG = 0
NPT = 24


class WStream:
    def __init__(self, P, nslots, cache=None):
        self.P = P
        self.slots = [P.sb([128, 8, 512], BF16) for _ in range(nslots)]
        self.i = 0
        self.cache = cache
        self.ids = {}

    def load(self, wd, r0, nrows, c0, ncols=512, cacheable=True):
        P = self.P
        slot = self.slots[self.i % len(self.slots)]
        self.i += 1
        nkc = nrows // 128
        sv = slot[:, 0:nkc, 0:ncols]
        key = (id(wd), r0, c0)
        if self.cache is not None and cacheable and key in self.ids:
            pid = self.ids[key]
            cv = self.cache.k(pid)[pid, 0:128 * nkc * ncols].rr("(p k n) -> p k n", p=128, k=nkc)
            P.dma('sp', sv, cv)
            return slot
        src = wd[r0:r0 + nrows, c0:c0 + ncols].rr("(kc p) n -> p kc n", p=128)
        P.dma('pool', sv, src)
        if self.cache is not None and cacheable:
            pid = len(self.ids)
            self.ids[key] = pid
            cv = self.cache.k(pid)[pid, 0:128 * nkc * ncols].rr("(p k n) -> p k n", p=128, k=nkc)
            P.dma('sp', cv, sv)
        return slot


def emit_T(P, io, ntiles=NTOK // TT):
    ws = WStream(P, 4, io.get('wcache'))
    prmT = P.sb([128, NPT], F32)
    lnp = P.sb([128, 4, D], F32)
    identf = P.sb([128, 128], F32)
    identb = P.sb([128, 128], BF16)
    P.dma('sp', prmT[:, :], io['prmT'][:, :])
    P.dma('sp', lnp[:, :, :], io['lnp'][:, :].rr("p (a b) -> p a b", a=4))
    P.dma('sp', identf[:, :], io['cst'][:, 0:128])
    P.copy(identb[:, :], identf[:, :])

    memT = P.sb([128, 8, NMEM], BF16)
    P.dma('pool', memT[:, :, :], io['memT'][:, :].rr("(kc p) m -> p kc m", p=128))
    KT = P.sb([128, 8, NMEM], BF16)
    Vt = P.sb([128, 2, 1024], BF16)
    for ph in range(2):
        pan = ws.load(io['wkv'], 0, 1024, ph * 512, cacheable=False)
        for c in range(4):
            ps = P.ps()
            for kc in range(8):
                P.mm(ps[:, 0:NMEM], pan[:, kc, c * 128:(c + 1) * 128], memT[:, kc, :], start=(kc == 0), stop=(kc == 7))
            P.copy(KT.k(ph * 4 + c)[:, ph * 4 + c, :], ps[:, 0:NMEM])
    for ph in range(2):
        pan = ws.load(io['wkv'], 0, 1024, 1024 + ph * 512, cacheable=False)
        for mc in range(2):
            ps = P.ps()
            for kc in range(8):
                P.mm(ps[:, :], memT[:, kc, mc * 128:(mc + 1) * 128], pan[:, kc, :], start=(kc == 0), stop=(kc == 7))
            P.copy(Vt.k((mc, ph))[:, mc, ph * 512:(ph + 1) * 512], ps[:, :])

    xT = P.sb([128, 8, TT], BF16)
    xtok = P.sb([128, 4, D], F32)
    x1 = P.sb([128, 4, D], F32)
    qm = P.sb([128, 8, TT], BF16)
    x1T = P.sb([128, 8, TT], BF16)
    PT = [P.sb([128, 2, TT], BF16) for _ in range(2)]
    yxaT = P.sb([128, 8, TT], BF16)
    ybuf = P.sb([128, 16, TT], BF16)
    acc = P.sb([128, 8, TT], F32)
    hT = P.sb([128, 22, TT], BF16)
    gs = [P.sb([128, TT], F32) for _ in range(2)]
    es_ = [P.sb([128, NMEM], F32) for _ in range(2)]
    pn = [P.sb([128, NMEM], BF16) for _ in range(2)]
    sm = [P.sb([128, 8], F32) for _ in range(2)]
    st = [P.sb([128, 12], F32) for _ in range(2)]
    mv = [P.sb([128, 4], F32) for _ in range(2)]

    lncnt = [0]

    def layer_norm(v, g, b):
        i = lncnt[0] % 2
        lncnt[0] += 1
        P.bnstats(st[i][:, 0:6], v[:, 0:512])
        P.bnstats(st[i][:, 6:12], v[:, 512:1024])
        P.bnaggr(mv[i][:, 0:2], st[i][:, :])
        P.ts(mv[i][:, 2:3], mv[i][:, 1:2], EPS, None, ALU.add)
        P.act(mv[i][:, 2:3], mv[i][:, 2:3], AF.Sqrt)
        P.recip(mv[i][:, 3:4], mv[i][:, 2:3])
        P.ts(v, v, mv[i][:, 0:1], mv[i][:, 3:4], ALU.subtract, ALU.mult)
        P.tt(v, v, g, ALU.mult)
        P.tt(v, v, b, ALU.add)

    for ti in range(ntiles):
        t0 = ti * TT
        io['load_xT'](ti, xT)
        io['load_xtok'](ti, xtok)
        for ph in range(2):
            pan = ws.load(io['wq'], 0, 1024, ph * 512)
            for c in range(4):
                dc = ph * 4 + c
                ps = P.ps()
                for kc in range(8):
                    P.mm(ps[:, :], pan[:, kc, c * 128:(c + 1) * 128], xT[:, kc, :], start=(kc == 0), stop=(kc == 7))
                P.copy(qm.k(dc)[:, dc, :], ps[:, :])
        def gen_A(pool):
            cnt = 0
            for h in range(4):
                PTh = PT[h % 2]
                for s in range(4):
                    i = cnt % 2
                    cnt += 1
                    tok = slice(s * 128, (s + 1) * 128)
                    ps = P.ps(pool)
                    for dd in range(2):
                        P.mm(ps[:, 0:NMEM], qm.k(2 * h + dd)[:, 2 * h + dd, tok], KT.k(2 * h + dd)[:, 2 * h + dd, :], start=(dd == 0), stop=(dd == 1))
                    mx, nb, sm_, rs = sm[i][:, 0:1], sm[i][:, 1:2], sm[i][:, 2:3], sm[i][:, 3:4]
                    P.red(mx, ps[:, 0:NMEM], ALU.max)
                    P.ts(nb, mx, -1.0 / 16.0, None, ALU.mult)
                    yield
                    P.act(es_[i][:, :], ps[:, 0:NMEM], AF.Exp, bias=nb, scale=1.0 / 16.0)
                    P.red(sm_, es_[i][:, :], ALU.add)
                    P.recip(rs, sm_)
                    yield
                    P.ts(pn[i][:, :], es_[i][:, :], rs, None, ALU.mult)
                    pst = P.ps(pool)
                    pstb = pst[:, :].bitcast(BF16)
                    for mc in range(2):
                        P.tr(pstb[:, mc * 128:(mc + 1) * 128], pn[i][:, mc * 128:(mc + 1) * 128], identb[:, :])
                    P.copy(PTh[:, :, tok], pstb[:, 0:256].rr("p (m t) -> p m t", m=2))
                    yield
                for dd in range(2):
                    ps = P.ps(pool)
                    for mc in range(2):
                        P.mm(ps[:, :], Vt[:, mc, h * 256 + dd * 128:h * 256 + (dd + 1) * 128], PTh[:, mc, :], start=(mc == 0), stop=(mc == 1))
                    P.copy(yxaT.k(2 * h + dd)[:, 2 * h + dd, :], ps[:, :])
                yield

        def gen_B(bs, pool):
            for b in bs:
                wname, nkp = [('wbl', 1), ('wbs', 2), ('wbx', 1)][b]
                if b < 2:
                    io['load_y'](ti, b, ybuf)
                ysb = ybuf if b < 2 else yxaT
                for ph in range(2):
                    pans = [ws.load(io[wname], kp * 1024, 1024, ph * 512) for kp in range(nkp)]
                    gp = ws.load(io['wg'], 0, 1024, b * 1024 + ph * 512)
                    for c in range(4):
                        dc = ph * 4 + c
                        psp = P.ps(pool)
                        n = nkp * 8
                        i = 0
                        for kp in range(nkp):
                            for kc in range(8):
                                P.mm(psp[:, :], pans[kp][:, kc, c * 128:(c + 1) * 128], ysb[:, kp * 8 + kc, :], start=(i == 0), stop=(i == n - 1))
                                i += 1
                            yield
                        psg = P.ps(pool)
                        for kc in range(8):
                            P.mm(psg[:, :], gp[:, kc, c * 128:(c + 1) * 128], xT[:, kc, :], start=(kc == 0), stop=(kc == 7))
                        g = gs[dc % 2]
                        P.act(g[:, :], psg[:, :], AF.Sigmoid, bias=prmT[:, P_BG + b * 8 + dc:P_BG + b * 8 + dc + 1])
                        yield
                        if b == 0:
                            P.tt(acc.k(dc)[:, dc, :], g[:, :], psp[:, :], ALU.mult)
                        else:
                            P.tt(g[:, :], g[:, :], psp[:, :], ALU.mult)
                            if b == 1:
                                P.tt(acc.k(dc)[:, dc, :], acc.k(dc)[:, dc, :], g[:, :], ALU.add)
                            else:
                                P.tt(qm.k(dc)[:, dc, :], acc.k(dc)[:, dc, :], g[:, :], ALU.add)
                        yield

        for _ in gen_A(None):
            pass
        for _ in gen_B([0, 1, 2], None):
            pass
        wo_p = [ws.load(io['wo'], 0, 1024, ph * 512) for ph in range(2)]

        def wo_mm(s):
            tok = slice(s * 128, (s + 1) * 128)
            for ph in range(2):
                ps = P.ps()
                for kc in range(8):
                    P.mm(ps[:, :], qm[:, kc, tok], wo_p[ph][:, kc, :], start=(kc == 0), stop=(kc == 7))
                P.stt(x1.k(s)[:, s, ph * 512:(ph + 1) * 512], xtok.k(s)[:, s, ph * 512:(ph + 1) * 512], ALPHA, ps[:, :], ALU.mult, ALU.add)

        def ln1_tr(s):
            tok = slice(s * 128, (s + 1) * 128)
            layer_norm(x1.k(s)[:, s, :], lnp[:, 0, :], lnp[:, 1, :])
            for g4 in range(2):
                ps = P.ps()
                for j in range(4):
                    dc = g4 * 4 + j
                    P.tr(ps[:, j * 128:(j + 1) * 128], x1.k(s)[:, s, dc * 128:(dc + 1) * 128], identf[:, :])
                P.copy(x1T[:, g4 * 4:(g4 + 1) * 4, tok], ps[:, :].rr("p (j t) -> p j t", j=4))

        wo_mm(0)
        for s in range(1, 4):
            wo_mm(s)
            ln1_tr(s - 1)
        ln1_tr(3)
        for p in range(11):
            pan = ws.load(io['wfi'], 0, 1024, p * 512)
            for c in range(4):
                col = p * 512 + c * 128
                ps = P.ps()
                for kc in range(8):
                    P.mm(ps[:, :], pan[:, kc, c * 128:(c + 1) * 128], x1T[:, kc, :], start=(kc == 0), stop=(kc == 7))
                if col < DFF:
                    ffc = col // 128
                    P.act(hT.k(ffc)[:, ffc, :], ps[:, :], AF.Silu)
                else:
                    ffc = (col - DFF) // 128
                    P.tt(hT.k(ffc)[:, ffc, :], hT.k(ffc)[:, ffc, :], ps[:, :], ALU.mult)
        for ph in range(2):
            pss = [P.ps() for _ in range(4)]
            for (r0, nr) in [(0, 1024), (1024, 1024), (2048, 768)]:
                pan = ws.load(io['wfd'], r0, nr, ph * 512)
                for s in range(4):
                    tok = slice(s * 128, (s + 1) * 128)
                    for kc in range(nr // 128):
                        ffc = r0 // 128 + kc
                        P.mm(pss[s][:, :], hT.k(ffc)[:, ffc, tok], pan[:, kc, :], start=(ffc == 0), stop=(ffc == 21))
            for s in range(4):
                P.stt(xtok.k(s)[:, s, ph * 512:(ph + 1) * 512], x1.k(s)[:, s, ph * 512:(ph + 1) * 512], ALPHA, pss[s][:, :], ALU.mult, ALU.add)
        for s in range(4):
            layer_norm(xtok.k(s)[:, s, :], lnp[:, 2, :], lnp[:, 3, :])
        io['store_out'](ti, xtok)
        if io.get('store_xT') is not None:
            for s in range(4):
                tok = slice(s * 128, (s + 1) * 128)
                for g4 in range(2):
                    ps = P.ps()
                    for j in range(4):
                        dc = g4 * 4 + j
                        P.tr(ps[:, j * 128:(j + 1) * 128], xtok.k(s)[:, s, dc * 128:(dc + 1) * 128], identf[:, :])
                    P.copy(qm[:, g4 * 4:(g4 + 1) * 4, tok], ps[:, :].rr("p (j t) -> p j t", j=4))
            io['store_xT'](ti, qm)


R_W = {'wR': [D, NWR], 'wai': [128, 512], 'prm': [128, NPR]}
T_W = {'wq': [D, 1024], 'wg': [D, 3072], 'wkv': [D, 2048], 'wbl': [1024, D], 'wbs': [2048, D], 'wbx': [1024, D],
       'wo': [D, D], 'wfi': [D, 2 * DFF], 'wfd': [DFF, D], 'prmT': [128, NPT], 'lnp': [128, 4 * D]}


def build_fused(nc, es, stop=None):
    P = Prog(nc, es)

    def ext(n, sh):
        return P.dram(n, sh, F32, "ExternalInput")

    xT0 = ext("xT0", [D, SEQ])
    xTt = ext("xTt", [D, NTOK])
    xtok_d = ext("xtok", [NTOK, D])
    memT = ext("memT", [D, NMEM])
    cst = ext("cst", [128, 1024])
    out_d = P.dram("out", [NTOK, D], F32, "ExternalOutput")
    WR = [{k: ext(f"{k}{l}", sh) for k, sh in R_W.items()} for l in range(2)]
    WT = [{k: ext(f"{k}{l}", sh) for k, sh in T_W.items()} for l in range(2)]
    YT = 768 * TT
    y_loc = [P.dram(f"y_loc{l}", [16, YT], BF16) for l in range(2)]
    y_all = [P.dram(f"y_all{l}", [16, 4 * YT], BF16) for l in range(2)]
    y_mine = [P.dram(f"y_mine{l}", [4, 4 * YT], BF16, "ExternalOutput" if stop == f"R{l}" else None) for l in range(2)]
    x1_d = P.dram("x1_d", [NTOK, D], F32, "ExternalOutput" if stop in ("T0", "DBG") else None)
    wcache = [P.dram(f"wcache{l}", [36, 128 * 8 * 512], BF16) for l in range(2)]
    XT = 512 * TT
    x1T_loc = P.dram("x1T_loc", [8, XT], BF16)
    x1T_all = P.dram("x1T_all", [8, 4 * XT], BF16)

    def tokview(t):
        return t[:, :].rr("(s p) d -> p s d", p=128)

    for l in range(2):
        if l == 0:
            xv = xT0[:, :].rr("(kc p) t -> p kc t", p=128)

            def load_xT_R(ti, xT, xv=xv):
                P.dma('pool', xT[:, :, :], xv[:, :, ti * TT:(ti + 1) * TT])
        else:
            def load_xT_R(ti, xT):
                q, i = ti // 4, ti % 4
                for h in range(2):
                    src = x1T_all[i * 2 + h, q * XT:(q + 1) * XT].rr("(c p t) -> p c t", p=128, t=TT)
                    P.dma('pool', xT[:, 4 * h:4 * h + 4, :], src)
        yl, yal = y_loc[l], y_all[l]

        pend = []

        def tile_done(ti, yl=yl, yal=yal, pend=pend):
            while pend:
                pend.pop(0)()
            pend.append(lambda: P.allgather(yal.k(ti)[ti, :], yl.k(ti)[ti, :]))
        ioR = dict(WR[l])
        ioR.update(cst=cst, yT=y_loc[l], load_xT=load_xT_R, tile_done=tile_done)
        emit_R(P, ioR)
        while pend:
            pend.pop(0)()
        P.barrier()

        ya = y_all[l]

        ym = y_mine[l]

        def fn(e, ym=ym, ya=ya):
            q = e.partition_id() % 4
            yv = ya.base.rearrange("(q i) (r c) -> q (i r) c", q=4, c=16384)
            src = yv[bass.ds(q, 1), :, :].rearrange("o r c -> (o r) c")
            dst = ym.base.rearrange("i (r c) -> (i r) c", c=16384)
            return e.dma_start(out=dst, in_=src)
        op_ = P.dma_fn('pool', ym[:, :], ya[:, :], fn)
        if stop == f"R{l}":
            P.out_dmas.append(op_)
            break

        def load_y(ti, b, ybuf, ym=ym):
            r0, r1, ncc = (0, 256, 2) if b == 0 else (256, 768, 4)
            for j in range(4):
                src = ym[ti, j * YT:(j + 1) * YT].rr("(f t) -> f t", t=TT)[r0:r1, :].rr("(c p) t -> p c t", p=128)
                P.dma('sp', ybuf[:, j * ncc:(j + 1) * ncc, :], src)

        ioT = dict(WT[l])
        ioT.update(cst=cst, memT=memT, load_y=load_y, wcache=None)
        if l == 0:
            xtv = xTt[:, :].rr("(kc p) t -> p kc t", p=128)
            ioT['load_xT'] = lambda ti, xT, xtv=xtv: P.dma('pool', xT[:, :, :], xtv[:, :, ti * TT:(ti + 1) * TT])
            ioT['load_xtok'] = lambda ti, xt: P.dma('sp', xt[:, :, :], tokview(xtok_d)[:, ti * 4:(ti + 1) * 4, :])
            ioT['store_out'] = lambda ti, xt: P.dma('sp', tokview(x1_d)[:, ti * 4:(ti + 1) * 4, :], xt[:, :, :])
            pend2 = []

            def store_xT(ti, qm):
                while pend2:
                    pend2.pop(0)()
                for h in range(2):
                    n = ti * 2 + h
                    dst = x1T_loc.k(n)[n, :].rr("(c p t) -> p c t", p=128, t=TT)
                    P.dma('sp', dst, qm[:, 4 * h:4 * h + 4, :])
                    pend2.append(lambda n=n: P.allgather(x1T_all.k(n)[n, :], x1T_loc.k(n)[n, :]))
            ioT['store_xT'] = store_xT
        else:
            def load_xT_T(ti, xT):
                for h in range(2):
                    src = x1T_loc[ti * 2 + h, :].rr("(c p t) -> p c t", p=128, t=TT)
                    P.dma('pool', xT[:, 4 * h:4 * h + 4, :], src)
            ioT['load_xT'] = load_xT_T
            ioT['load_xtok'] = lambda ti, xt: P.dma('sp', xt[:, :, :], tokview(x1_d)[:, ti * 4:(ti + 1) * 4, :])
            ioT['store_out'] = lambda ti, xt: P.dma('sp', tokview(out_d)[:, ti * 4:(ti + 1) * 4, :], xt[:, :, :], is_out=True)
            ioT['store_xT'] = None
        emit_T(P, ioT)
        if l == 1 and stop == "DBG":
            dbg = P.dram("dbg_x1T", [8, XT], BF16, "ExternalOutput")
            P.dma('sp', dbg[:, :], x1T_loc[:, :], is_out=True)
        if l == 0:
            while pend2:
                pend2.pop(0)()
            P.barrier()
            if stop == "T0":
                break
    P.run()


OFF = {'lrux': 0, 'lrug': 1024, 'z': 2048, 'xbc': 4096, 'dt': 7168, 'q': 7200, 'g': 8224}
_CST = make_consts()
_prog = []


def get_prog():
    if not _prog:
        nc = bass.Bass("TRN2", target_bir_lowering=False)
        es = ExitStack()
        build_fused(nc, es)
        _prog.append((nc, es))
    return _prog[0][0]


def rep(v, n=128):
    v = np.asarray(v, np.float32)
    return np.ascontiguousarray(np.broadcast_to(v.reshape(1, -1), (n, v.size)))


def pcol(v):
    v = np.asarray(v, np.float32)
    return np.ascontiguousarray(v.reshape(-1, 128).T)


def r_inputs(inp, l, j):
    w_in = inp['w_in'][l]
    lr = slice(256 * j, 256 * (j + 1))
    gr = slice(512 * j, 512 * (j + 1))
    hr = slice(8 * j, 8 * (j + 1))
    xbc0 = OFF['xbc']
    bsl = slice(2048 + 128 * j, 2048 + 128 * (j + 1))
    csl = slice(2560 + 128 * j, 2560 + 128 * (j + 1))
    wR = np.concatenate([
        w_in[:, OFF['lrux'] + 256 * j:OFF['lrux'] + 256 * (j + 1)],
        w_in[:, OFF['lrug'] + 256 * j:OFF['lrug'] + 256 * (j + 1)],
        w_in[:, xbc0 + 512 * j:xbc0 + 512 * (j + 1)],
        w_in[:, xbc0 + bsl.start:xbc0 + bsl.stop],
        w_in[:, xbc0 + csl.start:xbc0 + csl.stop],
        w_in[:, OFF['z'] + 512 * j:OFF['z'] + 512 * (j + 1)],
        w_in[:, OFF['dt'] + 8 * j:OFF['dt'] + 8 * (j + 1)],
    ], axis=1)
    lcw = inp['lru_conv_w'][l][:, lr]
    scw = inp['ssd_conv_w'][l]
    scb = inp['ssd_conv_b'][l]
    cw_cols = np.concatenate([lcw, scw[:, gr], scw[:, bsl], scw[:, csl]], axis=1)
    cb_cols = np.concatenate([inp['lru_conv_b'][l][lr], scb[gr], scb[bsl], scb[csl]])
    cw = cw_cols.reshape(4, 8, 128).transpose(2, 1, 0).reshape(128, 32)
    prm = np.concatenate([
        cw, pcol(cb_cols), pcol(inp['lru_b_a'][l][lr]), pcol(inp['lru_b_i'][l][lr]), pcol(inp['lru_lambda'][l][lr]),
        rep(inp['ssd_dt_bias'][l][hr]), rep(inp['ssd_a_log'][l][hr]), rep(inp['ssd_d'][l][hr]),
        rep(inp['ssd_norm_w'][l][gr]),
    ], axis=1).astype(np.float32)
    wa = inp['lru_w_a'][l][2 * j:2 * j + 2]
    wi = inp['lru_w_i'][l][2 * j:2 * j + 2]
    wai = np.concatenate([wa, wi], axis=0).transpose(1, 0, 2).reshape(128, 512)
    return {f"wR{l}": np.ascontiguousarray(wR, dtype=np.float32), f"wai{l}": np.ascontiguousarray(wai, dtype=np.float32),
            f"prm{l}": np.ascontiguousarray(prm)}


def t_inputs(inp, l):
    w_in = inp['w_in'][l]
    bg = inp['b_gate'][l]
    prmT = bg.reshape(3, 8, 128).transpose(2, 0, 1).reshape(128, 24)
    lnp = np.concatenate([rep(inp['ln1_g'][l]), rep(inp['ln1_b'][l]), rep(inp['ln2_g'][l]), rep(inp['ln2_b'][l])], axis=1)
    c = np.ascontiguousarray
    return {
        f"wq{l}": c(w_in[:, OFF['q']:OFF['q'] + 1024]), f"wg{l}": c(w_in[:, OFF['g']:OFF['g'] + 3072]),
        f"wkv{l}": c(inp['mem_w_kv'][l]), f"wbl{l}": c(inp['w_br_lru'][l]), f"wbs{l}": c(inp['w_br_ssd'][l]),
        f"wbx{l}": c(inp['w_br_xa'][l]), f"wo{l}": c(inp['w_out'][l]), f"wfi{l}": c(inp['ffn_w_in'][l]),
        f"wfd{l}": c(inp['ffn_w_down'][l]), f"prmT{l}": c(prmT.astype(np.float32)), f"lnp{l}": c(lnp),
    }


def kernel(**inputs):
    inp = {k: np.asarray(v, dtype=np.float32) for k, v in inputs.items()}
    x = inp['x']
    nc = get_prog()
    xTs = [np.ascontiguousarray(x[b].T) for b in range(2)]
    memTs = [np.ascontiguousarray(inp['mem'][b].T) for b in range(2)]
    tw = {}
    for l in range(2):
        tw.update(t_inputs(inp, l))
    rw = [{} for _ in range(4)]
    for j in range(4):
        for l in range(2):
            rw[j].update(r_inputs(inp, l, j))
    in_maps = []
    for c in range(8):
        b, q = c // 4, c % 4
        m = {"xT0": xTs[b], "xTt": np.ascontiguousarray(xTs[b][:, q * NTOK:(q + 1) * NTOK]),
             "xtok": np.ascontiguousarray(x[b, q * NTOK:(q + 1) * NTOK]), "memT": memTs[b], "cst": _CST}
        m.update(tw)
        m.update(rw[q])
        in_maps.append(m)
    res = run_bass_kernel_spmd(nc, in_maps, core_ids=list(range(8)))
    out = np.empty((2, SEQ, D), np.float32)
    for c in range(8):
        b, q = c // 4, c % 4
        out[b, q * NTOK:(q + 1) * NTOK] = np.asarray(res.results[c]["out"])
    return out
```

```python
import numpy as np
from contextlib import ExitStack
import ml_dtypes
import concourse.bass as bass
import concourse.mybir as mybir
from concourse.bass_utils import run_bass_kernel_spmd

F32 = mybir.dt.float32
BF16 = mybir.dt.bfloat16
AF = mybir.ActivationFunctionType
ALU = mybir.AluOpType
AX = mybir.AxisListType

D = 1024
SEQ = 8192
NMEM = 256
DFF = 2816
ALPHA = 4.0 ** 0.25
EPS = 1e-5
TT = 512
SAME_ENG_SYNC = True
ARENA_KB = 196
GROUPS = [[0, 1, 2, 3], [4, 5, 6, 7]]
BLK = {'pe': 'tensor', 'act': 'scalar', 'dve': 'vector', 'pool': 'gpsimd', 'sp': 'sync'}


class Buf:
    __slots__ = ("w", "r", "wsig")

    def __init__(self):
        self.w = []
        self.r = []
        self.wsig = None


def apsig(v):
    a = v.ap
    n = 1
    for d in a.shape[1:]:
        n *= d
    try:
        a = a.opt()
    except Exception:
        pass
    return (n, tuple(map(tuple, a.ap)), a.offset, str(a.dtype))


class V:
    __slots__ = ("ap", "tile", "key")

    def __init__(self, ap, tile, key):
        self.ap = ap
        self.tile = tile
        self.key = key

    def __getitem__(self, idx):
        return V(self.ap[idx], self.tile, self.key)

    def rr(self, pat, **kw):
        return V(self.ap.rearrange(pat, **kw), self.tile, self.key)

    def bc(self, shape):
        return V(self.ap.to_broadcast(list(shape)), self.tile, self.key)

    def unsq(self, ax):
        return V(self.ap.unsqueeze(ax), self.tile, self.key)

    def bitcast(self, dt):
        return V(self.ap.bitcast(dt), self.tile, self.key)


class TK:
    def __init__(self, tile, key):
        self.tile = tile
        self.key = key

    def __getitem__(self, idx):
        return V(self.tile.base[idx], self.tile, self.key)


class Tile:
    def __init__(self, base, is_psum=False):
        self.base = base
        self.whole = Buf()
        self.kids = {}
        self.is_psum = is_psum

    def __getitem__(self, idx):
        return V(self.base[idx], self, None)

    def k(self, key):
        return TK(self, key)

    def chk(self, key):
        if key is None:
            return [self.whole] + list(self.kids.values())
        if key not in self.kids:
            self.kids[key] = Buf()
        return [self.whole, self.kids[key]]

    def upd(self, key):
        if key is None:
            return self.whole
        if key not in self.kids:
            self.kids[key] = Buf()
        return self.kids[key]


class Op:
    __slots__ = ("eng", "fn", "deps", "marked", "dma", "sem", "val", "inc")


class Prog:
    def __init__(self, nc, es):
        self.nc = nc
        self.es = es
        self.ops = {e: [] for e in BLK}
        self.esem = {e: es.enter_context(nc.semaphore(f"s_{e}")) for e in ('pe', 'act', 'dve', 'pool')}
        self.dsem = {e: [es.enter_context(nc.semaphore(f"d_{e}{i}")) for i in range(8)] for e in ('sp', 'pool', 'act')}
        self.dcnt = {e: 0 for e in self.dsem}
        self.nt = 0
        self.psb = []
        for i in range(8):
            t = es.enter_context(nc.psum_tensor(f"psb{i}", [128, 512], F32))
            self.psb.append(Tile(t, is_psum=True))
        self.psi = 0
        self.pcnt = {}
        self.out_dmas = []
        self.phase_dmas = []
        self.cache = {}
        self.arena = es.enter_context(nc.sbuf_tensor("arena", [128, ARENA_KB * 512], BF16))
        self.aoff = 0

    def sb(self, shape, dt, name=None):
        n = 1
        for d in shape[1:]:
            n *= d
        nb = n * (4 if dt == F32 else 2)
        nb = (nb + 63) // 64 * 64
        off = self.aoff
        self.aoff += nb
        assert self.aoff <= ARENA_KB * 1024, f"arena overflow {self.aoff}"
        ap = self.arena[:, off // 2:(off + nb) // 2]
        if dt == F32:
            ap = ap.bitcast(F32)
        ap = ap[:, 0:n]
        if len(shape) == 3:
            ap = ap.rearrange("p (a b) -> p a b", a=shape[1])
        elif len(shape) == 4:
            ap = ap.rearrange("p (a b c) -> p a b c", a=shape[1], b=shape[2])
        return Tile(ap)

    def barrier(self):
        lasts = []
        for eng in BLK:
            for op in reversed(self.ops[eng]):
                if op.fn is not None and not op.dma:
                    op.marked = True
                    lasts.append(op)
                    break
        dmas = list(self.phase_dmas)
        self.phase_dmas = []
        for eng in BLK:
            w = Op()
            w.eng = eng
            w.fn = None
            w.marked = False
            w.dma = False
            w.sem = None
            w.val = 0
            w.deps = list(lasts) + dmas
            self.ops[eng].append(w)
        self.aoff = 0

    def allgather(self, out_v, in_v):
        self.nt += 1
        sem = self.es.enter_context(self.nc.semaphore(f"cc{self.nt}"))
        out_ap, in_ap = out_v.ap.opt(), in_v.ap.opt()
        op = self._op('pool', lambda e: e.collective_compute("AllGather", ALU.bypass, replica_groups=GROUPS, ins=[in_ap], outs=[out_ap]),
                      [out_v], [in_v], dma=True)
        self.dcnt['pool'] -= 1
        op.sem = sem
        op.val = 1
        op.inc = 1
        return op

    def dma_fn(self, q, out, in_, fn):
        return self._op(q, fn, [out], [in_], dma=True)

    def ps(self, pool=None):
        if pool is None:
            t = self.psb[self.psi % 8]
            self.psi += 1
            return t
        key = tuple(pool)
        n = self.pcnt.get(key, 0)
        self.pcnt[key] = n + 1
        return self.psb[pool[n % len(pool)]]

    def dram(self, name, shape, dt, kind=None):
        if kind is None:
            t = self.nc.dram_tensor(name, list(shape), dt)
        else:
            t = self.nc.dram_tensor(name, list(shape), dt, kind=kind)
        return Tile(t.ap())

    def _op(self, eng, fn, outs, ins, dma=False):
        op = Op()
        op.eng = eng
        op.fn = fn
        op.marked = False
        op.dma = dma
        op.sem = None
        op.val = 0
        op.inc = 16
        deps = set()
        raw_hard = set()
        for v in ins:
            sig = apsig(v)
            for b in v.tile.chk(v.key):
                deps.update(b.w)
                for p in b.w:
                    if p.eng == eng and not p.dma and not (b.wsig is not None and b.wsig == sig and sig[0] >= 256):
                        raw_hard.add(p)
                if v.tile.is_psum:
                    deps.update(o for o in b.r if o.eng != eng)
        for v in outs:
            for b in v.tile.chk(v.key):
                deps.update(b.w)
                deps.update(b.r)
        deps.discard(op)
        keep = []
        for p in deps:
            if p.dma:
                keep.append(p)
            elif p.eng == eng:
                if eng == 'pe':
                    continue
                if dma or (SAME_ENG_SYNC and p in raw_hard):
                    p.marked = True
                    keep.append(p)
            else:
                p.marked = True
                keep.append(p)
        op.deps = keep
        for v in ins:
            b = v.tile.upd(v.key)
            if not dma:
                b.r = [o for o in b.r if o.dma or o.eng != eng]
            b.r.append(op)
        for v in outs:
            b = v.tile.upd(v.key)
            b.w = [op]
            b.r = []
            b.wsig = apsig(v)
        if dma:
            n = self.dcnt[eng]
            self.dcnt[eng] = n + 1
            op.sem = self.dsem[eng][n % 8]
            op.val = 16 * (n // 8 + 1)
            self.phase_dmas.append(op)
        self.ops[eng].append(op)
        return op

    def mm(self, out, lhsT, rhs, start=True, stop=True):
        self._op('pe', lambda e: e.matmul(out.ap, lhsT.ap, rhs.ap, start=start, stop=stop), [out], [lhsT, rhs])

    def tr(self, out, in_, ident):
        self._op('pe', lambda e: e.transpose(out.ap, in_.ap, ident.ap), [out], [in_, ident])

    def act(self, out, in_, func, bias=None, scale=None):
        ins = [in_]
        kw = {}
        if bias is not None:
            if isinstance(bias, V):
                ins.append(bias)
                kw['bias'] = bias.ap
            else:
                kw['bias'] = float(bias)
        if scale is not None:
            if isinstance(scale, V):
                ins.append(scale)
                kw['scale'] = scale.ap
            else:
                kw['scale'] = float(scale)
        self._op('act', lambda e: e.activation(out.ap, in_.ap, func, **kw), [out], ins)

    def copy(self, out, in_, eng='act'):
        if eng == 'act':
            self._op('act', lambda e: e.copy(out.ap, in_.ap), [out], [in_])
        else:
            self._op(eng, lambda e: e.tensor_copy(out.ap, in_.ap), [out], [in_])

    def tt(self, out, in0, in1, op, eng='dve'):
        self._op(eng, lambda e: e.tensor_tensor(out.ap, in0.ap, in1.ap, op), [out], [in0, in1])

    def ts(self, out, in0, s1, s2, op0, op1=None, eng='dve'):
        ins = [in0]
        a1 = s1
        a2 = s2
        if isinstance(s1, V):
            ins.append(s1)
            a1 = s1.ap
        if isinstance(s2, V):
            ins.append(s2)
            a2 = s2.ap
        if op1 is None:
            self._op(eng, lambda e: e.tensor_scalar(out.ap, in0.ap, a1, a2, op0), [out], ins)
        else:
            self._op(eng, lambda e: e.tensor_scalar(out.ap, in0.ap, a1, a2, op0, op1), [out], ins)

    def stt(self, out, in0, scalar, in1, op0, op1):
        ins = [in0, in1]
        a = scalar
        if isinstance(scalar, V):
            ins.append(scalar)
            a = scalar.ap
        self._op('dve', lambda e: e.scalar_tensor_tensor(out.ap, in0.ap, a, in1.ap, op0, op1), [out], ins)

    def scan(self, out, d0, d1, init, op0, op1):
        ins = [d0, d1]
        a = init
        if isinstance(init, V):
            ins.append(init)
            a = init.ap
        self._op('dve', lambda e: e.tensor_tensor_scan(out.ap, d0.ap, d1.ap, a, op0, op1), [out], ins)

    def red(self, out, in_, op):
        self._op('dve', lambda e: e.tensor_reduce(out.ap, in_.ap, AX.X, op), [out], [in_])

    def recip(self, out, in_):
        self._op('dve', lambda e: e.reciprocal(out.ap, in_.ap), [out], [in_])

    def bnstats(self, out, in_):
        self._op('dve', lambda e: e.bn_stats(out.ap, in_.ap), [out], [in_])

    def bnaggr(self, out, in_):
        self._op('dve', lambda e: e.bn_aggr(out.ap, in_.ap), [out], [in_])

    def memset(self, out, val, eng='dve'):
        self._op(eng, lambda e: e.memset(out.ap, val), [out], [])

    def dma(self, q, out, in_, is_out=False):
        op = self._op(q, lambda e: e.dma_start(out=out.ap, in_=in_.ap), [out], [in_], dma=True)
        if is_out:
            self.out_dmas.append(op)
        return op

    def run(self):
        fin = Op()
        fin.eng = 'sp'
        fin.fn = None
        fin.marked = False
        fin.dma = False
        fin.deps = list(self.out_dmas)
        fin.sem = None
        fin.val = 0
        self.ops['sp'].append(fin)
        for eng, lst in self.ops.items():
            c = 0
            for op in lst:
                if op.dma:
                    continue
                if op.marked:
                    c += 1
                    op.val = c
                    op.sem = self.esem[eng]
        with self.nc.Block() as block:
            for eng in BLK:
                if not self.ops[eng]:
                    continue

                def body(e, eng=eng):
                    seen = {}
                    for op in self.ops[eng]:
                        need = {}
                        for p in op.deps:
                            key = id(p.sem)
                            if seen.get(key, 0) >= p.val:
                                continue
                            if key not in need or need[key][1] < p.val:
                                need[key] = (p.sem, p.val)
                        for key, (sem_, val_) in need.items():
                            e.wait_ge(sem_, val_)
                            seen[key] = val_
                        if op.fn is None:
                            continue
                        ins = op.fn(e)
                        if op.dma:
                            ins.then_inc(op.sem, op.inc)
                        elif op.marked:
                            ins.then_inc(op.sem, 1)

                getattr(block, BLK[eng])(body)


def make_consts():
    j = np.arange(128)[:, None]
    l = np.arange(128)[None, :]
    same = (j // 64) == (l // 64)
    c = np.zeros((128, 8, 128), np.float32)
    c[:, 0] = (j == l)
    c[:, 1] = same & (j <= l)
    c[:, 2] = same
    c[:, 3] = (j < 64)
    c[:, 4] = (j >= 64)
    c[:, 5] = same & (j > l)
    c[:, 6] = (j <= l)
    c[:, 7] = same & (l >= j)
    return np.ascontiguousarray(c.reshape(128, 1024))


def softplus(P, out, x, tmp, neg=False):
    ax, y, w, q = tmp
    P.ts(ax, x, -1.0, None, ALU.mult)
    P.tt(ax, ax, x, ALU.max)
    P.act(y, ax, AF.Exp, scale=-1.0)
    P.ts(w, y, 2.0, None, ALU.add)
    P.recip(w, w)
    P.tt(w, w, y, ALU.mult)
    P.tt(y, w, w, ALU.mult)
    P.ts(q, y, 1.0 / 13.0, None, ALU.mult)
    for c in (1.0 / 11.0, 1.0 / 9.0, 1.0 / 7.0, 1.0 / 5.0, 1.0 / 3.0):
        P.stt(q, q, c, y, ALU.add, ALU.mult)
    P.stt(q, q, 1.0, w, ALU.add, ALU.mult)
    P.ts(ax, x, (-1.0 if neg else 1.0), 0.0, ALU.mult, ALU.max)
    P.stt(out, q, 2.0, ax, ALU.mult, ALU.add)

C_LRUX, C_LRUG, C_XS, C_B, C_C, C_Z, C_DT = 0, 256, 512, 1024, 1152, 1280, 1792
NWR = 1800
P_CW, P_CB, P_BA, P_BI, P_LAM, P_DTB, P_ALOG, P_DSK, P_NW = 0, 32, 40, 42, 44, 46, 54, 62, 70
NPR = 70 + 512


def emit_R(P, io, ntiles=SEQ // TT):
    wR_d, wai_d, prm_d, cst_d, yT_d = io['wR'], io['wai'], io['prm'], io['cst'], io['yT']
    prm = P.sb([128, NPR], F32)
    cst = P.sb([128, 8, 128], F32)
    identb = P.sb([128, 128], BF16)
    wR = P.sb([128, 8, NWR], BF16)
    wai = P.sb([128, 4, 128], BF16)
    P.dma('sp', prm[:, :], prm_d[:, :])
    P.dma('sp', cst[:, :, :], cst_d[:, :].rr("p (a b) -> p a b", a=8))
    P.dma('pool', wai[:, :, :], wai_d[:, :].rr("p (a b) -> p a b", a=4))
    wRv = wR_d[:, :].rr("(kc p) n -> p kc n", p=128)
    for kc in range(8):
        P.dma('pool', wR[:, kc, :], wRv[:, kc, :])
    P.copy(identb[:, :], cst[:, 0, :])
    TriBD, OnesBD, Half0, Half1, GtBD, mask2, maskCB = (cst[:, i, :] for i in range(1, 8))

    clru = P.sb([128, 2], F32)
    spw = P.sb([128, 8], F32)
    softplus(P, clru[:, :], prm[:, P_LAM:P_LAM + 2], [spw[:, 2 * i:2 * i + 2] for i in range(4)], neg=True)
    P.ts(clru[:, :], clru[:, :], -8.0, None, ALU.mult)
    Abc = P.sb([128, 8], F32)
    P.act(Abc[:, :], prm[:, P_ALOG:P_ALOG + 8], AF.Exp)
    P.ts(Abc[:, :], Abc[:, :], -1.0, None, ALU.mult)

    cbuf = [P.sb([128, 4 + TT], BF16) for _ in range(8)]
    for c in range(8):
        P.memset(cbuf[c][:, 0:3], 0.0)
    dg = P.sb([128, 32, 128], BF16)
    for c in range(8):
        for j in range(4):
            P.ts(dg[:, c * 4 + j, :], cst[:, 0, :], prm[:, P_CW + c * 4 + j:P_CW + c * 4 + j + 1], None, ALU.mult)
    hst = P.sb([128, 2], F32)
    P.memset(hst[:, :], 0.0)
    state = P.sb([128, 8, 64], F32)
    P.memset(state[:, :, :], 0.0)
    CTp = [P.sb([128, 4, 2, 128], BF16) for _ in range(2)]
    for t in CTp:
        P.memset(t[:, :, :, :], 0.0)

    xTs = [P.sb([128, 8, TT], BF16) for _ in range(2)]
    accs = [P.sb([128, TT], F32) for _ in range(2)]
    xc = [P.sb([128, TT], F32) for _ in range(2)]
    xcb = [P.sb([128, TT], BF16) for _ in range(2)]
    xsT = [P.sb([128, 4, TT], BF16) for _ in range(2)]
    BT = [P.sb([128, TT], BF16) for _ in range(2)]
    CT = [P.sb([128, TT], BF16) for _ in range(2)]
    lr = [P.sb([128, TT], F32) for _ in range(6)]
    ylT = [P.sb([128, 2, TT], BF16) for _ in range(2)]
    ysT = [P.sb([128, 4, TT], BF16) for _ in range(2)]
    zs = [P.sb([128, 512], F32) for _ in range(2)]
    sm = [P.sb([128, 96], F32) for _ in range(2)]
    Rm = [P.sb([128, 8, 128], F32) for _ in range(2)]
    dec = [P.sb([128, 8, 128], F32) for _ in range(2)]
    CBm = [P.sb([128, 128], F32) for _ in range(2)]
    MT = [P.sb([128, 8, 128], BF16) for _ in range(2)]
    xs_tok = [P.sb([128, 8, 64], BF16) for _ in range(2)]
    B_tok = [P.sb([128, 128], BF16) for _ in range(2)]
    xdt = [P.sb([128, 8, 64], BF16) for _ in range(2)]
    xdec = [P.sb([128, 8, 64], BF16) for _ in range(2)]
    prevb = [P.sb([128, 512], BF16) for _ in range(4)]
    yb = [P.sb([128, 8, 64], F32) for _ in range(2)]
    tmpx = [P.sb([128, 8, 64], F32) for _ in range(2)]
    y3 = [P.sb([128, 512], BF16) for _ in range(2)]
    dtall = [P.sb([128, 32], F32) for _ in range(2)]
    dtw = [P.sb([128, 32], F32) for _ in range(5)]

    ccol = [C_LRUX, C_LRUX + 128, C_XS, C_XS + 128, C_XS + 256, C_XS + 384, C_B, C_C]

    PX, PY = [6, 7], [[0, 1, 2], [3, 4, 5]]

    def gen_X(ti):
        xT = xTs[ti % 2]
        io['load_xT'](ti, xT)
        xsTt, BTt, CTt, CTpt = xsT[ti % 2], BT[ti % 2], CT[ti % 2], CTp[ti % 2]

        def fm_chunk(col):
            ps = P.ps(PX)
            for kc in range(8):
                P.mm(ps[:, :], wR[:, kc, col:col + 128], xT[:, kc, :], start=(kc == 0), stop=(kc == 7))
            return ps

        for c in range(8):
            ps = fm_chunk(ccol[c])
            cb = cbuf[c]
            P.copy(cb[:, 3:3 + TT], ps[:, :])
            yield
            acc = P.ps(PX)
            for j in range(4):
                P.mm(acc[:, :], dg[:, c * 4 + j, :], cb[:, j:j + TT], start=(j == 0), stop=(j == 3))
            P.copy(cb[:, 0:3], cb[:, TT:TT + 3], eng='dve')
            yield
            bias = prm[:, P_CB + c:P_CB + c + 1]
            if c < 2:
                P.act(xc[c][:, :], acc[:, :], AF.Identity, bias=bias)
                P.act(xcb[c][:, :], acc[:, :], AF.Identity, bias=bias)
            elif c < 6:
                P.act(xsTt[:, c - 2, :], acc[:, :], AF.Silu, bias=bias)
            elif c == 6:
                P.act(BTt[:, :], acc[:, :], AF.Silu, bias=bias)
            else:
                P.act(CTt[:, :], acc[:, :], AF.Silu, bias=bias)
                a4 = acc[:, :].rr("p (s h t) -> p s h t", s=4, h=2)
                for h in range(2):
                    P.act(CTpt[:, :, h, h * 64:(h + 1) * 64], a4[:, :, h, :], AF.Silu, bias=bias)
            yield
        psd = P.ps(PX)
        for sub in range(4):
            for kc in range(8):
                P.mm(psd[:, sub * 8:(sub + 1) * 8], xT[:, kc, sub * 128:(sub + 1) * 128], wR[:, kc, C_DT:C_DT + 8], start=(kc == 0), stop=(kc == 7))
        dtx = dtw[0][:, :]
        P.tt(dtx.rr("p (s k) -> p s k", s=4), psd[:, 0:32].rr("p (s k) -> p s k", s=4), prm[:, P_DTB:P_DTB + 8].unsq(1).bc([128, 4, 8]), ALU.add)
        yield
        softplus(P, dtall[ti % 2][:, :], dtx, [t[:, :] for t in dtw[1:5]])
        yield
        ylTt = ylT[ti % 2]
        for n in range(2):
            r_, i_, a_, s_, u_, h_ = lr
            ps = P.ps(PX)
            P.mm(ps[:, :], wai[:, n, :], xcb[n][:, :])
            P.act(r_[:, :], ps[:, :], AF.Sigmoid, bias=prm[:, P_BA + n:P_BA + n + 1])
            ps = P.ps(PX)
            P.mm(ps[:, :], wai[:, 2 + n, :], xcb[n][:, :])
            P.act(i_[:, :], ps[:, :], AF.Sigmoid, bias=prm[:, P_BI + n:P_BI + n + 1])
            yield
            P.act(a_[:, :], r_[:, :], AF.Exp, scale=clru[:, n:n + 1])
            P.tt(s_[:, :], a_[:, :], a_[:, :], ALU.mult)
            P.act(s_[:, :], s_[:, :], AF.Sqrt, bias=1.0, scale=-1.0)
            yield
            P.tt(u_[:, :], i_[:, :], xc[n][:, :], ALU.mult)
            P.tt(u_[:, :], u_[:, :], s_[:, :], ALU.mult)
            yield
            P.scan(h_[:, :], a_[:, :], u_[:, :], hst[:, n:n + 1], ALU.mult, ALU.add)
            P.copy(hst[:, n:n + 1], h_[:, TT - 1:TT], eng='dve')
            yield
            psg = fm_chunk(C_LRUG + n * 128)
            P.act(r_[:, :], psg[:, :], AF.Square)
            P.ts(r_[:, :], r_[:, :], 0.044715, 1.0, ALU.mult, ALU.add)
            yield
            P.tt(r_[:, :], r_[:, :], psg[:, :], ALU.mult)
            P.act(r_[:, :], r_[:, :], AF.Sigmoid, scale=1.5957691216057308)
            yield
            P.tt(r_[:, :], r_[:, :], psg[:, :], ALU.mult)
            P.tt(ylTt[:, n, :], r_[:, :], h_[:, :], ALU.mult)
            yield
        yt2 = yT_d.k(ti)[ti, :].rr("(f t) -> f t", t=TT)
        P.dma('sp', yt2[0:256, :].rr("(c p) t -> p c t", p=128), ylTt[:, :, :])

    def gen_Y(ti, sub):
        xT = xTs[ti % 2]
        xsTt, BTt, CTt, CTpt = xsT[ti % 2], BT[ti % 2], CT[ti % 2], CTp[ti % 2]
        ysTt = ysT[ti % 2]
        q = sub % 2
        pool = PY[q]
        tok = slice(sub * 128, (sub + 1) * 128)
        smq = sm[q]
        da = smq[:, 24:32]
        cs_sb, expcs, dend, cdec = smq[:, 32:40], smq[:, 40:48], smq[:, 48:56], smq[:, 56:72]
        tmp8, ss, rstd = smq[:, 72:80], smq[:, 80:81], smq[:, 81:82]
        psz = P.ps(pool)
        for kc in range(8):
            P.mm(psz[:, :], xT[:, kc, tok], wR[:, kc, C_Z:C_Z + 512], start=(kc == 0), stop=(kc == 7))
        P.act(zs[q][:, :], psz[:, :], AF.Silu)
        dt = dtall[ti % 2][:, sub * 8:(sub + 1) * 8]
        P.tt(da, dt, Abc[:, :], ALU.mult)
        yield
        pst = P.ps(pool)
        pstb = pst[:, :].bitcast(BF16)
        for c in range(4):
            P.tr(pstb[:, c * 128:(c + 1) * 128], xsTt[:, c, tok], identb[:, :])
        P.copy(xs_tok[q][:, :, :], pstb[:, 0:512].rr("p (k d) -> p k d", k=8))
        psb_ = P.ps(pool)
        psbb = psb_[:, :].bitcast(BF16)
        P.tr(psbb[:, 0:128], BTt[:, tok], identb[:, :])
        P.copy(B_tok[q][:, :], psbb[:, 0:128])
        yield
        psc = P.ps(pool)
        P.mm(psc[:, 0:8], TriBD, da)
        P.mm(psc[:, 8:16], OnesBD, da)
        P.mm(psc[:, 16:24], Half0, da)
        P.mm(psc[:, 24:32], Half1, da)
        P.copy(cs_sb, psc[:, 0:8])
        P.act(expcs, psc[:, 0:8], AF.Exp)
        yield
        P.tt(tmp8, psc[:, 8:16], cs_sb, ALU.subtract)
        P.act(dend, tmp8, AF.Exp)
        P.act(cdec, psc[:, 16:32], AF.Exp)
        yield
        P.tt(Rm[q][:, :, :], da.unsq(2).bc([128, 8, 128]), mask2.unsq(1).bc([128, 8, 128]), ALU.mult)
        yield
        Rf = Rm[q][:, :, :].rr("p k l -> p (k l)")
        psA = P.ps(pool)
        P.mm(psA[:, :], GtBD, Rf[:, 0:512])
        psB = P.ps(pool)
        P.mm(psB[:, :], GtBD, Rf[:, 512:1024])
        df = dec[q][:, :, :].rr("p k l -> p (k l)")
        P.act(df[:, 0:512], psA[:, :], AF.Exp)
        P.act(df[:, 512:1024], psB[:, :], AF.Exp)
        yield
        psC = P.ps(pool)
        P.mm(psC[:, 0:128], BTt[:, tok], CTt[:, tok])
        P.tt(CBm[q][:, :], psC[:, 0:128], maskCB, ALU.mult)
        yield
        P.tt(MT[q][:, :, :], dec[q][:, :, :], CBm[q][:, :].unsq(1).bc([128, 8, 128]), ALU.mult)
        yield
        P.tt(xdt[q][:, :, :], xs_tok[q][:, :, :], dt.unsq(2).bc([128, 8, 64]), ALU.mult)
        P.tt(xdec[q][:, :, :], xdt[q][:, :, :], dend.unsq(2).bc([128, 8, 64]), ALU.mult)
        yield
        psO = P.ps(pool)
        xdf = xdec[q][:, :, :].rr("p k d -> p (k d)")
        sf = state[:, :, :].rr("p k d -> p (k d)")
        for h in range(2):
            psS = P.ps(pool)
            P.mm(psS[:, :], B_tok[q][64 * h:64 * h + 64, :], xdf[64 * h:64 * h + 64, :])
            pv = prevb[(2 * q + h)]
            P.copy(pv[:, :], sf)
            P.tt(state[:, :, :], state[:, :, :], cdec[:, 8 * h:8 * h + 8].unsq(2).bc([128, 8, 64]), ALU.mult)
            P.tt(sf, sf, psS[:, :], ALU.add)
            P.mm(psO[:, :], CTpt[:, sub, h, :], pv[:, :], start=(h == 0), stop=(h == 1))
            yield
        y = yb[q]
        yf = y[:, :, :].rr("p k d -> p (k d)")
        P.tt(y[:, :, :], psO[:, :].rr("p (k d) -> p k d", k=8), expcs.unsq(2).bc([128, 8, 64]), ALU.mult)
        yield
        psY = P.ps(pool)
        for k in range(8):
            P.mm(psY[:, k * 64:(k + 1) * 64], MT[q][:, k, :], xdt[q][:, k, :])
        P.tt(yf, yf, psY[:, :], ALU.add)
        yield
        P.tt(tmpx[q][:, :, :], xs_tok[q][:, :, :], prm[:, P_DSK:P_DSK + 8].unsq(2).bc([128, 8, 64]), ALU.mult)
        P.tt(y[:, :, :], y[:, :, :], tmpx[q][:, :, :], ALU.add)
        yield
        P.tt(yf, yf, zs[q][:, :], ALU.mult)
        tf = tmpx[q][:, :, :].rr("p k d -> p (k d)")
        P.tt(tf, yf, yf, ALU.mult)
        yield
        P.red(ss, tf, ALU.add)
        P.ts(ss, ss, 1.0 / 512.0, EPS, ALU.mult, ALU.add)
        P.act(ss, ss, AF.Sqrt)
        P.recip(rstd, ss)
        yield
        P.stt(y3[q][:, :], yf, rstd, prm[:, P_NW:P_NW + 512], ALU.mult, ALU.mult)
        psT = P.ps(pool)
        psTb = psT[:, :].bitcast(BF16)
        for c in range(4):
            P.tr(psTb[:, c * 128:(c + 1) * 128], y3[q][:, c * 128:(c + 1) * 128], identb[:, :])
        P.copy(ysTt[:, :, tok], psTb[:, 0:512].rr("p (c t) -> p c t", c=4))

    def run_tile(ygens, bg):
        SKEW = 3
        active = []
        pending = list(ygens)
        bg_done = bg is None
        while pending or active:
            if pending and len(active) < 2 and (not active or active[-1][1] >= SKEW):
                active.append([pending.pop(0), 0])
            for a in list(active):
                try:
                    next(a[0])
                    a[1] += 1
                except StopIteration:
                    active.remove(a)
            if not bg_done:
                try:
                    next(bg)
                except StopIteration:
                    bg_done = True
        while not bg_done:
            try:
                next(bg)
            except StopIteration:
                bg_done = True

    run_tile([], gen_X(0))
    for ti in range(ntiles):
        bg = gen_X(ti + 1) if ti + 1 < ntiles else None
        run_tile([gen_Y(ti, sub) for sub in range(4)], bg)
        yt2 = yT_d.k(ti)[ti, :].rr("(f t) -> f t", t=TT)
        P.dma('sp', yt2[256:768, :].rr("(c p) t -> p c t", p=128), ysT[ti % 2][:, :, :])
        io['tile_done'](ti)


NTOK = 2048
P_BG = 0
NPT = 24


class WStream:
    def __init__(self, P, nslots, cache=None):
        self.P = P
        self.slots = [P.sb([128, 8, 512], BF16) for _ in range(nslots)]
        self.i = 0
        self.cache = cache
        self.ids = {}

    def load(self, wd, r0, nrows, c0, ncols=512, cacheable=True):
        P = self.P
        slot = self.slots[self.i % len(self.slots)]
        self.i += 1
        nkc = nrows // 128
        sv = slot[:, 0:nkc, 0:ncols]
        key = (id(wd), r0, c0)
        if self.cache is not None and cacheable and key in self.ids:
            pid = self.ids[key]
            cv = self.cache.k(pid)[pid, 0:128 * nkc * ncols].rr("(p k n) -> p k n", p=128, k=nkc)
            P.dma('sp', sv, cv)
            return slot
        src = wd[r0:r0 + nrows, c0:c0 + ncols].rr("(kc p) n -> p kc n", p=128)
        P.dma('pool', sv, src)
        if self.cache is not None and cacheable:
            pid = len(self.ids)
            self.ids[key] = pid
            cv = self.cache.k(pid)[pid, 0:128 * nkc * ncols].rr("(p k n) -> p k n", p=128, k=nkc)
            P.dma('sp', cv, sv)
        return slot


def emit_T(P, io, ntiles=NTOK // TT):
    ws = WStream(P, 4, io.get('wcache'))
    prmT = P.sb([128, NPT], F32)
    lnp = P.sb([128, 4, D], F32)
    identf = P.sb([128, 128], F32)
    identb = P.sb([128, 128], BF16)
    P.dma('sp', prmT[:, :], io['prmT'][:, :])
    P.dma('sp', lnp[:, :, :], io['lnp'][:, :].rr("p (a b) -> p a b", a=4))
    P.dma('sp', identf[:, :], io['cst'][:, 0:128])
    P.copy(identb[:, :], identf[:, :])

    memT = P.sb([128, 8, NMEM], BF16)
    P.dma('pool', memT[:, :, :], io['memT'][:, :].rr("(kc p) m -> p kc m", p=128))
    KT = P.sb([128, 8, NMEM], BF16)
    Vt = P.sb([128, 2, 1024], BF16)
    for ph in range(2):
        pan = ws.load(io['wkv'], 0, 1024, ph * 512, cacheable=False)
        for c in range(4):
            ps = P.ps()
            for kc in range(8):
                P.mm(ps[:, 0:NMEM], pan[:, kc, c * 128:(c + 1) * 128], memT[:, kc, :], start=(kc == 0), stop=(kc == 7))
            P.copy(KT.k(ph * 4 + c)[:, ph * 4 + c, :], ps[:, 0:NMEM])
    for ph in range(2):
        pan = ws.load(io['wkv'], 0, 1024, 1024 + ph * 512, cacheable=False)
        for mc in range(2):
            ps = P.ps()
            for kc in range(8):
                P.mm(ps[:, :], memT[:, kc, mc * 128:(mc + 1) * 128], pan[:, kc, :], start=(kc == 0), stop=(kc == 7))
            P.copy(Vt.k((mc, ph))[:, mc, ph * 512:(ph + 1) * 512], ps[:, :])

    xT = P.sb([128, 8, TT], BF16)
    xtok = P.sb([128, 4, D], F32)
    x1 = P.sb([128, 4, D], F32)
    qm = P.sb([128, 8, TT], BF16)
    x1T = P.sb([128, 8, TT], BF16)
    PT = [P.sb([128, 2, TT], BF16) for _ in range(2)]
    yxaT = P.sb([128, 8, TT], BF16)
    ybuf = P.sb([128, 16, TT], BF16)
    acc = P.sb([128, 8, TT], F32)
    hT = P.sb([128, 22, TT], BF16)
    gs = [P.sb([128, TT], F32) for _ in range(2)]
    es_ = [P.sb([128, NMEM], F32) for _ in range(2)]
    pn = [P.sb([128, NMEM], BF16) for _ in range(2)]
    sm = [P.sb([128, 8], F32) for _ in range(2)]
    st = [P.sb([128, 12], F32) for _ in range(2)]
    mv = [P.sb([128, 4], F32) for _ in range(2)]

    lncnt = [0]

    def layer_norm(v, g, b):
        i = lncnt[0] % 2
        lncnt[0] += 1
        P.bnstats(st[i][:, 0:6], v[:, 0:512])
        P.bnstats(st[i][:, 6:12], v[:, 512:1024])
        P.bnaggr(mv[i][:, 0:2], st[i][:, :])
        P.ts(mv[i][:, 2:3], mv[i][:, 1:2], EPS, None, ALU.add)
        P.act(mv[i][:, 2:3], mv[i][:, 2:3], AF.Sqrt)
        P.recip(mv[i][:, 3:4], mv[i][:, 2:3])
        P.ts(v, v, mv[i][:, 0:1], mv[i][:, 3:4], ALU.subtract, ALU.mult)
        P.tt(v, v, g, ALU.mult)
        P.tt(v, v, b, ALU.add)

    for ti in range(ntiles):
        t0 = ti * TT
        io['load_xT'](ti, xT)
        io['load_xtok'](ti, xtok)
        for ph in range(2):
            pan = ws.load(io['wq'], 0, 1024, ph * 512)
            for c in range(4):
                dc = ph * 4 + c
                ps = P.ps()
                for kc in range(8):
                    P.mm(ps[:, :], pan[:, kc, c * 128:(c + 1) * 128], xT[:, kc, :], start=(kc == 0), stop=(kc == 7))
                P.copy(qm.k(dc)[:, dc, :], ps[:, :])
        def gen_A(pool):
            cnt = 0
            for h in range(4):
                PTh = PT[h % 2]
                for s in range(4):
                    i = cnt % 2
                    cnt += 1
                    tok = slice(s * 128, (s + 1) * 128)
                    ps = P.ps(pool)
                    for dd in range(2):
                        P.mm(ps[:, 0:NMEM], qm.k(2 * h + dd)[:, 2 * h + dd, tok], KT.k(2 * h + dd)[:, 2 * h + dd, :], start=(dd == 0), stop=(dd == 1))
                    mx, nb, sm_, rs = sm[i][:, 0:1], sm[i][:, 1:2], sm[i][:, 2:3], sm[i][:, 3:4]
                    P.red(mx, ps[:, 0:NMEM], ALU.max)
                    P.ts(nb, mx, -1.0 / 16.0, None, ALU.mult)
                    yield
                    P.act(es_[i][:, :], ps[:, 0:NMEM], AF.Exp, bias=nb, scale=1.0 / 16.0)
                    P.red(sm_, es_[i][:, :], ALU.add)
                    P.recip(rs, sm_)
                    yield
                    P.ts(pn[i][:, :], es_[i][:, :], rs, None, ALU.mult)
                    pst = P.ps(pool)
                    pstb = pst[:, :].bitcast(BF16)
                    for mc in range(2):
                        P.tr(pstb[:, mc * 128:(mc + 1) * 128], pn[i][:, mc * 128:(mc + 1) * 128], identb[:, :])
                    P.copy(PTh[:, :, tok], pstb[:, 0:256].rr("p (m t) -> p m t", m=2))
                    yield
                for dd in range(2):
                    ps = P.ps(pool)
                    for mc in range(2):
                        P.mm(ps[:, :], Vt[:, mc, h * 256 + dd * 128:h * 256 + (dd + 1) * 128], PTh[:, mc, :], start=(mc == 0), stop=(mc == 1))
                    P.copy(yxaT.k(2 * h + dd)[:, 2 * h + dd, :], ps[:, :])
                yield

        def gen_B(bs, pool):
            for b in bs:
                wname, nkp = [('wbl', 1), ('wbs', 2), ('wbx', 1)][b]
                if b < 2:
                    io['load_y'](ti, b, ybuf)
                ysb = ybuf if b < 2 else yxaT
                for ph in range(2):
                    pans = [ws.load(io[wname], kp * 1024, 1024, ph * 512) for kp in range(nkp)]
                    gp = ws.load(io['wg'], 0, 1024, b * 1024 + ph * 512)
                    for c in range(4):
                        dc = ph * 4 + c
                        psp = P.ps(pool)
                        n = nkp * 8
                        i = 0
                        for kp in range(nkp):
                            for kc in range(8):
                                P.mm(psp[:, :], pans[kp][:, kc, c * 128:(c + 1) * 128], ysb[:, kp * 8 + kc, :], start=(i == 0), stop=(i == n - 1))
                                i += 1
                            yield
                        psg = P.ps(pool)
                        for kc in range(8):
                            P.mm(psg[:, :], gp[:, kc, c * 128:(c + 1) * 128], xT[:, kc, :], start=(kc == 0), stop=(kc == 7))
                        g = gs[dc % 2]
                        P.act(g[:, :], psg[:, :], AF.Sigmoid, bias=prmT[:, P_BG + b * 8 + dc:P_BG + b * 8 + dc + 1])
                        yield
                        if b == 0:
                            P.tt(acc.k(dc)[:, dc, :], g[:, :], psp[:, :], ALU.mult)
                        else:
                            P.tt(g[:, :], g[:, :], psp[:, :], ALU.mult)
                            if b == 1:
                                P.tt(acc.k(dc)[:, dc, :], acc.k(dc)[:, dc, :], g[:, :], ALU.add)
                            else:
                                P.tt(qm.k(dc)[:, dc, :], acc.k(dc)[:, dc, :], g[:, :], ALU.add)
                        yield

        for _ in gen_A(None):
            pass
        for _ in gen_B([0, 1, 2], None):
            pass
        wo_p = [ws.load(io['wo'], 0, 1024, ph * 512) for ph in range(2)]

        def wo_mm(s):
            tok = slice(s * 128, (s + 1) * 128)
            for ph in range(2):
                ps = P.ps()
                for kc in range(8):
                    P.mm(ps[:, :], qm[:, kc, tok], wo_p[ph][:, kc, :], start=(kc == 0), stop=(kc == 7))
                P.stt(x1.k(s)[:, s, ph * 512:(ph + 1) * 512], xtok.k(s)[:, s, ph * 512:(ph + 1) * 512], ALPHA, ps[:, :], ALU.mult, ALU.add)

        def ln1_tr(s):
            tok = slice(s * 128, (s + 1) * 128)
            layer_norm(x1.k(s)[:, s, :], lnp[:, 0, :], lnp[:, 1, :])
            for g4 in range(2):
                ps = P.ps()
                for j in range(4):
                    dc = g4 * 4 + j
                    P.tr(ps[:, j * 128:(j + 1) * 128], x1.k(s)[:, s, dc * 128:(dc + 1) * 128], identf[:, :])
                P.copy(x1T[:, g4 * 4:(g4 + 1) * 4, tok], ps[:, :].rr("p (j t) -> p j t", j=4))

        wo_mm(0)
        for s in range(1, 4):
            wo_mm(s)
            ln1_tr(s - 1)
        ln1_tr(3)
        for p in range(11):
            pan = ws.load(io['wfi'], 0, 1024, p * 512)
            for c in range(4):
                col = p * 512 + c * 128
                ps = P.ps()
                for kc in range(8):
                    P.mm(ps[:, :], pan[:, kc, c * 128:(c + 1) * 128], x1T[:, kc, :], start=(kc == 0), stop=(kc == 7))
                if col < DFF:
                    ffc = col // 128
                    P.act(hT.k(ffc)[:, ffc, :], ps[:, :], AF.Silu)
                else:
                    ffc = (col - DFF) // 128
                    P.tt(hT.k(ffc)[:, ffc, :], hT.k(ffc)[:, ffc, :], ps[:, :], ALU.mult)
        for ph in range(2):
            pss = [P.ps() for _ in range(4)]
            for (r0, nr) in [(0, 1024), (1024, 1024), (2048, 768)]:
                pan = ws.load(io['wfd'], r0, nr, ph * 512)
                for s in range(4):
                    tok = slice(s * 128, (s + 1) * 128)
                    for kc in range(nr // 128):
                        ffc = r0 // 128 + kc
                        P.mm(pss[s][:, :], hT.k(ffc)[:, ffc, tok], pan[:, kc, :], start=(ffc == 0), stop=(ffc == 21))
            for s in range(4):
                P.stt(xtok.k(s)[:, s, ph * 512:(ph + 1) * 512], x1.k(s)[:, s, ph * 512:(ph + 1) * 512], ALPHA, pss[s][:, :], ALU.mult, ALU.add)
        for s in range(4):
            layer_norm(xtok.k(s)[:, s, :], lnp[:, 2, :], lnp[:, 3, :])
        io['store_out'](ti, xtok)
        if io.get('store_xT') is not None:
            for s in range(4):
                tok = slice(s * 128, (s + 1) * 128)
                for g4 in range(2):
                    ps = P.ps()
                    for j in range(4):
                        dc = g4 * 4 + j
                        P.tr(ps[:, j * 128:(j + 1) * 128], xtok.k(s)[:, s, dc * 128:(dc + 1) * 128], identf[:, :])
                    P.copy(qm[:, g4 * 4:(g4 + 1) * 4, tok], ps[:, :].rr("p (j t) -> p j t", j=4))
            io['store_xT'](ti, qm)


R_W = {'wR': [D, NWR], 'wai': [128, 512], 'prm': [128, NPR]}
T_W = {'wq': [D, 1024], 'wg': [D, 3072], 'wkv': [D, 2048], 'wbl': [1024, D], 'wbs': [2048, D], 'wbx': [1024, D],
       'wo': [D, D], 'wfi': [D, 2 * DFF], 'wfd': [DFF, D], 'prmT': [128, NPT], 'lnp': [128, 4 * D]}


def build_fused(nc, es, stop=None):
    P = Prog(nc, es)

    def ext(n, sh):
        return P.dram(n, sh, F32, "ExternalInput")

    xT0 = ext("xT0", [D, SEQ])
    xTt = ext("xTt", [D, NTOK])
    xtok_d = ext("xtok", [NTOK, D])
    memT = ext("memT", [D, NMEM])
    cst = ext("cst", [128, 1024])
    out_d = P.dram("out", [NTOK, D], F32, "ExternalOutput")
    WR = [{k: ext(f"{k}{l}", sh) for k, sh in R_W.items()} for l in range(2)]
    WT = [{k: ext(f"{k}{l}", sh) for k, sh in T_W.items()} for l in range(2)]
    YT = 768 * TT
    y_loc = [P.dram(f"y_loc{l}", [16, YT], BF16) for l in range(2)]
    y_all = [P.dram(f"y_all{l}", [16, 4 * YT], BF16) for l in range(2)]
    y_mine = [P.dram(f"y_mine{l}", [4, 4 * YT], BF16, "ExternalOutput" if stop == f"R{l}" else None) for l in range(2)]
    x1_d = P.dram("x1_d", [NTOK, D], F32, "ExternalOutput" if stop in ("T0", "DBG") else None)
    wcache = [P.dram(f"wcache{l}", [36, 128 * 8 * 512], BF16) for l in range(2)]
    XT = 512 * TT
    x1T_loc = P.dram("x1T_loc", [8, XT], BF16)
    x1T_all = P.dram("x1T_all", [8, 4 * XT], BF16)

    def tokview(t):
        return t[:, :].rr("(s p) d -> p s d", p=128)

    for l in range(2):
        if l == 0:
            xv = xT0[:, :].rr("(kc p) t -> p kc t", p=128)

            def load_xT_R(ti, xT, xv=xv):
                P.dma('pool', xT[:, :, :], xv[:, :, ti * TT:(ti + 1) * TT])
        else:
            def load_xT_R(ti, xT):
                q, i = ti // 4, ti % 4
                for h in range(2):
                    src = x1T_all[i * 2 + h, q * XT:(q + 1) * XT].rr("(c p t) -> p c t", p=128, t=TT)
                    P.dma('pool', xT[:, 4 * h:4 * h + 4, :], src)
        yl, yal = y_loc[l], y_all[l]

        pend = []

        def tile_done(ti, yl=yl, yal=yal, pend=pend):
            while pend:
                pend.pop(0)()
            pend.append(lambda: P.allgather(yal.k(ti)[ti, :], yl.k(ti)[ti, :]))
        ioR = dict(WR[l])
        ioR.update(cst=cst, yT=y_loc[l], load_xT=load_xT_R, tile_done=tile_done)
        emit_R(P, ioR)
        while pend:
            pend.pop(0)()
        P.barrier()

        ya = y_all[l]

        ym = y_mine[l]

        def fn(e, ym=ym, ya=ya):
            q = e.partition_id() % 4
            yv = ya.base.rearrange("(q i) (r c) -> q (i r) c", q=4, c=16384)
            src = yv[bass.ds(q, 1), :, :].rearrange("o r c -> (o r) c")
            dst = ym.base.rearrange("i (r c) -> (i r) c", c=16384)
            return e.dma_start(out=dst, in_=src)
        op_ = P.dma_fn('pool', ym[:, :], ya[:, :], fn)
        if stop == f"R{l}":
            P.out_dmas.append(op_)
            break

        def load_y(ti, b, ybuf, ym=ym):
            r0, r1, ncc = (0, 256, 2) if b == 0 else (256, 768, 4)
            for j in range(4):
                src = ym[ti, j * YT:(j + 1) * YT].rr("(f t) -> f t", t=TT)[r0:r1, :].rr("(c p) t -> p c t", p=128)
                P.dma('sp', ybuf[:, j * ncc:(j + 1) * ncc, :], src)

        ioT = dict(WT[l])
        ioT.update(cst=cst, memT=memT, load_y=load_y, wcache=None)
        if l == 0:
            xtv = xTt[:, :].rr("(kc p) t -> p kc t", p=128)
            ioT['load_xT'] = lambda ti, xT, xtv=xtv: P.dma('pool', xT[:, :, :], xtv[:, :, ti * TT:(ti + 1) * TT])
            ioT['load_xtok'] = lambda ti, xt: P.dma('sp', xt[:, :, :], tokview(xtok_d)[:, ti * 4:(ti + 1) * 4, :])
            ioT['store_out'] = lambda ti, xt: P.dma('sp', tokview(x1_d)[:, ti * 4:(ti + 1) * 4, :], xt[:, :, :])
            pend2 = []

            def store_xT(ti, qm):
                while pend2:
                    pend2.pop(0)()
                for h in range(2):
                    n = ti * 2 + h
                    dst = x1T_loc.k(n)[n, :].rr("(c p t) -> p c t", p=128, t=TT)
                    P.dma('sp', dst, qm[:, 4 * h:4 * h + 4, :])
                    pend2.append(lambda n=n: P.allgather(x1T_all.k(n)[n, :], x1T_loc.k(n)[n, :]))
            ioT['store_xT'] = store_xT
        else:
            def load_xT_T(ti, xT):
                for h in range(2):
                    src = x1T_loc[ti * 2 + h, :].rr("(c p t) -> p c t", p=128, t=TT)
                    P.dma('pool', xT[:, 4 * h:4 * h + 4, :], src)
            ioT['load_xT'] = load_xT_T
            ioT['load_xtok'] = lambda ti, xt: P.dma('sp', xt[:, :, :], tokview(x1_d)[:, ti * 4:(ti + 1) * 4, :])
            ioT['store_out'] = lambda ti, xt: P.dma('sp', tokview(out_d)[:, ti * 4:(ti + 1) * 4, :], xt[:, :, :], is_out=True)
            ioT['store_xT'] = None
        emit_T(P, ioT)
        if l == 1 and stop == "DBG":
            dbg = P.dram("dbg_x1T", [8, XT], BF16, "ExternalOutput")
            P.dma('sp', dbg[:, :], x1T_loc[:, :], is_out=True)
        if l == 0:
            while pend2:
                pend2.pop(0)()
            P.barrier()
            if stop == "T0":
                break
    P.run()


OFF = {'lrux': 0, 'lrug': 1024, 'z': 2048, 'xbc': 4096, 'dt': 7168, 'q': 7200, 'g': 8224}
_CST = make_consts()
_prog = []


def get_prog():
    if not _prog:
        nc = bass.Bass("TRN2", target_bir_lowering=False)
        es = ExitStack()
        build_fused(nc, es)
        _prog.append((nc, es))
    return _prog[0][0]


def rep(v, n=128):
    v = np.asarray(v, np.float32)
    return np.ascontiguousarray(np.broadcast_to(v.reshape(1, -1), (n, v.size)))


def pcol(v):
    v = np.asarray(v, np.float32)
    return np.ascontiguousarray(v.reshape(-1, 128).T)


def r_inputs(inp, l, j):
    w_in = inp['w_in'][l]
    lr = slice(256 * j, 256 * (j + 1))
    gr = slice(512 * j, 512 * (j + 1))
    hr = slice(8 * j, 8 * (j + 1))
    xbc0 = OFF['xbc']
    bsl = slice(2048 + 128 * j, 2048 + 128 * (j + 1))
    csl = slice(2560 + 128 * j, 2560 + 128 * (j + 1))
    wR = np.concatenate([
        w_in[:, OFF['lrux'] + 256 * j:OFF['lrux'] + 256 * (j + 1)],
        w_in[:, OFF['lrug'] + 256 * j:OFF['lrug'] + 256 * (j + 1)],
        w_in[:, xbc0 + 512 * j:xbc0 + 512 * (j + 1)],
        w_in[:, xbc0 + bsl.start:xbc0 + bsl.stop],
        w_in[:, xbc0 + csl.start:xbc0 + csl.stop],
        w_in[:, OFF['z'] + 512 * j:OFF['z'] + 512 * (j + 1)],
        w_in[:, OFF['dt'] + 8 * j:OFF['dt'] + 8 * (j + 1)],
    ], axis=1)
    lcw = inp['lru_conv_w'][l][:, lr]
    scw = inp['ssd_conv_w'][l]
    scb = inp['ssd_conv_b'][l]
    cw_cols = np.concatenate([lcw, scw[:, gr], scw[:, bsl], scw[:, csl]], axis=1)
    cb_cols = np.concatenate([inp['lru_conv_b'][l][lr], scb[gr], scb[bsl], scb[csl]])
    cw = cw_cols.reshape(4, 8, 128).transpose(2, 1, 0).reshape(128, 32)
    prm = np.concatenate([
        cw, pcol(cb_cols), pcol(inp['lru_b_a'][l][lr]), pcol(inp['lru_b_i'][l][lr]), pcol(inp['lru_lambda'][l][lr]),
        rep(inp['ssd_dt_bias'][l][hr]), rep(inp['ssd_a_log'][l][hr]), rep(inp['ssd_d'][l][hr]),
        rep(inp['ssd_norm_w'][l][gr]),
    ], axis=1).astype(np.float32)
    wa = inp['lru_w_a'][l][2 * j:2 * j + 2]
    wi = inp['lru_w_i'][l][2 * j:2 * j + 2]
    wai = np.concatenate([wa, wi], axis=0).transpose(1, 0, 2).reshape(128, 512)
    return {f"wR{l}": np.ascontiguousarray(wR, dtype=np.float32), f"wai{l}": np.ascontiguousarray(wai, dtype=np.float32),
            f"prm{l}": np.ascontiguousarray(prm)}


def t_inputs(inp, l):
    w_in = inp['w_in'][l]
    bg = inp['b_gate'][l]
    prmT = bg.reshape(3, 8, 128).transpose(2, 0, 1).reshape(128, 24)
    lnp = np.concatenate([rep(inp['ln1_g'][l]), rep(inp['ln1_b'][l]), rep(inp['ln2_g'][l]), rep(inp['ln2_b'][l])], axis=1)
    c = np.ascontiguousarray
    return {
        f"wq{l}": c(w_in[:, OFF['q']:OFF['q'] + 1024]), f"wg{l}": c(w_in[:, OFF['g']:OFF['g'] + 3072]),
        f"wkv{l}": c(inp['mem_w_kv'][l]), f"wbl{l}": c(inp['w_br_lru'][l]), f"wbs{l}": c(inp['w_br_ssd'][l]),
        f"wbx{l}": c(inp['w_br_xa'][l]), f"wo{l}": c(inp['w_out'][l]), f"wfi{l}": c(inp['ffn_w_in'][l]),
        f"wfd{l}": c(inp['ffn_w_down'][l]), f"prmT{l}": c(prmT.astype(np.float32)), f"lnp{l}": c(lnp),
    }


def kernel(**inputs):
    inp = {k: np.asarray(v, dtype=np.float32) for k, v in inputs.items()}
    x = inp['x']
    nc = get_prog()
    xTs = [np.ascontiguousarray(x[b].T) for b in range(2)]
    memTs = [np.ascontiguousarray(inp['mem'][b].T) for b in range(2)]
    tw = {}
    for l in range(2):
        tw.update(t_inputs(inp, l))
    rw = [{} for _ in range(4)]
    for j in range(4):
        for l in range(2):
            rw[j].update(r_inputs(inp, l, j))
    in_maps = []
    for c in range(8):
        b, q = c // 4, c % 4
        m = {"xT0": xTs[b], "xTt": np.ascontiguousarray(xTs[b][:, q * NTOK:(q + 1) * NTOK]),
             "xtok": np.ascontiguousarray(x[b, q * NTOK:(q + 1) * NTOK]), "memT": memTs[b], "cst": _CST}
        m.update(tw)
        m.update(rw[q])
        in_maps.append(m)
    res = run_bass_kernel_spmd(nc, in_maps, core_ids=list(range(8)))
    out = np.empty((2, SEQ, D), np.float32)
    for c in range(8):
        b, q = c // 4, c % 4
        out[b, q * NTOK:(q + 1) * NTOK] = np.asarray(res.results[c]["out"])
    return out
```
